# Optimizing a Trainium2 kernel written in Bass

```python
import jax, jax.numpy as jnp
from jax import lax
import numpy as np

D_MODEL = 2048
BATCH = 2
SEQ = 4096
DEPTH = 1

ATTN_WIDTH = D_MODEL // 2
HEAD_DIM = 64
N_Q_HEADS = ATTN_WIDTH // HEAD_DIM
N_KV_HEADS = 2
GROUP = N_Q_HEADS // N_KV_HEADS
KV_WIDTH = N_KV_HEADS * HEAD_DIM
WINDOW = 128
BLOCK = 128
RNN_WIDTH = D_MODEL - ATTN_WIDTH
RNN_HEAD_DIM = 128
N_RNN_HEADS = RNN_WIDTH // RNN_HEAD_DIM
CHUNK = 64
MIX_WIDTH = ATTN_WIDTH + RNN_WIDTH
D_FF = 4 * D_MODEL
IN_WIDTH = ATTN_WIDTH + 2 * KV_WIDTH + 4 * RNN_WIDTH
SPLITS = tuple(int(s) for s in np.cumsum([ATTN_WIDTH, KV_WIDTH, KV_WIDTH,
                                           RNN_WIDTH, RNN_WIDTH, RNN_WIDTH]))
EPS = 1e-6

kernel_name = "hymba_swa_sink_hgrn2_sqrelu_sandwich"


def rmsnorm(x, gain):
    xf = x.astype(jnp.float32)
    y = xf * lax.rsqrt(jnp.mean(xf * xf, axis=-1, keepdims=True) + EPS)
    return (y * gain.astype(jnp.float32)).astype(x.dtype)


def alibi_slopes(n_heads):
    return jnp.exp2(-8.0 * jnp.arange(1, n_heads + 1, dtype=jnp.float32) / n_heads)


def sliding_window_attention(q, k, v, sinks):
    B, S, _ = q.shape
    nb = S // BLOCK
    qb = q.reshape(B, nb, BLOCK, N_KV_HEADS, GROUP, HEAD_DIM)
    kb = k.reshape(B, nb, BLOCK, N_KV_HEADS, HEAD_DIM)
    vb = v.reshape(B, nb, BLOCK, N_KV_HEADS, HEAD_DIM)
    pad = ((0, 0), (1, 0), (0, 0), (0, 0), (0, 0))
    kcat = jnp.concatenate([jnp.pad(kb, pad)[:, :-1], kb], axis=2)
    vcat = jnp.concatenate([jnp.pad(vb, pad)[:, :-1], vb], axis=2)
    scores = jnp.einsum('bnqhgd,bnkhd->bnhgqk', qb, kcat,
                        preferred_element_type=jnp.float32) * (HEAD_DIM ** -0.5)
    q_pos = jnp.arange(BLOCK) + BLOCK
    k_pos = jnp.arange(2 * BLOCK)
    dist = (q_pos[:, None] - k_pos[None, :]).astype(jnp.float32)
    band = (dist >= 0) & (dist < WINDOW)
    abs_k = jnp.arange(nb)[:, None] * BLOCK - BLOCK + k_pos[None, :]
    valid = band[None] & (abs_k >= 0)[:, None, :]
    slopes = alibi_slopes(N_Q_HEADS).reshape(N_KV_HEADS, GROUP, 1, 1)
    scores = scores - slopes * dist
    scores = jnp.where(valid[None, :, None, None], scores, -jnp.inf)
    sink = sinks.astype(jnp.float32).reshape(N_KV_HEADS, GROUP, 1, 1)
    m = jnp.maximum(jnp.max(scores, axis=-1, keepdims=True), sink)
    p = jnp.exp(scores - m)
    probs = p / (jnp.sum(p, axis=-1, keepdims=True) + jnp.exp(sink - m))
    out = jnp.einsum('bnhgqk,bnkhd->bnqhgd', probs.astype(v.dtype), vcat)
    return out.reshape(B, S, ATTN_WIDTH)


def hgrn2_chunkwise(q, f_logit, i, g, lb, norm_gain):
    B, S, _ = q.shape
    nc = S // CHUNK
    f32 = jnp.float32
    f = lb + (1.0 - lb) * jax.nn.sigmoid(f_logit.astype(f32))
    log_f = jnp.log(f)
    key = 1.0 - f
    qf = jax.nn.silu(q.astype(f32))
    vf = i.astype(f32)

    def to_chunks(t):
        return t.reshape(B, nc, CHUNK, N_RNN_HEADS, RNN_HEAD_DIM).transpose(1, 0, 3, 2, 4)

    causal = jnp.tril(jnp.ones((CHUNK, CHUNK), dtype=bool))

    def step(state, inp):
        qc, kc, vc, lfc = inp
        b = jnp.cumsum(lfc, axis=-2)
        o_inter = jnp.einsum('bhtk,bhkv->bhtv', qc * jnp.exp(b), state)
        diff = b[:, :, :, None, :] - b[:, :, None, :, :]
        decay = jnp.exp(jnp.where(causal[:, :, None], diff, -jnp.inf))
        att = jnp.einsum('bhtk,bhsk,bhtsk->bhts', qc, kc, decay)
        o_intra = jnp.einsum('bhts,bhsv->bhtv', att, vc)
        b_last = b[:, :, -1:, :]
        new_state = (jnp.exp(b_last[:, :, 0, :])[..., None] * state
                     + jnp.einsum('bhsk,bhsv->bhkv', kc * jnp.exp(b_last - b), vc))
        return new_state, o_inter + o_intra

    s0 = jnp.zeros((B, N_RNN_HEADS, RNN_HEAD_DIM, RNN_HEAD_DIM), f32)
    _, o = lax.scan(step, s0, (to_chunks(qf), to_chunks(key), to_chunks(vf), to_chunks(log_f)))
    o = o.transpose(1, 0, 3, 2, 4).reshape(B, S, N_RNN_HEADS, RNN_HEAD_DIM)
    o = o * lax.rsqrt(jnp.mean(o * o, axis=-1, keepdims=True) + EPS) * norm_gain.astype(f32)
    gate = jax.nn.silu(g.astype(f32)).reshape(B, S, N_RNN_HEADS, RNN_HEAD_DIM)
    return (o * gate).reshape(B, S, RNN_WIDTH).astype(q.dtype)


def setup_inputs(seed: int = 0) -> dict:
    key = jax.random.key(seed)
    ks = jax.random.split(key, 14)
    f32 = jnp.float32

    def gain(k, shape):
        return 1.0 + 0.05 * jax.random.normal(k, shape, f32)

    return {
        "x": jax.random.normal(ks[0], (BATCH, SEQ, D_MODEL), f32),
        "w_in": jax.random.normal(ks[1], (DEPTH, D_MODEL, IN_WIDTH), f32) * D_MODEL ** -0.5,
        "attn_sinks": 0.5 * jax.random.normal(ks[2], (DEPTH, N_Q_HEADS), f32),
        "attn_out_gain": gain(ks[3], (DEPTH, ATTN_WIDTH)),
        "rnn_lb_logits": 0.1 * jax.random.normal(ks[4], (DEPTH + 1, RNN_WIDTH), f32),
        "rnn_norm_gain": gain(ks[5], (DEPTH, RNN_HEAD_DIM)),
        "w_out": jax.random.normal(ks[6], (DEPTH, MIX_WIDTH, D_MODEL), f32) * MIX_WIDTH ** -0.5,
        "mix_pre_gain": gain(ks[7], (DEPTH, D_MODEL)),
        "mix_post_gain": gain(ks[8], (DEPTH, D_MODEL)),
        "mlp_pre_gain": gain(ks[9], (DEPTH, D_MODEL)),
        "mlp_post_gain": gain(ks[10], (DEPTH, D_MODEL)),
        "w_up": jax.random.normal(ks[11], (DEPTH, D_MODEL, D_FF), f32) * D_MODEL ** -0.5,
        "w_down": jax.random.normal(ks[12], (DEPTH, D_FF, D_MODEL), f32) * D_FF ** -0.5,
    }


def reference(x, w_in, attn_sinks, attn_out_gain, rnn_lb_logits, rnn_norm_gain, w_out,
              mix_pre_gain, mix_post_gain, mlp_pre_gain, mlp_post_gain, w_up, w_down):
    lb_all = jnp.cumsum(jax.nn.softmax(rnn_lb_logits.astype(jnp.float32), axis=0), axis=0)
    for layer in range(DEPTH):
        h = rmsnorm(x, mix_pre_gain[layer])
        proj = jnp.einsum('bsd,de->bse', h, w_in[layer])
        q_a, k_a, v_a, q_r, f_r, i_r, g_r = jnp.split(proj, SPLITS, axis=-1)
        attn = sliding_window_attention(q_a, k_a, v_a, attn_sinks[layer])
        attn = rmsnorm(attn, attn_out_gain[layer])
        rnn = hgrn2_chunkwise(q_r, f_r, i_r, g_r, lb_all[layer], rnn_norm_gain[layer])
        mixed = jnp.einsum('bse,ed->bsd', jnp.concatenate([attn, rnn], axis=-1), w_out[layer])
        x = x + rmsnorm(mixed, mix_post_gain[layer])
        h = rmsnorm(x, mlp_pre_gain[layer])
        u = jax.nn.relu(jnp.einsum('bsd,df->bsf', h, w_up[layer]))
        y = jnp.einsum('bsf,fd->bsd', u * u, w_down[layer])
        x = x + rmsnorm(y, mlp_post_gain[layer])
    return x
```

```python
import numpy as np
import concourse.bass as bass
import concourse.mybir as mybir
from concourse.bass_utils import run_bass_kernel_spmd

F32 = mybir.dt.float32
BF16 = mybir.dt.bfloat16
I32 = mybir.dt.int32
U8 = mybir.dt.uint8
AF = mybir.ActivationFunctionType
ALU = mybir.AluOpType

ENGS = ("pe", "dve", "act", "pool", "sp")
HANDLES = {"pe": "tensor", "dve": "vector", "act": "scalar", "pool": "gpsimd", "sp": "sync"}


class _Op:
    __slots__ = ("eng", "fn", "deps", "needed", "dom", "val", "waits", "is_dma", "dsem", "idx")


class MK:
    def __init__(self, nc, same_engine_sync=True):
        self.nc = nc
        self.ops = []
        self.last_w = {}
        self.readers = {}
        self.same_engine_sync = same_engine_sync
        self.token = None
        self.last_eng = {}
        self.last_sem = {}

    def _record(self, eng, fn, reads, writes, is_dma=False, dsem=None, free=False, extra=()):
        op = _Op()
        op.eng, op.fn, op.is_dma, op.dsem = eng, fn, is_dma, dsem
        op.needed = False
        op.idx = len(self.ops)
        deps = set(extra)
        if self.token is not None and not free:
            deps.add(self.token)
        for k in reads:
            w = self.last_w.get(k)
            if w is not None:
                deps.add(w)
        for k in writes:
            w = self.last_w.get(k)
            if w is not None:
                deps.add(w)
            for r in self.readers.get(k, ()):
                deps.add(r)
        deps.discard(op.idx)
        op.deps = deps
        self.ops.append(op)
        for k in reads:
            self.readers.setdefault(k, []).append(op.idx)
        for k in writes:
            self.last_w[k] = op.idx
            self.readers[k] = []
        if is_dma:
            self.last_sem[dsem] = op.idx
        else:
            self.last_eng[eng] = op.idx
        return op

    def op(self, eng, fn, reads=(), writes=(), free=False):
        return self._record(eng, fn, tuple(reads), tuple(writes), free=free)

    def dma(self, fn, reads=(), writes=(), sem="dma0", queue="sp", free=False):
        return self._record(queue, fn, tuple(reads), tuple(writes), is_dma=True, dsem=sem, free=free)

    def barrier(self, fn, exclude_sem_prefix="wb"):
        extra = set(self.last_eng.values())
        for s, i in self.last_sem.items():
            if not s.startswith(exclude_sem_prefix):
                extra.add(i)
        op = self._record("dve", fn, (), (), extra=extra)
        self.token = op.idx
        return op

    def emit(self, final_wait_keys=()):
        nc = self.nc
        ops = self.ops
        self._record("sp", None, tuple(final_wait_keys), ())
        for op in ops:
            for d in op.deps:
                ops[d].needed = True
        cnt = {}
        clock_of = [None] * len(ops)
        eng_clock = {e: {} for e in ENGS}
        per_eng = {e: [] for e in ENGS}
        for op in ops:
            e = op.eng
            ec = eng_clock[e]
            waits = {}
            for d in op.deps:
                dop = ops[d]
                if (not dop.is_dma) and dop.eng == e and (e == "pe" or not self.same_engine_sync):
                    continue
                dom, val = dop.dom, dop.val
                if ec.get(dom, 0) >= val:
                    continue
                if waits.get(dom, 0) < val:
                    waits[dom] = val
            if waits:
                for d in op.deps:
                    dop = ops[d]
                    if dop.dom in waits and waits[dop.dom] >= dop.val:
                        for k2, v2 in clock_of[d].items():
                            if ec.get(k2, 0) < v2:
                                ec[k2] = v2
                for dom, val in waits.items():
                    if ec.get(dom, 0) < val:
                        ec[dom] = val
            op.waits = list(waits.items())
            if op.is_dma:
                dom = ("dma", op.dsem)
                cnt[dom] = cnt.get(dom, 0) + 16
                op.dom, op.val = dom, cnt[dom]
                ck = dict(ec)
                ck[dom] = op.val
                clock_of[op.idx] = ck
            else:
                dom = ("eng", e)
                if op.needed:
                    cnt[dom] = cnt.get(dom, 0) + 1
                    op.dom, op.val = dom, cnt[dom]
                    ck = dict(ec)
                    ck[dom] = op.val
                    clock_of[op.idx] = ck
                else:
                    op.dom, op.val = dom, None
            per_eng[e].append(op)
        self.stats = {e: len(v) for e, v in per_eng.items()}
        self.stats["waits"] = sum(len(o.waits) for o in ops)
        doms = set()
        for op in ops:
            if op.is_dma or op.needed:
                doms.add(op.dom)
        self.stats["sems"] = len(doms)
        from contextlib import ExitStack
        with ExitStack() as st:
            sem = {}
            for i, dom in enumerate(sorted(doms, key=str)):
                sem[dom] = st.enter_context(nc.semaphore(f"s{i}"))
            block = st.enter_context(nc.Block())

            def make(ename):
                lst = per_eng[ename]

                def body(eng):
                    for op in lst:
                        for dom, val in op.waits:
                            eng.wait_ge(sem[dom], val)
                        if op.fn is None:
                            continue
                        ins = op.fn(eng)
                        if op.is_dma:
                            ins.then_inc(sem[op.dom], 16)
                        elif op.needed:
                            ins.then_inc(sem[op.dom], 1)
                return body

            for ename in ENGS:
                if per_eng[ename]:
                    getattr(block, HANDLES[ename])(make(ename))
        return self.stats


D = 2048
DC = 16
NT = 8
TT = 9
TOK = 1152
OWN = 1024
NH = 16
DFF = 8192
EPS = 1e-6
NEG8 = -30000.0
ARENA = 206 * 1024
WUNIT = 4096
NWU = 8

C_Q, C_K, C_V, C_QR, C_FR, C_IR, C_GR = 0, 1024, 1152, 1280, 2304, 3328, 4352


def build_program(debug=False, stop=None, reg=None):
    nc = bass.Bass("TRN2", target_bir_lowering=False)
    reg = {} if reg is None else reg

    def dram(name, shape, dt=F32, kind="ExternalInput"):
        return nc.dram_tensor(name, list(shape), dt, kind=kind).ap()

    xin = dram("xin", [TOK, D])
    hm = dram("hm", [128, 1])
    w_in = dram("w_in", [D, 5376])
    w_out = dram("w_out", [D, D])
    w_up = dram("w_up", [D, DFF])
    w_down = dram("w_down", [DFF, D])
    sinks = dram("sinks", [1, NH])
    attn_gain = dram("attn_gain", [1, 1024])
    lb_logits = dram("lb_logits", [2, 1024])
    rnn_gain = dram("rnn_gain", [1, 128])
    g_mix_pre = dram("g_mix_pre", [1, D])
    g_mix_post = dram("g_mix_post", [1, D])
    g_mlp_pre = dram("g_mlp_pre", [1, D])
    g_mlp_post = dram("g_mlp_post", [1, D])
    out = dram("out", [OWN, D], kind="ExternalOutput")
    if debug:
        dbg_cat = dram("dbg_cat", [128, 16 * OWN], BF16, kind="ExternalOutput")
        dbg_x1 = dram("dbg_x1", [OWN, D], kind="ExternalOutput")

    m = MK(nc)
    arena = nc.alloc_sbuf_tensor("arena", [128, ARENA], U8)
    state = {"off": 0}

    def alloc(shape, dt=F32):
        esz = 2 if dt == BF16 else 4
        n = esz
        for s in shape[1:]:
            n *= s
        off = (state["off"] + 63) // 64 * 64
        reg[len(reg)] = (off, tuple(shape), "bf16" if dt == BF16 else ("i32" if dt == I32 else "f32"))
        assert off + n <= ARENA, f"SBUF arena overflow: {off + n} > {ARENA}"
        state["off"] = off + n
        v = arena[:, off:off + n].bitcast(dt)
        if len(shape) == 3:
            v = v.rearrange("p (a b) -> p a b", b=shape[2])
        elif len(shape) == 4:
            v = v.rearrange("p (a b c) -> p a b c", b=shape[2], c=shape[3])
        return v

    pd = [nc.alloc_psum_tensor(f"pd{i}", [128, 1024], F32) for i in range(4)]

    def bank(b):
        return pd[b // 2][:, (b % 2) * 512:(b % 2) * 512 + 512]

    def bank_bf(b):
        return bank(b).bitcast(BF16)

    V, S, A_, P = "dve", "act", "act", "pool"

    ident = alloc([128, 128], BF16)
    ones_f = alloc([128, 128], F32)
    mask2 = alloc([128, 128], F32)
    epsT = alloc([128, 1], F32)
    scr = alloc([128, 4], F32)
    sink_bc = alloc([128, NH], F32)
    rg = alloc([128, 1], F32)
    lbT = alloc([128, 8], F32)
    omlT = alloc([128, 8], F32)
    hm_sb = alloc([128, 1], F32)
    stat = alloc([128, 64], F32)
    wpool = alloc([128, NWU * WUNIT // 2], BF16)
    persist_mark = state["off"]
    stat_col = {"n": 0}

    def scol(n=1):
        c = stat_col["n"]
        stat_col["n"] += n
        assert stat_col["n"] <= 64
        return c

    wstate = {"next": 0}

    def walloc(units):
        u = wstate["next"]
        u = (u + units - 1) // units * units
        if u + units > NWU:
            u = 0
        wstate["next"] = u + units
        keys = [f"wb{u + i}" for i in range(units)]
        base = wpool[:, u * (WUNIT // 2):(u + units) * (WUNIT // 2)]
        return base, keys, f"wb{u}"

    def load_w(src_ap, view_fn, units):
        base, keys, sem = walloc(units)
        dst = view_fn(base)
        m.dma(lambda e, dst=dst, src=src_ap: e.dma_start(out=dst, in_=src), writes=keys, sem=sem,
              queue="pool", free=True)
        return dst, keys

    def wview3(n):
        return lambda base: base.rearrange("p (a b) -> p a b", b=n)

    def rows_view(w, c0, n):
        return w[:, c0:c0 + n].rearrange("(dc p) c -> p dc c", p=128)

    setup_mark = state["off"]
    catT = alloc([128, 16, OWN], BF16)
    identf = alloc([128, 128], F32)
    m.op(P, lambda e: e.memset(identf, 1.0), writes=["identf"])
    m.op(P, lambda e: e.affine_select(out=identf, in_=identf, pattern=[[-1, 128]], compare_op=ALU.is_equal,
                                      fill=0.0, base=0, channel_multiplier=1), reads=["identf"], writes=["identf"])
    m.op(V, lambda e: e.tensor_copy(out=ident, in_=identf), reads=["identf"], writes=["ident"])
    m.op(P, lambda e: e.memset(ones_f, 1.0), writes=["ones_f"])
    m.op(P, lambda e: e.memset(epsT, EPS), writes=["epsT"])
    m.op(V, lambda e: e.memset(scr, 0.0), writes=["scr"])
    m.op(P, lambda e: e.memset(mask2, 1.0), writes=["mask2"])
    m.op(P, lambda e: e.affine_select(out=mask2, in_=mask2, pattern=[[1, 128]], compare_op=ALU.is_ge,
                                      fill=0.0, base=0, channel_multiplier=-1), reads=["mask2"], writes=["mask2"])
    m.op(P, lambda e: e.memset(mask2[0:64, 64:128], 0.0), reads=["mask2"], writes=["mask2"])
    m.dma(lambda e: e.dma_start(out=sink_bc, in_=sinks.partition_broadcast(128)), writes=["sink_bc"], sem="c_sink")
    m.dma(lambda e: e.dma_start(out=rg, in_=rnn_gain.rearrange("o k -> k o")), writes=["rg"], sem="c_rg")
    m.dma(lambda e: e.dma_start(out=hm_sb, in_=hm), writes=["hm"], sem="c_hm")
    l0 = alloc([128, 8], F32)
    l1 = alloc([128, 8], F32)
    m.dma(lambda e: e.dma_start(out=l0, in_=lb_logits[0:1, :].rearrange("o (h k) -> k (o h)", k=128),
                                allow_slow_non_contiguous=True), writes=["l0"], sem="c_l0")
    m.dma(lambda e: e.dma_start(out=l1, in_=lb_logits[1:2, :].rearrange("o (h k) -> k (o h)", k=128),
                                allow_slow_non_contiguous=True), writes=["l1"], sem="c_l1")
    m.op(S, lambda e: e.activation(out=l0, in_=l0, func=AF.Exp), reads=["l0"], writes=["l0"])
    m.op(S, lambda e: e.activation(out=l1, in_=l1, func=AF.Exp), reads=["l1"], writes=["l1"])
    m.op(V, lambda e: e.tensor_tensor(out=omlT, in0=l0, in1=l1, op=ALU.add), reads=["l0", "l1"], writes=["omlT"])
    m.op(V, lambda e: e.reciprocal(out=omlT, in_=omlT), reads=["omlT"], writes=["omlT"])
    m.op(V, lambda e: e.tensor_tensor(out=lbT, in0=l0, in1=omlT, op=ALU.mult), reads=["l0", "omlT"], writes=["lbT"])
    m.op(V, lambda e: e.tensor_scalar(out=omlT, in0=lbT, scalar1=-1.0, scalar2=1.0, op0=ALU.mult, op1=ALU.add),
         reads=["lbT"], writes=["omlT"])

    hT = alloc([128, DC, TOK], BF16)
    ac_mark = state["off"]
    Mt = alloc([128, 256], F32)
    Mt8 = alloc([128, 256], BF16)
    Mt8_0 = alloc([128, 256], BF16)
    Kr_i = alloc([128, 256], I32)
    Kr = alloc([128, 256], BF16)
    Sl = alloc([128, NH, 128], BF16)
    qc_i = alloc([128, 1], I32)
    qcol = alloc([128, 1], F32)
    sc = alloc([128, NH], F32)
    nsc = alloc([128, NH], F32)
    ab_mark = state["off"]
    gA = alloc([128, D], F32)
    xb = [alloc([128, D], F32) for _ in range(3)]
    hn = [alloc([128, D], BF16) for _ in range(3)]
    m.dma(lambda e: e.dma_start(out=gA, in_=g_mix_pre.partition_broadcast(128)), writes=["gA"], sem="gA")

    def rstd_chain(ss_ap, key, n_inv):
        m.op(V, lambda e: e.tensor_scalar(out=ss_ap, in0=ss_ap, scalar1=n_inv, scalar2=EPS, op0=ALU.mult, op1=ALU.add),
             reads=[key], writes=[key])
        m.op(S, lambda e: e.activation(out=ss_ap, in_=ss_ap, func=AF.Sqrt), reads=[key], writes=[key])
        m.op(V, lambda e: e.reciprocal(out=ss_ap, in_=ss_ap), reads=[key], writes=[key])

    def norm_transpose(src, src_keys, gain, gain_key, hnb, hnkey, dstT, dst_key, col0, stc, bnk):
        ss = stat[:, stc:stc + 1]
        skey = f"stat{stc}"
        m.op(S, lambda e: e.activation(out=hnb, in_=src, func=AF.Square, accum_out=ss),
             reads=src_keys, writes=[hnkey, skey])
        rstd_chain(ss, skey, 1.0 / D)
        m.op(V, lambda e: e.scalar_tensor_tensor(out=hnb, in0=src, scalar=ss, in1=gain, op0=ALU.mult, op1=ALU.mult),
             reads=list(src_keys) + [skey, gain_key], writes=[hnkey])
        for half in range(2):
            b = bnk[half]
            pb = bank_bf(b).rearrange("p (a c) -> p a c", c=128)
            for j in range(8):
                dc = half * 8 + j
                m.op("pe", lambda e, o=pb[:, j, :], i=hnb[:, dc * 128:(dc + 1) * 128]: e.transpose(out=o, in_=i, identity=ident),
                     reads=[hnkey, "ident"], writes=[f"pb{b}"])
            eng = S if half == 0 else V
            dst = dstT[:, half * 8:(half + 1) * 8, col0:col0 + 128]
            if eng == S:
                m.op(S, lambda e, o=dst, i=pb[:, 0:8, :]: e.copy(out=o, in_=i), reads=[f"pb{b}"], writes=[dst_key])
            else:
                m.op(V, lambda e, o=dst, i=pb[:, 0:8, :]: e.tensor_copy(out=o, in_=i), reads=[f"pb{b}"], writes=[dst_key])

    m.op(P, lambda e: e.memset(Mt, 0.0), writes=["Mt"])
    m.op(P, lambda e: e.affine_select(out=Mt, in_=Mt, pattern=[[1, 256]], compare_op=ALU.is_ge, fill=8.0 * NEG8,
                                      base=-1, channel_multiplier=-1), reads=["Mt"], writes=["Mt"])
    m.op(P, lambda e: e.affine_select(out=Mt, in_=Mt, pattern=[[-1, 256]], compare_op=ALU.is_ge, fill=8.0 * NEG8,
                                      base=128, channel_multiplier=1), reads=["Mt"], writes=["Mt"])
    m.op(V, lambda e: e.tensor_copy(out=Mt8, in_=Mt), reads=["Mt"], writes=["Mt8"])
    m.op(V, lambda e: e.tensor_copy(out=Mt8_0[:, 128:256], in_=Mt[:, 128:256]), reads=["Mt"], writes=["Mt8_0"])
    m.op(V, lambda e: e.scalar_tensor_tensor(out=Mt8_0[:, 0:128], in0=hm_sb[:, 0:1].broadcast_to([128, 128]), scalar=8.0,
                                            in1=Mt[:, 0:128], op0=ALU.mult, op1=ALU.add),
         reads=["Mt", "hm"], writes=["Mt8_0"])
    m.op(P, lambda e: e.iota(Kr_i, pattern=[[1, 256]], base=0, channel_multiplier=0), writes=["Kr_i"])
    m.op(V, lambda e: e.tensor_copy(out=Kr, in_=Kr_i), reads=["Kr_i"], writes=["Kr"])
    m.op(P, lambda e: e.iota(qc_i, pattern=[[0, 1]], base=128, channel_multiplier=1), writes=["qc_i"])
    m.op(V, lambda e: e.tensor_copy(out=qcol, in_=qc_i), reads=["qc_i"], writes=["qcol"])
    m.op(P, lambda e: e.memset(Sl, 0.0), writes=["Sl"])
    import ml_dtypes as _mld
    for h in range(NH):
        slope = 2.0 ** (-8.0 * (h + 1) / NH)
        hi = float(np.float32(8.0 * slope).astype(_mld.bfloat16))
        mid = float(np.float32(8.0 * slope - hi).astype(_mld.bfloat16))
        lo = float(np.float32(8.0 * slope - hi - mid).astype(_mld.bfloat16))
        m.op(P, lambda e, h=h, hi=hi: e.memset(Sl[0:1, h, :], hi), reads=["Sl"], writes=["Sl"])
        m.op(P, lambda e, h=h, mid=mid: e.memset(Sl[32:33, h, :], mid), reads=["Sl"], writes=["Sl"])
        m.op(P, lambda e, h=h, lo=lo: e.memset(Sl[64:65, h, :], lo), reads=["Sl"], writes=["Sl"])
        m.op(V, lambda e, h=h, slope=slope: e.scalar_tensor_tensor(out=sc[:, h:h + 1], in0=qcol, scalar=slope, in1=sink_bc[:, h:h + 1],
                                                                  op0=ALU.mult, op1=ALU.add),
             reads=["qcol", "sink_bc"], writes=["sc"])
    m.op(V, lambda e: e.tensor_scalar(out=nsc, in0=sc, scalar1=-1.0, scalar2=None, op0=ALU.mult), reads=["sc"], writes=["nsc"])

    astat = {}

    def a_s1(tt):
        xt = xb[tt % 3]
        m.dma(lambda e: e.dma_start(out=xt, in_=xin[tt * 128:(tt + 1) * 128, :]), writes=[f"xb{tt % 3}"], sem=f"xb{tt % 3}")
        stc = scol()
        ss = stat[:, stc:stc + 1]
        skey = f"stat{stc}"
        astat[tt] = (ss, skey)
        m.op(S, lambda e: e.activation(out=hn[tt % 3], in_=xt, func=AF.Square, accum_out=ss),
             reads=[f"xb{tt % 3}"], writes=[f"hn{tt % 3}", skey])
        rstd_chain(ss, skey, 1.0 / D)

    def a_s2(tt):
        ss, skey = astat[tt]
        m.op(V, lambda e: e.scalar_tensor_tensor(out=hn[tt % 3], in0=xb[tt % 3], scalar=ss, in1=gA, op0=ALU.mult, op1=ALU.mult),
             reads=[f"xb{tt % 3}", skey, "gA"], writes=[f"hn{tt % 3}"])

    def a_s3(tt):
        hnb, hnkey = hn[tt % 3], f"hn{tt % 3}"
        for half in range(2):
            b = 2 * (tt % 4) + half
            pb = bank_bf(b).rearrange("p (a c) -> p a c", c=128)
            for j in range(8):
                dc = half * 8 + j
                m.op("pe", lambda e, o=pb[:, j, :], i=hnb[:, dc * 128:(dc + 1) * 128]: e.transpose(out=o, in_=i, identity=ident),
                     reads=[hnkey, "ident"], writes=[f"pb{b}"])
            dst = hT[:, half * 8:(half + 1) * 8, tt * 128:(tt + 1) * 128]
            if half == 0:
                m.op(S, lambda e, o=dst, i=pb[:, 0:8, :]: e.copy(out=o, in_=i), reads=[f"pb{b}"], writes=[f"hT{tt}"])
            else:
                m.op(V, lambda e, o=dst, i=pb[:, 0:8, :]: e.tensor_copy(out=o, in_=i), reads=[f"pb{b}"], writes=[f"hT{tt}"])

    ag_bc = alloc([128, 1024], F32)
    qT_all = alloc([128, 8, OWN], BF16)
    kT = alloc([128, TOK], BF16)
    v_sb = alloc([128, TT, 128], BF16)
    p_bf = [alloc([128, 256], BF16) for _ in range(3)]
    pT_sb = [alloc([128, 2, 128], BF16) for _ in range(2)]
    attn_sb = alloc([128, NH, 64], F32)
    an_bf = alloc([128, 1024], BF16)
    hstat = [alloc([128, 5, NH], F32) for _ in range(2)]

    def hkeys(t0, t1):
        return [f"hT{t}" for t in range(t0 // 128, (t1 + 127) // 128)]

    m.dma(lambda e: e.dma_start(out=ag_bc, in_=attn_gain.partition_broadcast(128)), writes=["ag_bc"], sem="ag_bc")
    wkv = alloc([128, DC, 256], BF16)
    wkv_keys = ["wkv"]
    m.dma(lambda e: e.dma_start(out=wkv, in_=rows_view(w_in, C_K, 256)), writes=wkv_keys, sem="wkv", queue="pool", free=True)
    slabs = [(0, 512), (512, 1024), (1024, 1152)]

    def kslab(si):
        t0, t1 = slabs[si]
        b = si % 2
        n = t1 - t0
        for dc in range(DC):
            m.op("pe", lambda e, dc=dc: e.matmul(bank(b)[:, 0:n], lhsT=wkv[:, dc, 0:128],
                                                 rhs=hT[:, dc, t0:t1], start=(dc == 0), stop=(dc == DC - 1)),
                 reads=wkv_keys + hkeys(t0, t1), writes=[f"pb{b}"])
        m.op(S, lambda e: e.copy(out=kT[:, t0:t1], in_=bank(b)[:, 0:n]), reads=[f"pb{b}"], writes=["kT"])

    def vgrp(grp):
        b = 2 + grp % 2
        tts = list(range(grp * 4, min(TT, grp * 4 + 4)))
        for j, tt in enumerate(tts):
            for dc in range(DC):
                m.op("pe", lambda e, j=j, tt=tt, dc=dc: e.matmul(bank(b)[:, j * 128:(j + 1) * 128],
                                                                 lhsT=hT[:, dc, tt * 128:(tt + 1) * 128],
                                                                 rhs=wkv[:, dc, 128:256], start=(dc == 0), stop=(dc == DC - 1)),
                     reads=wkv_keys + [f"hT{tt}"], writes=[f"pb{b}"])
        nt_ = len(tts)
        m.op(V, lambda e: e.tensor_copy(out=v_sb[:, tts[0]:tts[0] + nt_, :],
                                        in_=bank(b)[:, 0:nt_ * 128].rearrange("p (a c) -> p a c", c=128)),
             reads=[f"pb{b}"], writes=["v_sb"])

    def qproj(pr, sl):
        base, keys, sem = walloc(1)
        wq = base.rearrange("p (a b c) -> p a b c", b=2, c=64)
        for hh in range(2):
            c0 = C_Q + (pr + 8 * hh) * 64
            m.dma(lambda e, hh=hh, c0=c0: e.dma_start(out=wq[:, :, hh, :], in_=rows_view(w_in, c0, 64)),
                  writes=keys, sem=sem, queue="pool", free=True)
        b = 4 + (pr * 2 + sl) % 2
        t0 = 128 + sl * 512
        for dc in range(DC):
            m.op("pe", lambda e, dc=dc: e.matmul(bank(b)[:, 0:512], lhsT=wq[:, dc, :, :],
                                                 rhs=hT[:, dc, t0:t0 + 512], start=(dc == 0), stop=(dc == DC - 1)),
                 reads=keys + hkeys(t0, t0 + 512), writes=[f"pb{b}"])
        if sl == 0:
            m.op(S, lambda e: e.copy(out=qT_all[:, pr, sl * 512:(sl + 1) * 512], in_=bank(b)[:, 0:512]),
                 reads=[f"pb{b}"], writes=[f"qT{pr}"])
        else:
            m.op(V, lambda e: e.tensor_copy(out=qT_all[:, pr, sl * 512:(sl + 1) * 512], in_=bank(b)[:, 0:512]),
                 reads=[f"pb{b}"], writes=[f"qT{pr}"])

    bunits = ([(3, lambda: kslab(0)), (3, lambda: vgrp(0))] + [(4, (lambda pr=pr: qproj(pr, 0))) for pr in range(8)]
              + [(7, lambda: kslab(1)), (7, lambda: vgrp(1))] + [(8, (lambda pr=pr: qproj(pr, 1))) for pr in range(8)]
              + [(8, lambda: kslab(2)), (8, lambda: vgrp(2))])

    for step in range(TT + 2):
        for lag, fn in ((0, a_s1), (1, a_s2), (2, a_s3)):
            if 0 <= step - lag < TT:
                fn(step - lag)
        nem = 0
        while bunits and bunits[0][0] <= step - 2 and nem < 3:
            bunits.pop(0)[1]()
            nem += 1
    while bunits:
        bunits.pop(0)[1]()

    pending_tail = []
    for qt in range(NT):
        hs = hstat[qt % 2]
        hsk = f"hstat{qt % 2}"
        o_ps = pd[2 + qt % 2]
        okeys = [f"pb{4 + 2 * (qt % 2)}", f"pb{5 + 2 * (qt % 2)}"]

        def st_S(h, qt=qt, hs=hs, hsk=hsk):
            pr, hh = h % 8, h // 8
            i = h % 3
            mt, mk = (Mt8_0, "Mt8_0") if qt == 0 else (Mt8, "Mt8")
            m.op("pe", lambda e: e.matmul(bank(i)[:, 0:256], lhsT=qT_all[hh * 64:(hh + 1) * 64, pr, qt * 128:(qt + 1) * 128],
                                          rhs=kT[hh * 64:(hh + 1) * 64, qt * 128:qt * 128 + 256], start=True, stop=False),
                 reads=[f"qT{pr}", "kT"], writes=[f"pb{i}"])
            m.op("pe", lambda e: e.matmul(bank(i)[:, 0:256], lhsT=ident, rhs=mt, start=False, stop=False),
                 reads=["ident", mk], writes=[f"pb{i}"])
            m.op("pe", lambda e: e.matmul(bank(i)[:, 0:256], lhsT=Sl[:, h, :], rhs=Kr, start=False, stop=True),
                 reads=["Sl", "Kr"], writes=[f"pb{i}"])
            m.op(V, lambda e: e.tensor_reduce(out=hs[:, 0, h:h + 1], in_=bank(i)[:, 0:256], axis=mybir.AxisListType.X, op=ALU.max),
                 reads=[f"pb{i}"], writes=[f"{hsk}_rmax{h}"])
            m.op(V, lambda e: e.tensor_scalar(out=hs[:, 1, h:h + 1], in0=hs[:, 0, h:h + 1], scalar1=-0.125, scalar2=nsc[:, h:h + 1],
                                             op0=ALU.mult, op1=ALU.min),
                 reads=[f"{hsk}_rmax{h}", "nsc"], writes=[f"{hsk}_negm{h}"])
            m.op(S, lambda e: e.activation(out=p_bf[i], in_=bank(i)[:, 0:256], func=AF.Exp, bias=hs[:, 1, h:h + 1], scale=0.125,
                                          accum_out=hs[:, 2, h:h + 1]),
                 reads=[f"pb{i}", f"{hsk}_negm{h}"], writes=[f"p_bf{i}", f"{hsk}_rsum{h}"])

        def st_T(h, qt=qt):
            i = h % 3
            j = h % 2
            pb = bank_bf(3).rearrange("p (a c) -> p a c", c=128)
            for kb in range(2):
                m.op("pe", lambda e, kb=kb: e.transpose(out=pb[:, kb, :], in_=p_bf[i][:, kb * 128:(kb + 1) * 128], identity=ident),
                     reads=[f"p_bf{i}", "ident"], writes=["pb3"])
            if j == 0:
                m.op(S, lambda e: e.copy(out=pT_sb[j], in_=pb[:, 0:2, :]), reads=["pb3"], writes=[f"pT_sb{j}"])
            else:
                m.op(V, lambda e: e.tensor_copy(out=pT_sb[j], in_=pb[:, 0:2, :]), reads=["pb3"], writes=[f"pT_sb{j}"])

        def st_PV(h, qt=qt, o_ps=o_ps, okeys=okeys):
            i = h % 2
            hh = h // 8
            for kb in range(2):
                m.op("pe", lambda e, kb=kb: e.matmul(o_ps[:, h * 64:(h + 1) * 64], lhsT=pT_sb[i][:, kb, :],
                                                     rhs=v_sb[:, qt + kb, hh * 64:(hh + 1) * 64], start=(kb == 0), stop=(kb == 1)),
                     reads=[f"pT_sb{i}", "v_sb"], writes=[okeys[h // 8]])

        def tail(qt=qt, hs=hs, hsk=hsk, o_ps=o_ps, okeys=okeys):
            allk = lambda nm: [f"{hsk}_{nm}{h}" for h in range(NH)]
            stc = scol()
            ss = stat[:, stc:stc + 1]
            skey = f"stat{stc}"
            attn_flat = attn_sb.rearrange("p h c -> p (h c)")
            g = []
            g.append(lambda: m.op(V, lambda e: e.tensor_tensor(out=hs[:, 3, :], in0=sc, in1=hs[:, 1, :], op=ALU.add),
                                  reads=["sc"] + allk("negm"), writes=[f"{hsk}_es"]))
            g.append(lambda: m.op(S, lambda e: e.activation(out=hs[:, 3, :], in_=hs[:, 3, :], func=AF.Exp),
                                  reads=[f"{hsk}_es"], writes=[f"{hsk}_es"]))
            g.append(lambda: m.op(V, lambda e: e.tensor_tensor(out=hs[:, 3, :], in0=hs[:, 3, :], in1=hs[:, 2, :], op=ALU.add),
                                  reads=[f"{hsk}_es"] + allk("rsum"), writes=[f"{hsk}_es"]))
            g.append(lambda: m.op(V, lambda e: e.reciprocal(out=hs[:, 4, :], in_=hs[:, 3, :]), reads=[f"{hsk}_es"], writes=[f"{hsk}_rinv"]))
            g.append(lambda: m.op(V, lambda e: e.tensor_tensor(out=attn_sb, in0=o_ps.rearrange("p (h c) -> p h c", c=64),
                                                              in1=hs[:, 4, :].unsqueeze(2).broadcast_to([128, NH, 64]), op=ALU.mult),
                                  reads=okeys + [f"{hsk}_rinv"], writes=["attn_sb"]))
            g.append(lambda: m.op(S, lambda e: e.activation(out=an_bf, in_=attn_flat, func=AF.Square, accum_out=ss),
                                  reads=["attn_sb"], writes=["an_bf", skey]))
            g.append(lambda: m.op(V, lambda e: e.tensor_scalar(out=ss, in0=ss, scalar1=1.0 / 1024, scalar2=EPS, op0=ALU.mult, op1=ALU.add),
                                  reads=[skey], writes=[skey]))
            g.append(lambda: m.op(S, lambda e: e.activation(out=ss, in_=ss, func=AF.Sqrt), reads=[skey], writes=[skey]))
            g.append(lambda: m.op(V, lambda e: e.reciprocal(out=ss, in_=ss), reads=[skey], writes=[skey]))
            g.append(lambda: m.op(V, lambda e: e.scalar_tensor_tensor(out=an_bf, in0=attn_flat, scalar=ss, in1=ag_bc, op0=ALU.mult, op1=ALU.mult),
                                  reads=["attn_sb", skey, "ag_bc"], writes=["an_bf"]))

            def tr():
                b = 3
                pb = bank_bf(b).rearrange("p (a c) -> p a c", c=128)
                for j in range(8):
                    m.op("pe", lambda e, j=j: e.transpose(out=pb[:, j, :], in_=an_bf[:, j * 128:(j + 1) * 128], identity=ident),
                         reads=["an_bf", "ident"], writes=[f"pb{b}"])
                m.op(S, lambda e: e.copy(out=catT[:, 0:8, qt * 128:(qt + 1) * 128], in_=pb[:, 0:8, :]),
                     reads=[f"pb{b}"], writes=[f"catA{qt}"])
            g.append(tr)
            return g

        for step in range(NH + 3):
            if step < NH:
                st_S(step)
            if 0 <= step - 2 < NH:
                st_T(step - 2)
            if 0 <= step - 3 < NH:
                st_PV(step - 3)
            if step >= 4 and pending_tail:
                pending_tail.pop(0)()
        for f in pending_tail:
            f()
        pending_tail = tail()
    for f in pending_tail:
        f()
    m.barrier(lambda e: e.memset(scr[:, 1:2], 0.0))
    state["off"] = ac_mark
    smask = alloc([128, TOK], F32)
    tA = [alloc([128, TOK], F32) for _ in range(2)]
    tQ = [alloc([128, OWN], F32) for _ in range(2)]
    tB = alloc([128, TOK], F32)
    tC = alloc([128, TOK], F32)
    tE = alloc([128, TOK], F32)
    tF = alloc([128, TOK], F32)
    kd_bf = alloc([128, TOK], BF16)
    gate = [alloc([128, OWN], F32) for _ in range(3)]
    i_tm = [alloc([128, TT, 128], BF16) for _ in range(3)]
    qb = [alloc([128, OWN], BF16) for _ in range(2)]
    kb_ = [alloc([128, TOK], BF16) for _ in range(2)]
    kdtm = [alloc([128, TT, 128], BF16) for _ in range(2)]
    decs = [alloc([128, 18], F32) for _ in range(2)]
    S_f = [alloc([128, 128], F32) for _ in range(2)]
    S_bf = alloc([128, 16, 128], BF16)
    attT_bf = [alloc([128, 128], BF16) for _ in range(2)]
    osq = alloc([128, 512], F32)
    rst = alloc([128, 512], F32)
    t1 = alloc([128, 512], F32)

    m.op(P, lambda e: e.memset(smask, 1.0), writes=["smask"])
    m.op(P, lambda e: e.memset(smask.rearrange("p (c t) -> p c t", t=64)[:, :, 0:1], 0.0), reads=["smask"], writes=["smask"])
    pcnt = {"b": 0}

    headw = {}

    def load_head(h):
        headw[h] = (load_w(rows_view(w_in, C_FR + h * 128, 128), wview3(128), 1),
                    load_w(rows_view(w_in, C_QR + h * 128, 128), wview3(128), 1),
                    load_w(rows_view(w_in, C_GR + h * 128, 128), wview3(128), 1),
                    load_w(rows_view(w_in, C_IR + h * 128, 128), wview3(128), 1))

    def P_items(h):
        h2, h3 = h % 2, h % 3
        if h not in headw:
            load_head(h)
        (wfr, kf), (wqr, kq), (wgr, kg), (wir, ki) = headw.pop(h)
        if h + 1 < 8:
            load_head(h + 1)
        items = []

        def slab(w, wk, t0, t1_, func, dst, dkey):
            def f():
                b = pcnt["b"] % 3
                pcnt["b"] += 1
                n = t1_ - t0
                for dc in range(DC):
                    m.op("pe", lambda e, dc=dc: e.matmul(bank(b)[:, 0:n], lhsT=w[:, dc, :], rhs=hT[:, dc, t0:t1_],
                                                         start=(dc == 0), stop=(dc == DC - 1)),
                         reads=wk + hkeys(t0, t1_), writes=[f"pb{b}"])
                m.op(S, lambda e: e.activation(out=dst, in_=bank(b)[:, 0:n], func=func), reads=[f"pb{b}"], writes=[dkey])
            return f

        for (t0, t1_) in slabs:
            items.append(slab(wfr, kf, t0, t1_, AF.Sigmoid, tA[h2][:, t0:t1_], f"tA{h2}"))
        for sl in range(2):
            t0 = 128 + sl * 512
            items.append(slab(wqr, kq, t0, t0 + 512, AF.Silu, tQ[h2][:, sl * 512:(sl + 1) * 512], f"tQ{h2}"))
        for sl in range(2):
            t0 = 128 + sl * 512
            items.append(slab(wgr, kg, t0, t0 + 512, AF.Silu, gate[h3][:, sl * 512:(sl + 1) * 512], f"gate{h3}"))

        def igrp(grp):
            def f():
                b = pcnt["b"] % 3
                pcnt["b"] += 1
                tts = list(range(grp * 4, min(TT, grp * 4 + 4)))
                for j, tt in enumerate(tts):
                    for dc in range(DC):
                        m.op("pe", lambda e, j=j, tt=tt, dc=dc: e.matmul(bank(b)[:, j * 128:(j + 1) * 128],
                                                                         lhsT=hT[:, dc, tt * 128:(tt + 1) * 128], rhs=wir[:, dc, :],
                                                                         start=(dc == 0), stop=(dc == DC - 1)),
                             reads=ki + [f"hT{tt}"], writes=[f"pb{b}"])
                nt_ = len(tts)
                m.op(V, lambda e: e.tensor_copy(out=i_tm[h3][:, tts[0]:tts[0] + nt_, :],
                                                in_=bank(b)[:, 0:nt_ * 128].rearrange("p (a c) -> p a c", c=128)),
                     reads=[f"pb{b}"], writes=[f"i_tm{h3}"])
            return f

        for grp in range(3):
            items.append(igrp(grp))
        return items

    def E_chain(h):
        ops = []
        h2 = h % 2
        A_, Q_ = tA[h2], tQ[h2]
        ak, qk = f"tA{h2}", f"tQ{h2}"
        ops.append(lambda: m.op(V, lambda e: e.tensor_scalar(out=A_, in0=A_, scalar1=omlT[:, h:h + 1], scalar2=lbT[:, h:h + 1], op0=ALU.mult, op1=ALU.add),
             reads=[ak, "omlT", "lbT"], writes=[ak]))
        ops.append(lambda: m.op(S, lambda e: e.activation(out=tB, in_=A_, func=AF.Ln),
                                reads=[ak], writes=["tB"]))
        ops.append(lambda: m.op(V, lambda e: e.tensor_tensor_scan(out=tC, data0=smask, data1=tB, initial=0.0, op0=ALU.mult, op1=ALU.add),
             reads=["smask", "tB"], writes=["tC"]))
        ops.append(lambda: m.op(S, lambda e: e.activation(out=tE, in_=tC, func=AF.Exp), reads=["tC"], writes=["tE"]))
        ops.append(lambda: m.op(S, lambda e: e.activation(out=tF, in_=tC, func=AF.Exp, scale=-1.0), reads=["tC"], writes=["tF"]))
        ops.append(lambda: m.op(V, lambda e: e.tensor_scalar(out=A_, in0=A_, scalar1=-1.0, scalar2=1.0, op0=ALU.mult, op1=ALU.add),
             reads=[ak], writes=[ak]))
        ops.append(lambda: m.op(V, lambda e: e.tensor_tensor(out=qb[h2], in0=Q_, in1=tE[:, 128:TOK], op=ALU.mult), reads=[qk, "tE"], writes=[f"qb{h2}"]))
        ops.append(lambda: m.op(V, lambda e: e.tensor_tensor(out=kb_[h2], in0=A_, in1=tF, op=ALU.mult), reads=[ak, "tF"], writes=[f"kb{h2}"]))
        tE3 = tE.rearrange("p (c t) -> p c t", t=64)
        ops.append(lambda: m.op(V, lambda e: e.tensor_copy(out=decs[h2], in_=tE3[:, :, 63]), reads=["tE"], writes=[f"decs{h2}"]))
        ops.append(lambda: m.op(V, lambda e: e.tensor_tensor(out=tB.rearrange("p (c t) -> p c t", t=64), in0=tF.rearrange("p (c t) -> p c t", t=64),
                                         in1=tE3[:, :, 63:64].broadcast_to([128, 18, 64]), op=ALU.mult),
             reads=["tF", "tE"], writes=["tB"]))
        ops.append(lambda: m.op(V, lambda e: e.tensor_tensor(out=kd_bf, in0=A_, in1=tB, op=ALU.mult), reads=[ak, "tB"], writes=["kd_bf"]))
        return ops

    def T_items(h):
        h2 = h % 2
        items = []

        def tgrp(grp):
            def f():
                b = 7
                pb = bank_bf(b).rearrange("p (a c) -> p a c", c=128)
                tts = list(range(grp * 8, min(TT, grp * 8 + 8)))
                for j, tt in enumerate(tts):
                    m.op("pe", lambda e, j=j, tt=tt: e.transpose(out=pb[:, j, :], in_=kd_bf[:, tt * 128:(tt + 1) * 128], identity=ident),
                         reads=["kd_bf", "ident"], writes=[f"pb{b}"])
                nt_ = len(tts)
                m.op(S, lambda e: e.copy(out=kdtm[h2][:, tts[0]:tts[0] + nt_, :], in_=pb[:, 0:nt_, :]),
                     reads=[f"pb{b}"], writes=[f"kdtm{h2}"])
            return f

        return [tgrp(0), tgrp(1)]

    attT4 = alloc([128, 4, 128], BF16)

    def S2_items(h):
        h2, h3 = h % 2, h % 3

        def uslot(c):
            if c == 16:
                return 3, 0
            rnd = c // 8
            return (3 + 2 * rnd + (c % 2)), (c % 8) // 2

        def uburst(cs):
            for c in cs:
                pair, part = c // 2, c % 2
                ubk, usl = uslot(c)
                ub = bank(ubk)[:, usl * 128:(usl + 1) * 128]
                m.op("pe", lambda e, pair=pair, part=part, ub=ub: e.matmul(ub, lhsT=kdtm[h2][part * 64:(part + 1) * 64, pair, :],
                                                                          rhs=i_tm[h3][part * 64:(part + 1) * 64, pair, :], start=True, stop=True),
                     reads=[f"kdtm{h2}", f"i_tm{h3}"], writes=[f"pb{ubk}"])

        def chain(cs):
            for c in cs:
                cur, nxt = c % 2, (c + 1) % 2
                ubk, usl = uslot(c)
                ub = bank(ubk)[:, usl * 128:(usl + 1) * 128]
                if c >= 2:
                    m.op(S, lambda e, c=c, cur=cur: e.copy(out=S_bf[:, c - 2, :], in_=S_f[cur]), reads=[f"S_f{cur}"], writes=[f"S_bf{c - 2}"])
                m.op(V, lambda e, c=c, cur=cur, nxt=nxt, ub=ub: e.scalar_tensor_tensor(out=S_f[nxt], in0=S_f[cur], scalar=decs[h2][:, c:c + 1],
                                                                                  in1=ub, op0=ALU.mult, op1=ALU.add),
                     reads=[f"S_f{cur}", f"decs{h2}", f"pb{ubk}"], writes=[f"S_f{nxt}"])

        def att(bq):
            for pl in range(4):
                pair = 1 + bq * 4 + pl
                q0 = (pair - 1) * 128
                m.op("pe", lambda e, pair=pair, q0=q0, pl=pl: e.matmul(bank(7)[:, pl * 128:(pl + 1) * 128], lhsT=kb_[h2][:, pair * 128:(pair + 1) * 128],
                                                                      rhs=qb[h2][:, q0:q0 + 128], start=True, stop=True),
                     reads=[f"kb{h2}", f"qb{h2}"], writes=["pb7"])
            m.op(V, lambda e: e.tensor_tensor(out=attT4, in0=bank(7).rearrange("p (a c) -> p a c", c=128),
                                             in1=mask2.unsqueeze(1).broadcast_to([128, 4, 128]), op=ALU.mult),
                 reads=["pb7", "mask2"], writes=["attT4"])

        def omm(bq):
            ob = 5 + bq % 2
            for pl in range(4):
                pair = 1 + bq * 4 + pl
                q0 = (pair - 1) * 128
                for part in range(2):
                    c = 2 * pair + part
                    col = pl * 128 + part * 64
                    oc = bank(ob)[:, col:col + 64]
                    m.op("pe", lambda e, c=c, oc=oc, part=part, q0=q0: e.matmul(oc, lhsT=S_bf[:, c - 2, :],
                                                                                rhs=qb[h2][:, q0 + part * 64:q0 + part * 64 + 64],
                                                                                start=True, stop=False),
                         reads=[f"S_bf{c - 2}", f"qb{h2}"], writes=[f"pb{ob}"])
                    m.op("pe", lambda e, oc=oc, part=part, pair=pair, pl=pl: e.matmul(
                        oc, lhsT=i_tm[h3][part * 64:(part + 1) * 64, pair, :],
                        rhs=attT4[part * 64:(part + 1) * 64, pl, part * 64:(part + 1) * 64], start=False, stop=True),
                         reads=[f"i_tm{h3}", "attT4"], writes=[f"pb{ob}"])

        def norm(bq):
            ob = 5 + bq % 2
            m.op(S, lambda e: e.activation(out=osq, in_=bank(ob), func=AF.Square), reads=[f"pb{ob}"], writes=["osq"])
            m.op("pe", lambda e: e.matmul(bank(7), lhsT=ones_f, rhs=osq, start=True, stop=True), reads=["ones_f", "osq"], writes=["pb7"])
            m.op(S, lambda e: e.activation(out=rst, in_=bank(7), func=AF.Ln, scale=1.0 / 128, bias=epsT[:, 0:1]),
                 reads=["pb7", "epsT"], writes=["rst"])
            m.op(S, lambda e: e.activation(out=rst, in_=rst, func=AF.Exp, scale=-0.5), reads=["rst"], writes=["rst"])
            m.op(V, lambda e: e.tensor_tensor(out=t1, in0=bank(ob), in1=rst, op=ALU.mult), reads=[f"pb{ob}", "rst"], writes=["t1"])
            m.op(V, lambda e: e.scalar_tensor_tensor(out=catT[:, 8 + h, bq * 512:(bq + 1) * 512], in0=t1, scalar=rg[:, 0:1],
                                                    in1=gate[h3][:, bq * 512:(bq + 1) * 512], op0=ALU.mult, op1=ALU.mult),
                 reads=["t1", "rg", f"gate{h3}"], writes=[f"catR{h}_{bq}"])

        def g0():
            m.op(V, lambda e: e.memset(S_f[0], 0.0), writes=["S_f0"])
            uburst(range(0, 8))

        def g1():
            chain(range(0, 8))
            uburst(range(8, 16))

        def g2():
            chain(range(8, 16))
            uburst([16])

        def g3():
            chain([16])
            m.op(S, lambda e: e.copy(out=S_bf[:, 15, :], in_=S_f[1]), reads=["S_f1"], writes=["S_bf15"])
            att(0)

        def g4():
            omm(0)

        def g5():
            norm(0)
            att(1)

        def g6():
            omm(1)

        def g7():
            norm(1)

        return [g0, g1, g2, g3, g4, g5, g6, g7, (lambda: None), (lambda: None)]

    if stop == "rnn0":
        for f in P_items(0):
            f()
        for f in E_chain(0):
            f()
        for f in T_items(0):
            f()
        for f in S2_items(0):
            f()
        stats = m.emit(final_wait_keys=[])
        return nc, stats
    for it in range(8 + 2):
        A = P_items(it) if it < 8 else [(lambda: None)] * 10
        B = S2_items(it - 2) if 0 <= it - 2 < 8 else [(lambda: None)] * 10
        C = E_chain(it - 1) if 0 <= it - 1 < 8 else []
        na, ncn = len(A), len(C)
        ci = 0
        for ai in range(na):
            A[ai]()
            tc = min(ncn, (ncn * (ai + 1) + 5) // 6)
            while ci < tc:
                C[ci]()
                ci += 1
            B[ai]()
            if ai == 7 and 0 <= it - 1 < 8:
                for f in T_items(it - 1):
                    f()
    cat_keys = [f"catA{qt}" for qt in range(NT)] + [f"catR{h}_{bq}" for h in range(8) for bq in range(2)]
    if debug:
        m.dma(lambda e: e.dma_start(out=dbg_cat, in_=catT.rearrange("p a b -> p (a b)")), reads=cat_keys, writes=["dbg_cat"], sem="dbg_cat")

    m.barrier(lambda e: e.memset(scr[:, 2:3], 0.0))
    state["off"] = setup_mark
    _catT_again = alloc([128, 16, OWN], BF16)
    h2T = alloc([128, DC, OWN], BF16)
    x1 = alloc([128, NT, D], F32)
    gP = alloc([128, D], F32)
    gQ = alloc([128, D], F32)
    xr = [alloc([128, D], F32) for _ in range(2)]
    hn2 = [alloc([128, D], BF16) for _ in range(2)]
    m.dma(lambda e: e.dma_start(out=gP, in_=g_mix_post.partition_broadcast(128)), writes=["gP"], sem="gP")
    m.dma(lambda e: e.dma_start(out=gQ, in_=g_mlp_pre.partition_broadcast(128)), writes=["gQ"], sem="gQ")
    out_keys = []

    def stream(pieces, load, compute, la):
        loaded = {}
        n = len(pieces)
        for k in range(min(la, n)):
            loaded[k] = load(pieces[k])
        for k in range(n):
            if k + la < n:
                loaded[k + la] = load(pieces[k + la])
            compute(pieces[k], *loaded.pop(k))

    def cat_keys_for(tok_tile):
        return [f"catA{tok_tile}"] + [f"catR{h}_{tok_tile // 4}" for h in range(8)]

    def d_load(pc):
        hf, cg = pc
        return load_w(rows_view(w_out, cg * 512, 512), wview3(512), 4)

    def d_comp(pc, wo, wk):
        hf, cg = pc
        for tt in range(4 * hf, 4 * hf + 4):
            b = (cg * 4 + tt) % 8
            for ec in range(DC):
                m.op("pe", lambda e, ec=ec, tt=tt, b=b: e.matmul(bank(b), lhsT=catT[:, ec, tt * 128:(tt + 1) * 128],
                                                                 rhs=wo[:, ec, :], start=(ec == 0), stop=(ec == DC - 1)),
                     reads=wk + cat_keys_for(tt), writes=[f"pb{b}"])
            if tt % 2 == 0:
                m.op(S, lambda e, tt=tt, b=b: e.copy(out=x1[:, tt, cg * 512:(cg + 1) * 512], in_=bank(b)),
                     reads=[f"pb{b}"], writes=[f"x1_{tt}"])
            else:
                m.op(V, lambda e, tt=tt, b=b: e.tensor_copy(out=x1[:, tt, cg * 512:(cg + 1) * 512], in_=bank(b)),
                     reads=[f"pb{b}"], writes=[f"x1_{tt}"])

    all_cat = [f"catA{qt}" for qt in range(NT)] + [f"catR{h}_{bq}" for h in range(8) for bq in range(2)]
    junkD = [catT[:, 4 * i:4 * i + 4, 0:512] for i in range(2)]
    junk_keys = [f"catA{t}" for t in range(4)]

    dstat = {}

    def d_s1a(tt):
        row0 = 128 + tt * 128
        m.dma(lambda e: e.dma_start(out=xr[tt % 2], in_=xin[row0:row0 + 128, :]), writes=[f"xr{tt % 2}"], sem=f"xr{tt % 2}")
        stc = scol()
        ss = stat[:, stc:stc + 1]
        skey = f"stat{stc}"
        dstat[(1, tt)] = (ss, skey)
        m.op(S, lambda e: e.activation(out=junkD[tt % 2], in_=x1[:, tt, :].rearrange("p (a b) -> p a b", b=512),
                                             func=AF.Square, accum_out=ss),
             reads=[f"x1_{tt}"], writes=junk_keys + [skey])
        rstd_chain(ss, skey, 1.0 / D)

    def d_s1b(tt):
        ss, skey = dstat[(1, tt)]
        m.op(V, lambda e: e.scalar_tensor_tensor(out=x1[:, tt, :], in0=x1[:, tt, :], scalar=ss, in1=gP,
                                                 op0=ALU.mult, op1=ALU.mult),
             reads=[f"x1_{tt}", skey, "gP"], writes=[f"x1_{tt}"])
        m.op(V, lambda e: e.tensor_tensor(out=x1[:, tt, :], in0=x1[:, tt, :], in1=xr[tt % 2], op=ALU.add),
             reads=[f"x1_{tt}", f"xr{tt % 2}"], writes=[f"x1_{tt}"])
        r0 = tt * 128
        m.dma(lambda e: e.dma_start(out=out[r0:r0 + 128, :], in_=x1[:, tt, :]), reads=[f"x1_{tt}"],
              writes=[f"out{tt}"], sem=f"outw{tt % 2}")
        if debug:
            m.dma(lambda e: e.dma_start(out=dbg_x1[r0:r0 + 128, :], in_=x1[:, tt, :]), reads=[f"x1_{tt}"],
                  writes=[f"dbgx{r0}"], sem=f"dbgx{tt % 2}")

    def d_s2a(tt):
        stc = scol()
        ss = stat[:, stc:stc + 1]
        skey = f"stat{stc}"
        dstat[(2, tt)] = (ss, skey)
        m.op(S, lambda e: e.activation(out=junkD[tt % 2], in_=x1[:, tt, :].rearrange("p (a b) -> p a b", b=512),
                                             func=AF.Square, accum_out=ss),
             reads=[f"x1_{tt}"], writes=junk_keys + [skey])
        rstd_chain(ss, skey, 1.0 / D)

    def d_s2b(tt):
        ss, skey = dstat[(2, tt)]
        hnb, hnkey = hn2[tt % 2], f"hn2{tt % 2}"
        m.op(V, lambda e: e.scalar_tensor_tensor(out=hnb, in0=x1[:, tt, :], scalar=ss, in1=gQ, op0=ALU.mult, op1=ALU.mult),
             reads=[f"x1_{tt}", skey, "gQ"], writes=[hnkey])

    def d_s3(tt):
        hnb, hnkey = hn2[tt % 2], f"hn2{tt % 2}"
        for half in range(2):
            b = 2 * (tt % 4) + half
            pb = bank_bf(b).rearrange("p (a c) -> p a c", c=128)
            for j in range(8):
                dc = half * 8 + j
                m.op("pe", lambda e, o=pb[:, j, :], i=hnb[:, dc * 128:(dc + 1) * 128]: e.transpose(out=o, in_=i, identity=ident),
                     reads=[hnkey, "ident"], writes=[f"pb{b}"])
            dst = h2T[:, half * 8:(half + 1) * 8, tt * 128:(tt + 1) * 128]
            if half == 0:
                m.op(S, lambda e, o=dst, i=pb[:, 0:8, :]: e.copy(out=o, in_=i), reads=[f"pb{b}"], writes=[f"h2T{tt}"])
            else:
                m.op(V, lambda e, o=dst, i=pb[:, 0:8, :]: e.tensor_copy(out=o, in_=i), reads=[f"pb{b}"], writes=[f"h2T{tt}"])

    def post_step(step, tiles):
        n = len(tiles)
        for lag, fn in ((0, d_s1a), (1, d_s1b), (2, d_s2a), (3, d_s2b), (4, d_s3)):
            if 0 <= step - lag < n:
                fn(tiles[step - lag])

    dpieces = [(hf, cg) for hf in range(2) for cg in range(4)]
    dloaded = {0: d_load(dpieces[0])}
    first_half_steps = {4: [0, 1], 5: [2, 3], 6: [4, 5], 7: [6, 7]}
    for k in range(8):
        if k + 1 < 8:
            dloaded[k + 1] = d_load(dpieces[k + 1])
        d_comp(dpieces[k], *dloaded.pop(k))
        for st_ in first_half_steps.get(k, []):
            post_step(st_, [0, 1, 2, 3])
    for st_ in range(8):
        post_step(st_, [4, 5, 6, 7])
    h2keys = [f"h2T{t}" for t in range(NT)]

    m.barrier(lambda e: e.memset(scr[:, 3:4], 0.0))
    state["off"] = setup_mark
    yA = alloc([128, 4, D], F32)
    _h2T_again = alloc([128, DC, OWN], BF16)
    yB = alloc([128, 4, D], F32)
    u2T = alloc([128, 32, OWN], BF16)
    rr = [alloc([128, 512], F32) for _ in range(2)]
    gP2 = alloc([128, D], F32)
    m.dma(lambda e: e.dma_start(out=gP2, in_=g_mlp_post.partition_broadcast(128)), writes=["gP2"], sem="gP2")

    def yv(tt):
        return (yA if tt < 4 else yB)[:, tt % 4, :]

    lastw = {}

    def final_tile(tt):
        stc = scol()
        ss = stat[:, stc:stc + 1]
        skey = f"stat{stc}"
        ri = tt % 2
        junk = h2T[:, ri * 2:ri * 2 + 2, :].rearrange("p a b -> p (a b)")
        jkeys = list(h2keys)
        m.op(S, lambda e, tt=tt, ss=ss, junk=junk: e.activation(out=junk, in_=yv(tt), func=AF.Square, accum_out=ss),
             reads=[f"y{tt}"], writes=jkeys + [skey])
        rstd_chain(ss, skey, 1.0 / D)
        m.op(V, lambda e, tt=tt, ss=ss: e.scalar_tensor_tensor(out=yv(tt), in0=yv(tt), scalar=ss, in1=gP2, op0=ALU.mult, op1=ALU.mult),
             reads=[f"y{tt}", skey, "gP2"], writes=[f"y{tt}"])
        r0 = tt * 128
        m.dma(lambda e, tt=tt, r0=r0: e.dma_start(out=out[r0:r0 + 128, :], in_=yv(tt), accum_op=ALU.add),
              reads=[f"y{tt}", f"out{tt}"], writes=[f"out{tt}"], sem=f"outa{tt}", queue="pool")
        out_keys.append(f"out{tt}")

    upc = {"n": 0}
    e_pieces = []
    for half in range(2):
        def u_load(fcg, half=half):
            return load_w(rows_view(w_up, (half * 16 + fcg) * 256, 256), wview3(256), 2)

        def u_comp(fcg, wu, wk, half=half):
            for fci in range(2):
                fcl = 2 * fcg + fci
                for sl in range(2):
                    b = upc["n"] % 8
                    ri = upc["n"] % 2
                    upc["n"] += 1
                    for dc in range(DC):
                        m.op("pe", lambda e, dc=dc, fci=fci, b=b, sl=sl: e.matmul(bank(b), lhsT=wu[:, dc, fci * 128:(fci + 1) * 128],
                                                                                  rhs=h2T[:, dc, sl * 512:(sl + 1) * 512],
                                                                                  start=(dc == 0), stop=(dc == DC - 1)),
                             reads=wk + h2keys[sl * 4:(sl + 1) * 4], writes=[f"pb{b}"])
                    m.op(S, lambda e, b=b, ri=ri: e.activation(out=rr[ri], in_=bank(b), func=AF.Relu), reads=[f"pb{b}"], writes=[f"rr{ri}"])
                    m.op(P, lambda e, fcl=fcl, ri=ri, sl=sl: e.tensor_tensor(out=u2T[:, fcl, sl * 512:(sl + 1) * 512], in0=rr[ri], in1=rr[ri], op=ALU.mult),
                         reads=[f"rr{ri}"], writes=[f"u2T{fcl}_{sl}"])

        e_pieces += [(u_load, u_comp, fcg) for fcg in range(16)]

        pieces = [(cg, r) for cg in range(4) for r in range(4)]

        def dn_load(pc, half=half):
            cg, r = pc
            f0 = (half * 32 + r * 8) * 128
            src = w_down[f0:f0 + 1024, cg * 512:(cg + 1) * 512].rearrange("(fc p) c -> p fc c", p=128)
            return load_w(src, wview3(512), 2)

        def dn_comp(pc, wd, wk, half=half):
            cg, r = pc
            if half == 1 and cg == 3:
                lastw[r] = (wd, wk)
                if r < 3:
                    return
                for tt in range(NT):
                    b = tt
                    for r2 in range(4):
                        wd2, wk2 = lastw[r2]
                        for j in range(8):
                            fcl = r2 * 8 + j
                            m.op("pe", lambda e, tt=tt, j=j, fcl=fcl, b=b, wd2=wd2, r2=r2: e.matmul(
                                bank(b), lhsT=u2T[:, fcl, tt * 128:(tt + 1) * 128], rhs=wd2[:, j, :],
                                start=(r2 == 0 and j == 0), stop=(r2 == 3 and j == 7)),
                                 reads=wk2 + [f"u2T{fcl}_{tt // 4}"], writes=[f"pb{b}"])
                    dst = yv(tt)[:, cg * 512:(cg + 1) * 512]
                    m.op(V, lambda e, dst=dst, b=b: e.tensor_tensor(out=dst, in0=dst, in1=bank(b), op=ALU.add),
                         reads=[f"pb{b}", f"y{tt}"], writes=[f"y{tt}"])
                    final_tile(tt)
                return
            for tt in range(NT):
                b = tt
                for j in range(8):
                    fcl = r * 8 + j
                    m.op("pe", lambda e, tt=tt, j=j, fcl=fcl, b=b: e.matmul(bank(b), lhsT=u2T[:, fcl, tt * 128:(tt + 1) * 128], rhs=wd[:, j, :],
                                                                            start=(r == 0 and j == 0), stop=(r == 3 and j == 7)),
                         reads=wk + [f"u2T{fcl}_{tt // 4}"], writes=[f"pb{b}"])
                if r == 3:
                    dst = yv(tt)[:, cg * 512:(cg + 1) * 512]
                    if half == 0:
                        if tt % 2 == 0:
                            m.op(S, lambda e, dst=dst, b=b: e.copy(out=dst, in_=bank(b)), reads=[f"pb{b}"], writes=[f"y{tt}"])
                        else:
                            m.op(V, lambda e, dst=dst, b=b: e.tensor_copy(out=dst, in_=bank(b)), reads=[f"pb{b}"], writes=[f"y{tt}"])
                    else:
                        m.op(V, lambda e, dst=dst, b=b: e.tensor_tensor(out=dst, in0=dst, in1=bank(b), op=ALU.add),
                             reads=[f"pb{b}", f"y{tt}"], writes=[f"y{tt}"])

        e_pieces += [(dn_load, dn_comp, pc) for pc in pieces]

    stream(e_pieces, lambda p: p[0](p[2]), lambda p, w, k: p[1](p[2], w, k), 3)

    fin = list(out_keys)
    if debug:
        fin += ["dbg_cat"] + [k for k in m.last_w if k.startswith("dbgx")]
    stats = m.emit(final_wait_keys=fin)
    return nc, stats


_CACHE = {}


def _prep_inputs(x, w_in, attn_sinks, attn_out_gain, rnn_lb_logits, rnn_norm_gain, w_out,
                 mix_pre_gain, mix_post_gain, mlp_pre_gain, mlp_post_gain, w_up, w_down, cores=range(8)):
    f = lambda a: np.ascontiguousarray(np.asarray(a, dtype=np.float32))
    x = f(x)
    shared = {
        "w_in": f(w_in)[0], "w_out": f(w_out)[0], "w_up": f(w_up)[0], "w_down": f(w_down)[0],
        "sinks": f(attn_sinks).reshape(1, NH), "attn_gain": f(attn_out_gain).reshape(1, 1024),
        "lb_logits": f(rnn_lb_logits).reshape(2, 1024), "rnn_gain": f(rnn_norm_gain).reshape(1, 128),
        "g_mix_pre": f(mix_pre_gain).reshape(1, D), "g_mix_post": f(mix_post_gain).reshape(1, D),
        "g_mlp_pre": f(mlp_pre_gain).reshape(1, D), "g_mlp_post": f(mlp_post_gain).reshape(1, D),
    }
    in_maps = []
    for c in cores:
        b, j = c // 4, c % 4
        xin = np.zeros((TOK, D), np.float32)
        xin[128:] = x[b, j * OWN:(j + 1) * OWN]
        if j > 0:
            xin[:128] = x[b, j * OWN - 128:j * OWN]
        hmv = np.full((128, 1), NEG8 if j == 0 else 0.0, np.float32)
        d = dict(shared)
        d["xin"] = xin
        d["hm"] = hmv
        in_maps.append(d)
    return in_maps


def kernel(**inputs):
    if "nc" not in _CACHE:
        _CACHE["nc"] = build_program(debug=False)[0]
    nc = _CACHE["nc"]
    in_maps = _prep_inputs(**inputs)
    res = run_bass_kernel_spmd(nc, in_maps, core_ids=list(range(8)))
    outp = np.zeros((2, 4096, D), np.float32)
    for c in range(8):
        b, j = c // 4, c % 4
        outp[b, j * OWN:(j + 1) * OWN] = res.results[c]["out"]
    return outp
```

```python
import numpy as np
import concourse.bass as bass
import concourse.mybir as mybir
from concourse.bass_utils import run_bass_kernel_spmd

F32 = mybir.dt.float32
BF16 = mybir.dt.bfloat16
I32 = mybir.dt.int32
U8 = mybir.dt.uint8
AF = mybir.ActivationFunctionType
ALU = mybir.AluOpType

ENGS = ("pe", "dve", "act", "pool", "sp")
HANDLES = {"pe": "tensor", "dve": "vector", "act": "scalar", "pool": "gpsimd", "sp": "sync"}


class _Op:
    __slots__ = ("eng", "fn", "deps", "needed", "dom", "val", "waits", "is_dma", "dsem", "idx")


class MK:
    def __init__(self, nc, same_engine_sync=True):
        self.nc = nc
        self.ops = []
        self.last_w = {}
        self.readers = {}
        self.same_engine_sync = same_engine_sync
        self.token = None
        self.last_eng = {}
        self.last_sem = {}

    def _record(self, eng, fn, reads, writes, is_dma=False, dsem=None, free=False, extra=()):
        op = _Op()
        op.eng, op.fn, op.is_dma, op.dsem = eng, fn, is_dma, dsem
        op.needed = False
        op.idx = len(self.ops)
        deps = set(extra)
        if self.token is not None and not free:
            deps.add(self.token)
        for k in reads:
            w = self.last_w.get(k)
            if w is not None:
                deps.add(w)
        for k in writes:
            w = self.last_w.get(k)
            if w is not None:
                deps.add(w)
            for r in self.readers.get(k, ()):
                deps.add(r)
        deps.discard(op.idx)
        op.deps = deps
        self.ops.append(op)
        for k in reads:
            self.readers.setdefault(k, []).append(op.idx)
        for k in writes:
            self.last_w[k] = op.idx
            self.readers[k] = []
        if is_dma:
            self.last_sem[dsem] = op.idx
        else:
            self.last_eng[eng] = op.idx
        return op

    def op(self, eng, fn, reads=(), writes=(), free=False):
        return self._record(eng, fn, tuple(reads), tuple(writes), free=free)

    def dma(self, fn, reads=(), writes=(), sem="dma0", queue="sp", free=False):
        return self._record(queue, fn, tuple(reads), tuple(writes), is_dma=True, dsem=sem, free=free)

    def barrier(self, fn, exclude_sem_prefix="wb"):
        extra = set(self.last_eng.values())
        for s, i in self.last_sem.items():
            if not s.startswith(exclude_sem_prefix):
                extra.add(i)
        op = self._record("dve", fn, (), (), extra=extra)
        self.token = op.idx
        return op

    def emit(self, final_wait_keys=()):
        nc = self.nc
        ops = self.ops
        self._record("sp", None, tuple(final_wait_keys), ())
        for op in ops:
            for d in op.deps:
                ops[d].needed = True
        cnt = {}
        clock_of = [None] * len(ops)
        eng_clock = {e: {} for e in ENGS}
        per_eng = {e: [] for e in ENGS}
        for op in ops:
            e = op.eng
            ec = eng_clock[e]
            waits = {}
            for d in op.deps:
                dop = ops[d]
                if (not dop.is_dma) and dop.eng == e and (e == "pe" or not self.same_engine_sync):
                    continue
                dom, val = dop.dom, dop.val
                if ec.get(dom, 0) >= val:
                    continue
                if waits.get(dom, 0) < val:
                    waits[dom] = val
            if waits:
                for d in op.deps:
                    dop = ops[d]
                    if dop.dom in waits and waits[dop.dom] >= dop.val:
                        for k2, v2 in clock_of[d].items():
                            if ec.get(k2, 0) < v2:
                                ec[k2] = v2
                for dom, val in waits.items():
                    if ec.get(dom, 0) < val:
                        ec[dom] = val
            op.waits = list(waits.items())
            if op.is_dma:
                dom = ("dma", op.dsem)
                cnt[dom] = cnt.get(dom, 0) + 16
                op.dom, op.val = dom, cnt[dom]
                ck = dict(ec)
                ck[dom] = op.val
                clock_of[op.idx] = ck
            else:
                dom = ("eng", e)
                if op.needed:
                    cnt[dom] = cnt.get(dom, 0) + 1
                    op.dom, op.val = dom, cnt[dom]
                    ck = dict(ec)
                    ck[dom] = op.val
                    clock_of[op.idx] = ck
                else:
                    op.dom, op.val = dom, None
            per_eng[e].append(op)
        self.stats = {e: len(v) for e, v in per_eng.items()}
        self.stats["waits"] = sum(len(o.waits) for o in ops)
        doms = set()
        for op in ops:
            if op.is_dma or op.needed:
                doms.add(op.dom)
        self.stats["sems"] = len(doms)
        from contextlib import ExitStack
        with ExitStack() as st:
            sem = {}
            for i, dom in enumerate(sorted(doms, key=str)):
                sem[dom] = st.enter_context(nc.semaphore(f"s{i}"))
            block = st.enter_context(nc.Block())

            def make(ename):
                lst = per_eng[ename]

                def body(eng):
                    for op in lst:
                        for dom, val in op.waits:
                            eng.wait_ge(sem[dom], val)
                        if op.fn is None:
                            continue
                        ins = op.fn(eng)
                        if op.is_dma:
                            ins.then_inc(sem[op.dom], 16)
                        elif op.needed:
                            ins.then_inc(sem[op.dom], 1)
                return body

            for ename in ENGS:
                if per_eng[ename]:
                    getattr(block, HANDLES[ename])(make(ename))
        return self.stats


D = 2048
DC = 16
NT = 8
TT = 9
TOK = 1152
OWN = 1024
NH = 16
DFF = 8192
EPS = 1e-6
NEG8 = -30000.0
ARENA = 206 * 1024
WUNIT = 4096
NWU = 8

C_Q, C_K, C_V, C_QR, C_FR, C_IR, C_GR = 0, 1024, 1152, 1280, 2304, 3328, 4352


def build_program(debug=False, stop=None, reg=None):
    nc = bass.Bass("TRN2", target_bir_lowering=False)
    reg = {} if reg is None else reg

    def dram(name, shape, dt=F32, kind="ExternalInput"):
        return nc.dram_tensor(name, list(shape), dt, kind=kind).ap()

    xin = dram("xin", [TOK, D])
    hm = dram("hm", [128, 1])
    w_in = dram("w_in", [D, 5376])
    w_out = dram("w_out", [D, D])
    w_up = dram("w_up", [D, DFF])
    w_down = dram("w_down", [DFF, D])
    sinks = dram("sinks", [1, NH])
    attn_gain = dram("attn_gain", [1, 1024])
    lb_logits = dram("lb_logits", [2, 1024])
    rnn_gain = dram("rnn_gain", [1, 128])
    g_mix_pre = dram("g_mix_pre", [1, D])
    g_mix_post = dram("g_mix_post", [1, D])
    g_mlp_pre = dram("g_mlp_pre", [1, D])
    g_mlp_post = dram("g_mlp_post", [1, D])
    out = dram("out", [OWN, D], kind="ExternalOutput")
    if debug:
        dbg_cat = dram("dbg_cat", [128, 16 * OWN], BF16, kind="ExternalOutput")
        dbg_x1 = dram("dbg_x1", [OWN, D], kind="ExternalOutput")

    m = MK(nc)
    arena = nc.alloc_sbuf_tensor("arena", [128, ARENA], U8)
    state = {"off": 0}

    def alloc(shape, dt=F32):
        esz = 2 if dt == BF16 else 4
        n = esz
        for s in shape[1:]:
            n *= s
        off = (state["off"] + 63) // 64 * 64
        reg[len(reg)] = (off, tuple(shape), "bf16" if dt == BF16 else ("i32" if dt == I32 else "f32"))
        assert off + n <= ARENA, f"SBUF arena overflow: {off + n} > {ARENA}"
        state["off"] = off + n
        v = arena[:, off:off + n].bitcast(dt)
        if len(shape) == 3:
            v = v.rearrange("p (a b) -> p a b", b=shape[2])
        elif len(shape) == 4:
            v = v.rearrange("p (a b c) -> p a b c", b=shape[2], c=shape[3])
        return v

    pd = [nc.alloc_psum_tensor(f"pd{i}", [128, 1024], F32) for i in range(4)]

    def bank(b):
        return pd[b // 2][:, (b % 2) * 512:(b % 2) * 512 + 512]

    def bank_bf(b):
        return bank(b).bitcast(BF16)

    V, S, A_, P = "dve", "act", "act", "pool"

    ident = alloc([128, 128], BF16)
    ones_f = alloc([128, 128], F32)
    mask2 = alloc([128, 128], F32)
    epsT = alloc([128, 1], F32)
    scr = alloc([128, 4], F32)
    sink_bc = alloc([128, NH], F32)
    rg = alloc([128, 1], F32)
    lbT = alloc([128, 8], F32)
    omlT = alloc([128, 8], F32)
    hm_sb = alloc([128, 1], F32)
    stat = alloc([128, 64], F32)
    wpool = alloc([128, NWU * WUNIT // 2], BF16)
    persist_mark = state["off"]
    stat_col = {"n": 0}

    def scol(n=1):
        c = stat_col["n"]
        stat_col["n"] += n
        assert stat_col["n"] <= 64
        return c

    wstate = {"next": 0}

    def walloc(units):
        u = wstate["next"]
        u = (u + units - 1) // units * units
        if u + units > NWU:
            u = 0
        wstate["next"] = u + units
        keys = [f"wb{u + i}" for i in range(units)]
        base = wpool[:, u * (WUNIT // 2):(u + units) * (WUNIT // 2)]
        return base, keys, f"wb{u}"

    def load_w(src_ap, view_fn, units):
        base, keys, sem = walloc(units)
        dst = view_fn(base)
        m.dma(lambda e, dst=dst, src=src_ap: e.dma_start(out=dst, in_=src), writes=keys, sem=sem,
              queue="pool", free=True)
        return dst, keys

    def wview3(n):
        return lambda base: base.rearrange("p (a b) -> p a b", b=n)

    def rows_view(w, c0, n):
        return w[:, c0:c0 + n].rearrange("(dc p) c -> p dc c", p=128)

    setup_mark = state["off"]
    catT = alloc([128, 16, OWN], BF16)
    identf = alloc([128, 128], F32)
    m.op(P, lambda e: e.memset(identf, 1.0), writes=["identf"])
    m.op(P, lambda e: e.affine_select(out=identf, in_=identf, pattern=[[-1, 128]], compare_op=ALU.is_equal,
                                      fill=0.0, base=0, channel_multiplier=1), reads=["identf"], writes=["identf"])
    m.op(V, lambda e: e.tensor_copy(out=ident, in_=identf), reads=["identf"], writes=["ident"])
    m.op(P, lambda e: e.memset(ones_f, 1.0), writes=["ones_f"])
    m.op(P, lambda e: e.memset(epsT, EPS), writes=["epsT"])
    m.op(V, lambda e: e.memset(scr, 0.0), writes=["scr"])
    m.op(P, lambda e: e.memset(mask2, 1.0), writes=["mask2"])
    m.op(P, lambda e: e.affine_select(out=mask2, in_=mask2, pattern=[[1, 128]], compare_op=ALU.is_ge,
                                      fill=0.0, base=0, channel_multiplier=-1), reads=["mask2"], writes=["mask2"])
    m.op(P, lambda e: e.memset(mask2[0:64, 64:128], 0.0), reads=["mask2"], writes=["mask2"])
    m.dma(lambda e: e.dma_start(out=sink_bc, in_=sinks.partition_broadcast(128)), writes=["sink_bc"], sem="c_sink")
    m.dma(lambda e: e.dma_start(out=rg, in_=rnn_gain.rearrange("o k -> k o")), writes=["rg"], sem="c_rg")
    m.dma(lambda e: e.dma_start(out=hm_sb, in_=hm), writes=["hm"], sem="c_hm")
    l0 = alloc([128, 8], F32)
    l1 = alloc([128, 8], F32)
    m.dma(lambda e: e.dma_start(out=l0, in_=lb_logits[0:1, :].rearrange("o (h k) -> k (o h)", k=128),
                                allow_slow_non_contiguous=True), writes=["l0"], sem="c_l0")
    m.dma(lambda e: e.dma_start(out=l1, in_=lb_logits[1:2, :].rearrange("o (h k) -> k (o h)", k=128),
                                allow_slow_non_contiguous=True), writes=["l1"], sem="c_l1")
    m.op(S, lambda e: e.activation(out=l0, in_=l0, func=AF.Exp), reads=["l0"], writes=["l0"])
    m.op(S, lambda e: e.activation(out=l1, in_=l1, func=AF.Exp), reads=["l1"], writes=["l1"])
    m.op(V, lambda e: e.tensor_tensor(out=omlT, in0=l0, in1=l1, op=ALU.add), reads=["l0", "l1"], writes=["omlT"])
    m.op(V, lambda e: e.reciprocal(out=omlT, in_=omlT), reads=["omlT"], writes=["omlT"])
    m.op(V, lambda e: e.tensor_tensor(out=lbT, in0=l0, in1=omlT, op=ALU.mult), reads=["l0", "omlT"], writes=["lbT"])
    m.op(V, lambda e: e.tensor_scalar(out=omlT, in0=lbT, scalar1=-1.0, scalar2=1.0, op0=ALU.mult, op1=ALU.add),
         reads=["lbT"], writes=["omlT"])

    hT = alloc([128, DC, TOK], BF16)
    ac_mark = state["off"]
    Mt = alloc([128, 256], F32)
    Mt8 = alloc([128, 256], BF16)
    Mt8_0 = alloc([128, 256], BF16)
    Kr_i = alloc([128, 256], I32)
    Kr = alloc([128, 256], BF16)
    Sl = alloc([128, NH, 128], BF16)
    qc_i = alloc([128, 1], I32)
    qcol = alloc([128, 1], F32)
    sc = alloc([128, NH], F32)
    nsc = alloc([128, NH], F32)
    ab_mark = state["off"]
    gA = alloc([128, D], F32)
    xb = [alloc([128, D], F32) for _ in range(3)]
    hn = [alloc([128, D], BF16) for _ in range(3)]
    m.dma(lambda e: e.dma_start(out=gA, in_=g_mix_pre.partition_broadcast(128)), writes=["gA"], sem="gA")

    def rstd_chain(ss_ap, key, n_inv):
        m.op(V, lambda e: e.tensor_scalar(out=ss_ap, in0=ss_ap, scalar1=n_inv, scalar2=EPS, op0=ALU.mult, op1=ALU.add),
             reads=[key], writes=[key])
        m.op(S, lambda e: e.activation(out=ss_ap, in_=ss_ap, func=AF.Sqrt), reads=[key], writes=[key])
        m.op(V, lambda e: e.reciprocal(out=ss_ap, in_=ss_ap), reads=[key], writes=[key])

    def norm_transpose(src, src_keys, gain, gain_key, hnb, hnkey, dstT, dst_key, col0, stc, bnk):
        ss = stat[:, stc:stc + 1]
        skey = f"stat{stc}"
        m.op(S, lambda e: e.activation(out=hnb, in_=src, func=AF.Square, accum_out=ss),
             reads=src_keys, writes=[hnkey, skey])
        rstd_chain(ss, skey, 1.0 / D)
        m.op(V, lambda e: e.scalar_tensor_tensor(out=hnb, in0=src, scalar=ss, in1=gain, op0=ALU.mult, op1=ALU.mult),
             reads=list(src_keys) + [skey, gain_key], writes=[hnkey])
        for half in range(2):
            b = bnk[half]
            pb = bank_bf(b).rearrange("p (a c) -> p a c", c=128)
            for j in range(8):
                dc = half * 8 + j
                m.op("pe", lambda e, o=pb[:, j, :], i=hnb[:, dc * 128:(dc + 1) * 128]: e.transpose(out=o, in_=i, identity=ident),
                     reads=[hnkey, "ident"], writes=[f"pb{b}"])
            eng = S if half == 0 else V
            dst = dstT[:, half * 8:(half + 1) * 8, col0:col0 + 128]
            if eng == S:
                m.op(S, lambda e, o=dst, i=pb[:, 0:8, :]: e.copy(out=o, in_=i), reads=[f"pb{b}"], writes=[dst_key])
            else:
                m.op(V, lambda e, o=dst, i=pb[:, 0:8, :]: e.tensor_copy(out=o, in_=i), reads=[f"pb{b}"], writes=[dst_key])

    m.op(P, lambda e: e.memset(Mt, 0.0), writes=["Mt"])
    m.op(P, lambda e: e.affine_select(out=Mt, in_=Mt, pattern=[[1, 256]], compare_op=ALU.is_ge, fill=8.0 * NEG8,
                                      base=-1, channel_multiplier=-1), reads=["Mt"], writes=["Mt"])
    m.op(P, lambda e: e.affine_select(out=Mt, in_=Mt, pattern=[[-1, 256]], compare_op=ALU.is_ge, fill=8.0 * NEG8,
                                      base=128, channel_multiplier=1), reads=["Mt"], writes=["Mt"])
    m.op(V, lambda e: e.tensor_copy(out=Mt8, in_=Mt), reads=["Mt"], writes=["Mt8"])
    m.op(V, lambda e: e.tensor_copy(out=Mt8_0[:, 128:256], in_=Mt[:, 128:256]), reads=["Mt"], writes=["Mt8_0"])
    m.op(V, lambda e: e.scalar_tensor_tensor(out=Mt8_0[:, 0:128], in0=hm_sb[:, 0:1].broadcast_to([128, 128]), scalar=8.0,
                                            in1=Mt[:, 0:128], op0=ALU.mult, op1=ALU.add),
         reads=["Mt", "hm"], writes=["Mt8_0"])
    m.op(P, lambda e: e.iota(Kr_i, pattern=[[1, 256]], base=0, channel_multiplier=0), writes=["Kr_i"])
    m.op(V, lambda e: e.tensor_copy(out=Kr, in_=Kr_i), reads=["Kr_i"], writes=["Kr"])
    m.op(P, lambda e: e.iota(qc_i, pattern=[[0, 1]], base=128, channel_multiplier=1), writes=["qc_i"])
    m.op(V, lambda e: e.tensor_copy(out=qcol, in_=qc_i), reads=["qc_i"], writes=["qcol"])
    m.op(P, lambda e: e.memset(Sl, 0.0), writes=["Sl"])
    import ml_dtypes as _mld
    for h in range(NH):
        slope = 2.0 ** (-8.0 * (h + 1) / NH)
        hi = float(np.float32(8.0 * slope).astype(_mld.bfloat16))
        mid = float(np.float32(8.0 * slope - hi).astype(_mld.bfloat16))
        lo = float(np.float32(8.0 * slope - hi - mid).astype(_mld.bfloat16))
        m.op(P, lambda e, h=h, hi=hi: e.memset(Sl[0:1, h, :], hi), reads=["Sl"], writes=["Sl"])
        m.op(P, lambda e, h=h, mid=mid: e.memset(Sl[32:33, h, :], mid), reads=["Sl"], writes=["Sl"])
        m.op(P, lambda e, h=h, lo=lo: e.memset(Sl[64:65, h, :], lo), reads=["Sl"], writes=["Sl"])
        m.op(V, lambda e, h=h, slope=slope: e.scalar_tensor_tensor(out=sc[:, h:h + 1], in0=qcol, scalar=slope, in1=sink_bc[:, h:h + 1],
                                                                  op0=ALU.mult, op1=ALU.add),
             reads=["qcol", "sink_bc"], writes=["sc"])
    m.op(V, lambda e: e.tensor_scalar(out=nsc, in0=sc, scalar1=-1.0, scalar2=None, op0=ALU.mult), reads=["sc"], writes=["nsc"])

    astat = {}

    def a_s1(tt):
        xt = xb[tt % 3]
        m.dma(lambda e: e.dma_start(out=xt, in_=xin[tt * 128:(tt + 1) * 128, :]), writes=[f"xb{tt % 3}"], sem=f"xb{tt % 3}")
        stc = scol()
        ss = stat[:, stc:stc + 1]
        skey = f"stat{stc}"
        astat[tt] = (ss, skey)
        m.op(S, lambda e: e.activation(out=hn[tt % 3], in_=xt, func=AF.Square, accum_out=ss),
             reads=[f"xb{tt % 3}"], writes=[f"hn{tt % 3}", skey])

    def a_s1c(tt):
        ss, skey = astat[tt]
        rstd_chain(ss, skey, 1.0 / D)

    def a_s2(tt):
        ss, skey = astat[tt]
        m.op(V, lambda e: e.scalar_tensor_tensor(out=hn[tt % 3], in0=xb[tt % 3], scalar=ss, in1=gA, op0=ALU.mult, op1=ALU.mult),
             reads=[f"xb{tt % 3}", skey, "gA"], writes=[f"hn{tt % 3}"])

    def a_s3(tt):
        hnb, hnkey = hn[tt % 3], f"hn{tt % 3}"
        for half in range(2):
            b = 2 * (tt % 4) + half
            pb = bank_bf(b).rearrange("p (a c) -> p a c", c=128)
            for j in range(8):
                dc = half * 8 + j
                m.op("pe", lambda e, o=pb[:, j, :], i=hnb[:, dc * 128:(dc + 1) * 128]: e.transpose(out=o, in_=i, identity=ident),
                     reads=[hnkey, "ident"], writes=[f"pb{b}"])
            dst = hT[:, half * 8:(half + 1) * 8, tt * 128:(tt + 1) * 128]
            if half == 0:
                m.op(S, lambda e, o=dst, i=pb[:, 0:8, :]: e.copy(out=o, in_=i), reads=[f"pb{b}"], writes=[f"hT{tt}"])
            else:
                m.op(V, lambda e, o=dst, i=pb[:, 0:8, :]: e.tensor_copy(out=o, in_=i), reads=[f"pb{b}"], writes=[f"hT{tt}"])

    for step in range(TT + 2):
        for lag, fn in ((0, a_s1), (1, a_s2), (2, a_s3), (0, a_s1c)):
            if 0 <= step - lag < TT:
                fn(step - lag)
    hT_keys = [f"hT{tt}" for tt in range(TT)]

    def hkeys(t0, t1):
        return [f"hT{t}" for t in range(t0 // 128, (t1 + 127) // 128)]

    m.barrier(lambda e: e.memset(scr[:, 0:1], 0.0))
    state["off"] = ab_mark
    ag_bc = alloc([128, 1024], F32)
    qT_all = alloc([128, 8, OWN], BF16)
    kT = alloc([128, TOK], BF16)
    v_sb = alloc([128, TT, 128], BF16)
    p_bf = [alloc([128, 256], BF16) for _ in range(3)]
    pT_sb = [alloc([128, 2, 128], BF16) for _ in range(2)]
    attn_sb = alloc([128, NH, 64], F32)
    an_bf = alloc([128, 1024], BF16)
    hstat = [alloc([128, 5, NH], F32) for _ in range(2)]

    m.dma(lambda e: e.dma_start(out=ag_bc, in_=attn_gain.partition_broadcast(128)), writes=["ag_bc"], sem="ag_bc")
    wkv, wkv_keys = load_w(rows_view(w_in, C_K, 256), wview3(256), 2)
    slabs = [(0, 512), (512, 1024), (1024, 1152)]
    for si, (t0, t1) in enumerate(slabs):
        b = si % 2
        n = t1 - t0
        for dc in range(DC):
            m.op("pe", lambda e, b=b, n=n, dc=dc, t0=t0, t1=t1: e.matmul(bank(b)[:, 0:n], lhsT=wkv[:, dc, 0:128],
                                                                        rhs=hT[:, dc, t0:t1], start=(dc == 0), stop=(dc == DC - 1)),
                 reads=wkv_keys + hkeys(t0, t1), writes=[f"pb{b}"])
        m.op(S, lambda e, b=b, n=n, t0=t0, t1=t1: e.copy(out=kT[:, t0:t1], in_=bank(b)[:, 0:n]),
             reads=[f"pb{b}"], writes=["kT"])
    for grp in range(3):
        b = 2 + grp % 2
        tts = list(range(grp * 4, min(TT, grp * 4 + 4)))
        for j, tt in enumerate(tts):
            for dc in range(DC):
                m.op("pe", lambda e, b=b, j=j, tt=tt, dc=dc: e.matmul(bank(b)[:, j * 128:(j + 1) * 128],
                                                                      lhsT=hT[:, dc, tt * 128:(tt + 1) * 128],
                                                                      rhs=wkv[:, dc, 128:256], start=(dc == 0), stop=(dc == DC - 1)),
                     reads=wkv_keys + [f"hT{tt}"], writes=[f"pb{b}"])
        nt_ = len(tts)
        m.op(V, lambda e, b=b, nt_=nt_, t0=tts[0]: e.tensor_copy(
            out=v_sb[:, t0:t0 + nt_, :], in_=bank(b)[:, 0:nt_ * 128].rearrange("p (a c) -> p a c", c=128)),
             reads=[f"pb{b}"], writes=["v_sb"])
    for pr in range(8):
        base, keys, sem = walloc(1)
        wq = base.rearrange("p (a b c) -> p a b c", b=2, c=64)
        for hh in range(2):
            c0 = C_Q + (pr + 8 * hh) * 64
            m.dma(lambda e, wq=wq, hh=hh, c0=c0: e.dma_start(out=wq[:, :, hh, :], in_=rows_view(w_in, c0, 64)),
                  writes=keys, sem=sem, queue="pool", free=True)
        for sl in range(2):
            b = 4 + (pr * 2 + sl) % 2
            t0 = 128 + sl * 512
            for dc in range(DC):
                m.op("pe", lambda e, b=b, wq=wq, dc=dc, t0=t0: e.matmul(bank(b)[:, 0:512], lhsT=wq[:, dc, :, :],
                                                                        rhs=hT[:, dc, t0:t0 + 512], start=(dc == 0), stop=(dc == DC - 1)),
                     reads=keys + hkeys(t0, t0 + 512), writes=[f"pb{b}"])
            eng = S if sl == 0 else V
            if eng == S:
                m.op(S, lambda e, b=b, pr=pr, sl=sl: e.copy(out=qT_all[:, pr, sl * 512:(sl + 1) * 512], in_=bank(b)[:, 0:512]),
                     reads=[f"pb{b}"], writes=[f"qT{pr}"])
            else:
                m.op(V, lambda e, b=b, pr=pr, sl=sl: e.tensor_copy(out=qT_all[:, pr, sl * 512:(sl + 1) * 512], in_=bank(b)[:, 0:512]),
                     reads=[f"pb{b}"], writes=[f"qT{pr}"])

    pending_tail = []
    for qt in range(NT):
        hs = hstat[qt % 2]
        hsk = f"hstat{qt % 2}"
        o_ps = pd[2 + qt % 2]
        okeys = [f"pb{4 + 2 * (qt % 2)}", f"pb{5 + 2 * (qt % 2)}"]

        def st_S(h, qt=qt, hs=hs, hsk=hsk):
            pr, hh = h % 8, h // 8
            i = h % 3
            mt, mk = (Mt8_0, "Mt8_0") if qt == 0 else (Mt8, "Mt8")
            m.op("pe", lambda e: e.matmul(bank(i)[:, 0:256], lhsT=qT_all[hh * 64:(hh + 1) * 64, pr, qt * 128:(qt + 1) * 128],
                                          rhs=kT[hh * 64:(hh + 1) * 64, qt * 128:qt * 128 + 256], start=True, stop=False),
                 reads=[f"qT{pr}", "kT"], writes=[f"pb{i}"])
            m.op("pe", lambda e: e.matmul(bank(i)[:, 0:256], lhsT=ident, rhs=mt, start=False, stop=False),
                 reads=["ident", mk], writes=[f"pb{i}"])
            m.op("pe", lambda e: e.matmul(bank(i)[:, 0:256], lhsT=Sl[:, h, :], rhs=Kr, start=False, stop=True),
                 reads=["Sl", "Kr"], writes=[f"pb{i}"])
            m.op(V, lambda e: e.tensor_reduce(out=hs[:, 0, h:h + 1], in_=bank(i)[:, 0:256], axis=mybir.AxisListType.X, op=ALU.max),
                 reads=[f"pb{i}"], writes=[f"{hsk}_rmax{h}"])
            m.op(V, lambda e: e.tensor_scalar(out=hs[:, 1, h:h + 1], in0=hs[:, 0, h:h + 1], scalar1=-0.125, scalar2=nsc[:, h:h + 1],
                                             op0=ALU.mult, op1=ALU.min),
                 reads=[f"{hsk}_rmax{h}", "nsc"], writes=[f"{hsk}_negm{h}"])
            m.op(S, lambda e: e.activation(out=p_bf[i], in_=bank(i)[:, 0:256], func=AF.Exp, bias=hs[:, 1, h:h + 1], scale=0.125,
                                          accum_out=hs[:, 2, h:h + 1]),
                 reads=[f"pb{i}", f"{hsk}_negm{h}"], writes=[f"p_bf{i}", f"{hsk}_rsum{h}"])

        def st_T(h, qt=qt):
            i = h % 3
            j = h % 2
            pb = bank_bf(3).rearrange("p (a c) -> p a c", c=128)
            for kb in range(2):
                m.op("pe", lambda e, kb=kb: e.transpose(out=pb[:, kb, :], in_=p_bf[i][:, kb * 128:(kb + 1) * 128], identity=ident),
                     reads=[f"p_bf{i}", "ident"], writes=["pb3"])
            if j == 0:
                m.op(S, lambda e: e.copy(out=pT_sb[j], in_=pb[:, 0:2, :]), reads=["pb3"], writes=[f"pT_sb{j}"])
            else:
                m.op(V, lambda e: e.tensor_copy(out=pT_sb[j], in_=pb[:, 0:2, :]), reads=["pb3"], writes=[f"pT_sb{j}"])

        def st_PV(h, qt=qt, o_ps=o_ps, okeys=okeys):
            i = h % 2
            hh = h // 8
            for kb in range(2):
                m.op("pe", lambda e, kb=kb: e.matmul(o_ps[:, h * 64:(h + 1) * 64], lhsT=pT_sb[i][:, kb, :],
                                                     rhs=v_sb[:, qt + kb, hh * 64:(hh + 1) * 64], start=(kb == 0), stop=(kb == 1)),
                     reads=[f"pT_sb{i}", "v_sb"], writes=[okeys[h // 8]])

        def tail(qt=qt, hs=hs, hsk=hsk, o_ps=o_ps, okeys=okeys):
            allk = lambda nm: [f"{hsk}_{nm}{h}" for h in range(NH)]
            stc = scol()
            ss = stat[:, stc:stc + 1]
            skey = f"stat{stc}"
            attn_flat = attn_sb.rearrange("p h c -> p (h c)")
            g = []
            g.append(lambda: m.op(V, lambda e: e.tensor_tensor(out=hs[:, 3, :], in0=sc, in1=hs[:, 1, :], op=ALU.add),
                                  reads=["sc"] + allk("negm"), writes=[f"{hsk}_es"]))
            g.append(lambda: m.op(S, lambda e: e.activation(out=hs[:, 3, :], in_=hs[:, 3, :], func=AF.Exp),
                                  reads=[f"{hsk}_es"], writes=[f"{hsk}_es"]))
            g.append(lambda: m.op(V, lambda e: e.tensor_tensor(out=hs[:, 3, :], in0=hs[:, 3, :], in1=hs[:, 2, :], op=ALU.add),
                                  reads=[f"{hsk}_es"] + allk("rsum"), writes=[f"{hsk}_es"]))
            g.append(lambda: m.op(V, lambda e: e.reciprocal(out=hs[:, 4, :], in_=hs[:, 3, :]), reads=[f"{hsk}_es"], writes=[f"{hsk}_rinv"]))
            g.append(lambda: m.op(V, lambda e: e.tensor_tensor(out=attn_sb, in0=o_ps.rearrange("p (h c) -> p h c", c=64),
                                                              in1=hs[:, 4, :].unsqueeze(2).broadcast_to([128, NH, 64]), op=ALU.mult),
                                  reads=okeys + [f"{hsk}_rinv"], writes=["attn_sb"]))
            g.append(lambda: m.op(S, lambda e: e.activation(out=an_bf, in_=attn_flat, func=AF.Square, accum_out=ss),
                                  reads=["attn_sb"], writes=["an_bf", skey]))
            g.append(lambda: m.op(V, lambda e: e.tensor_scalar(out=ss, in0=ss, scalar1=1.0 / 1024, scalar2=EPS, op0=ALU.mult, op1=ALU.add),
                                  reads=[skey], writes=[skey]))
            g.append(lambda: m.op(S, lambda e: e.activation(out=ss, in_=ss, func=AF.Sqrt), reads=[skey], writes=[skey]))
            g.append(lambda: m.op(V, lambda e: e.reciprocal(out=ss, in_=ss), reads=[skey], writes=[skey]))
            g.append(lambda: m.op(V, lambda e: e.scalar_tensor_tensor(out=an_bf, in0=attn_flat, scalar=ss, in1=ag_bc, op0=ALU.mult, op1=ALU.mult),
                                  reads=["attn_sb", skey, "ag_bc"], writes=["an_bf"]))

            def tr():
                b = 3
                pb = bank_bf(b).rearrange("p (a c) -> p a c", c=128)
                for j in range(8):
                    m.op("pe", lambda e, j=j: e.transpose(out=pb[:, j, :], in_=an_bf[:, j * 128:(j + 1) * 128], identity=ident),
                         reads=["an_bf", "ident"], writes=[f"pb{b}"])
                m.op(S, lambda e: e.copy(out=catT[:, 0:8, qt * 128:(qt + 1) * 128], in_=pb[:, 0:8, :]),
                     reads=[f"pb{b}"], writes=[f"catA{qt}"])
            g.append(tr)
            return g

        for step in range(NH + 3):
            if step < NH:
                st_S(step)
            if 0 <= step - 2 < NH:
                st_T(step - 2)
            if 0 <= step - 3 < NH:
                st_PV(step - 3)
            if step >= 4 and pending_tail:
                pending_tail.pop(0)()
        for f in pending_tail:
            f()
        pending_tail = tail()
    for f in pending_tail:
        f()
    m.barrier(lambda e: e.memset(scr[:, 1:2], 0.0))
    state["off"] = ac_mark
    smask = alloc([128, TOK], F32)
    tA = [alloc([128, TOK], F32) for _ in range(2)]
    tQ = [alloc([128, OWN], F32) for _ in range(2)]
    tB = alloc([128, TOK], F32)
    tC = alloc([128, TOK], F32)
    tE = alloc([128, TOK], F32)
    tF = alloc([128, TOK], F32)
    kd_bf = alloc([128, TOK], BF16)
    gate = [alloc([128, OWN], F32) for _ in range(3)]
    i_tm = [alloc([128, TT, 128], BF16) for _ in range(3)]
    qb = [alloc([128, OWN], BF16) for _ in range(2)]
    kb_ = [alloc([128, TOK], BF16) for _ in range(2)]
    kdtm = [alloc([128, TT, 128], BF16) for _ in range(2)]
    decs = [alloc([128, 18], F32) for _ in range(2)]
    S_f = [alloc([128, 128], F32) for _ in range(2)]
    S_bf = alloc([128, 16, 128], BF16)
    attT_bf = [alloc([128, 128], BF16) for _ in range(2)]
    osq = alloc([128, 512], F32)
    rst = alloc([128, 512], F32)
    t1 = alloc([128, 512], F32)

    m.op(P, lambda e: e.memset(smask, 1.0), writes=["smask"])
    m.op(P, lambda e: e.memset(smask.rearrange("p (c t) -> p c t", t=64)[:, :, 0:1], 0.0), reads=["smask"], writes=["smask"])
    pcnt = {"b": 0}

    headw = {}

    def load_head(h):
        headw[h] = (load_w(rows_view(w_in, C_FR + h * 128, 128), wview3(128), 1),
                    load_w(rows_view(w_in, C_QR + h * 128, 128), wview3(128), 1),
                    load_w(rows_view(w_in, C_GR + h * 128, 128), wview3(128), 1),
                    load_w(rows_view(w_in, C_IR + h * 128, 128), wview3(128), 1))

    def P_items(h):
        h2, h3 = h % 2, h % 3
        if h not in headw:
            load_head(h)
        (wfr, kf), (wqr, kq), (wgr, kg), (wir, ki) = headw.pop(h)
        if h + 1 < 8:
            load_head(h + 1)
        items = []

        def slab(w, wk, t0, t1_, func, dst, dkey):
            def f():
                b = pcnt["b"] % 3
                pcnt["b"] += 1
                n = t1_ - t0
                for dc in range(DC):
                    m.op("pe", lambda e, dc=dc: e.matmul(bank(b)[:, 0:n], lhsT=w[:, dc, :], rhs=hT[:, dc, t0:t1_],
                                                         start=(dc == 0), stop=(dc == DC - 1)),
                         reads=wk + hkeys(t0, t1_), writes=[f"pb{b}"])
                m.op(S, lambda e: e.activation(out=dst, in_=bank(b)[:, 0:n], func=func), reads=[f"pb{b}"], writes=[dkey])
            return f

        for (t0, t1_) in slabs:
            items.append(slab(wfr, kf, t0, t1_, AF.Sigmoid, tA[h2][:, t0:t1_], f"tA{h2}"))
        for sl in range(2):
            t0 = 128 + sl * 512
            items.append(slab(wqr, kq, t0, t0 + 512, AF.Silu, tQ[h2][:, sl * 512:(sl + 1) * 512], f"tQ{h2}"))
        for sl in range(2):
            t0 = 128 + sl * 512
            items.append(slab(wgr, kg, t0, t0 + 512, AF.Silu, gate[h3][:, sl * 512:(sl + 1) * 512], f"gate{h3}"))

        def igrp(grp):
            def f():
                b = pcnt["b"] % 3
                pcnt["b"] += 1
                tts = list(range(grp * 4, min(TT, grp * 4 + 4)))
                for j, tt in enumerate(tts):
                    for dc in range(DC):
                        m.op("pe", lambda e, j=j, tt=tt, dc=dc: e.matmul(bank(b)[:, j * 128:(j + 1) * 128],
                                                                         lhsT=hT[:, dc, tt * 128:(tt + 1) * 128], rhs=wir[:, dc, :],
                                                                         start=(dc == 0), stop=(dc == DC - 1)),
                             reads=ki + [f"hT{tt}"], writes=[f"pb{b}"])
                nt_ = len(tts)
                m.op(V, lambda e: e.tensor_copy(out=i_tm[h3][:, tts[0]:tts[0] + nt_, :],
                                                in_=bank(b)[:, 0:nt_ * 128].rearrange("p (a c) -> p a c", c=128)),
                     reads=[f"pb{b}"], writes=[f"i_tm{h3}"])
            return f

        for grp in range(3):
            items.append(igrp(grp))
        return items

    def E_chain(h):
        ops = []
        h2 = h % 2
        A_, Q_ = tA[h2], tQ[h2]
        ak, qk = f"tA{h2}", f"tQ{h2}"
        ops.append(lambda: m.op(V, lambda e: e.tensor_scalar(out=A_, in0=A_, scalar1=omlT[:, h:h + 1], scalar2=lbT[:, h:h + 1], op0=ALU.mult, op1=ALU.add),
             reads=[ak, "omlT", "lbT"], writes=[ak]))
        ops.append(lambda: m.op(S, lambda e: e.activation(out=tB, in_=A_, func=AF.Ln),
                                reads=[ak], writes=["tB"]))
        ops.append(lambda: m.op(V, lambda e: e.tensor_tensor_scan(out=tC, data0=smask, data1=tB, initial=0.0, op0=ALU.mult, op1=ALU.add),
             reads=["smask", "tB"], writes=["tC"]))
        ops.append(lambda: m.op(S, lambda e: e.activation(out=tE, in_=tC, func=AF.Exp), reads=["tC"], writes=["tE"]))
        ops.append(lambda: m.op(S, lambda e: e.activation(out=tF, in_=tC, func=AF.Exp, scale=-1.0), reads=["tC"], writes=["tF"]))
        ops.append(lambda: m.op(V, lambda e: e.tensor_scalar(out=A_, in0=A_, scalar1=-1.0, scalar2=1.0, op0=ALU.mult, op1=ALU.add),
             reads=[ak], writes=[ak]))
        ops.append(lambda: m.op(V, lambda e: e.tensor_tensor(out=qb[h2], in0=Q_, in1=tE[:, 128:TOK], op=ALU.mult), reads=[qk, "tE"], writes=[f"qb{h2}"]))
        ops.append(lambda: m.op(V, lambda e: e.tensor_tensor(out=kb_[h2], in0=A_, in1=tF, op=ALU.mult), reads=[ak, "tF"], writes=[f"kb{h2}"]))
        tE3 = tE.rearrange("p (c t) -> p c t", t=64)
        ops.append(lambda: m.op(V, lambda e: e.tensor_copy(out=decs[h2], in_=tE3[:, :, 63]), reads=["tE"], writes=[f"decs{h2}"]))
        ops.append(lambda: m.op(V, lambda e: e.tensor_tensor(out=tB.rearrange("p (c t) -> p c t", t=64), in0=tF.rearrange("p (c t) -> p c t", t=64),
                                         in1=tE3[:, :, 63:64].broadcast_to([128, 18, 64]), op=ALU.mult),
             reads=["tF", "tE"], writes=["tB"]))
        ops.append(lambda: m.op(V, lambda e: e.tensor_tensor(out=kd_bf, in0=A_, in1=tB, op=ALU.mult), reads=[ak, "tB"], writes=["kd_bf"]))
        return ops

    def T_items(h):
        h2 = h % 2
        items = []

        def tgrp(grp):
            def f():
                b = 7
                pb = bank_bf(b).rearrange("p (a c) -> p a c", c=128)
                tts = list(range(grp * 8, min(TT, grp * 8 + 8)))
                for j, tt in enumerate(tts):
                    m.op("pe", lambda e, j=j, tt=tt: e.transpose(out=pb[:, j, :], in_=kd_bf[:, tt * 128:(tt + 1) * 128], identity=ident),
                         reads=["kd_bf", "ident"], writes=[f"pb{b}"])
                nt_ = len(tts)
                m.op(S, lambda e: e.copy(out=kdtm[h2][:, tts[0]:tts[0] + nt_, :], in_=pb[:, 0:nt_, :]),
                     reads=[f"pb{b}"], writes=[f"kdtm{h2}"])
            return f

        return [tgrp(0), tgrp(1)]

    attT4 = alloc([128, 4, 128], BF16)

    def S2_items(h):
        h2, h3 = h % 2, h % 3

        def uslot(c):
            if c == 16:
                return 3, 0
            rnd = c // 8
            return (3 + 2 * rnd + (c % 2)), (c % 8) // 2

        def uburst(cs):
            for c in cs:
                pair, part = c // 2, c % 2
                ubk, usl = uslot(c)
                ub = bank(ubk)[:, usl * 128:(usl + 1) * 128]
                m.op("pe", lambda e, pair=pair, part=part, ub=ub: e.matmul(ub, lhsT=kdtm[h2][part * 64:(part + 1) * 64, pair, :],
                                                                          rhs=i_tm[h3][part * 64:(part + 1) * 64, pair, :], start=True, stop=True),
                     reads=[f"kdtm{h2}", f"i_tm{h3}"], writes=[f"pb{ubk}"])

        def chain(cs):
            for c in cs:
                cur, nxt = c % 2, (c + 1) % 2
                ubk, usl = uslot(c)
                ub = bank(ubk)[:, usl * 128:(usl + 1) * 128]
                if c >= 2:
                    m.op(S, lambda e, c=c, cur=cur: e.copy(out=S_bf[:, c - 2, :], in_=S_f[cur]), reads=[f"S_f{cur}"], writes=[f"S_bf{c - 2}"])
                m.op(V, lambda e, c=c, cur=cur, nxt=nxt, ub=ub: e.scalar_tensor_tensor(out=S_f[nxt], in0=S_f[cur], scalar=decs[h2][:, c:c + 1],
                                                                                  in1=ub, op0=ALU.mult, op1=ALU.add),
                     reads=[f"S_f{cur}", f"decs{h2}", f"pb{ubk}"], writes=[f"S_f{nxt}"])

        def att(bq):
            for pl in range(4):
                pair = 1 + bq * 4 + pl
                q0 = (pair - 1) * 128
                m.op("pe", lambda e, pair=pair, q0=q0, pl=pl: e.matmul(bank(7)[:, pl * 128:(pl + 1) * 128], lhsT=kb_[h2][:, pair * 128:(pair + 1) * 128],
                                                                      rhs=qb[h2][:, q0:q0 + 128], start=True, stop=True),
                     reads=[f"kb{h2}", f"qb{h2}"], writes=["pb7"])
            m.op(V, lambda e: e.tensor_tensor(out=attT4, in0=bank(7).rearrange("p (a c) -> p a c", c=128),
                                             in1=mask2.unsqueeze(1).broadcast_to([128, 4, 128]), op=ALU.mult),
                 reads=["pb7", "mask2"], writes=["attT4"])

        def omm(bq):
            ob = 5 + bq % 2
            for pl in range(4):
                pair = 1 + bq * 4 + pl
                q0 = (pair - 1) * 128
                for part in range(2):
                    c = 2 * pair + part
                    col = pl * 128 + part * 64
                    oc = bank(ob)[:, col:col + 64]
                    m.op("pe", lambda e, c=c, oc=oc, part=part, q0=q0: e.matmul(oc, lhsT=S_bf[:, c - 2, :],
                                                                                rhs=qb[h2][:, q0 + part * 64:q0 + part * 64 + 64],
                                                                                start=True, stop=False),
                         reads=[f"S_bf{c - 2}", f"qb{h2}"], writes=[f"pb{ob}"])
                    m.op("pe", lambda e, oc=oc, part=part, pair=pair, pl=pl: e.matmul(
                        oc, lhsT=i_tm[h3][part * 64:(part + 1) * 64, pair, :],
                        rhs=attT4[part * 64:(part + 1) * 64, pl, part * 64:(part + 1) * 64], start=False, stop=True),
                         reads=[f"i_tm{h3}", "attT4"], writes=[f"pb{ob}"])

        def norm(bq):
            ob = 5 + bq % 2
            m.op(S, lambda e: e.activation(out=osq, in_=bank(ob), func=AF.Square), reads=[f"pb{ob}"], writes=["osq"])
            m.op("pe", lambda e: e.matmul(bank(7), lhsT=ones_f, rhs=osq, start=True, stop=True), reads=["ones_f", "osq"], writes=["pb7"])
            m.op(S, lambda e: e.activation(out=rst, in_=bank(7), func=AF.Ln, scale=1.0 / 128, bias=epsT[:, 0:1]),
                 reads=["pb7", "epsT"], writes=["rst"])
            m.op(S, lambda e: e.activation(out=rst, in_=rst, func=AF.Exp, scale=-0.5), reads=["rst"], writes=["rst"])
            m.op(V, lambda e: e.tensor_tensor(out=t1, in0=bank(ob), in1=rst, op=ALU.mult), reads=[f"pb{ob}", "rst"], writes=["t1"])
            m.op(V, lambda e: e.scalar_tensor_tensor(out=catT[:, 8 + h, bq * 512:(bq + 1) * 512], in0=t1, scalar=rg[:, 0:1],
                                                    in1=gate[h3][:, bq * 512:(bq + 1) * 512], op0=ALU.mult, op1=ALU.mult),
                 reads=["t1", "rg", f"gate{h3}"], writes=[f"catR{h}_{bq}"])

        def g0():
            m.op(V, lambda e: e.memset(S_f[0], 0.0), writes=["S_f0"])
            uburst(range(0, 8))

        def g1():
            chain(range(0, 8))
            uburst(range(8, 16))

        def g2():
            chain(range(8, 16))
            uburst([16])

        def g3():
            chain([16])
            m.op(S, lambda e: e.copy(out=S_bf[:, 15, :], in_=S_f[1]), reads=["S_f1"], writes=["S_bf15"])
            att(0)

        def g4():
            omm(0)

        def g5():
            norm(0)
            att(1)

        def g6():
            omm(1)

        def g7():
            norm(1)

        return [g0, g1, g2, g3, g4, g5, g6, g7, (lambda: None), (lambda: None)]

    if stop == "rnn0":
        for f in P_items(0):
            f()
        for f in E_chain(0):
            f()
        for f in T_items(0):
            f()
        for f in S2_items(0):
            f()
        stats = m.emit(final_wait_keys=[])
        return nc, stats
    for it in range(8 + 2):
        A = P_items(it) if it < 8 else [(lambda: None)] * 10
        B = S2_items(it - 2) if 0 <= it - 2 < 8 else [(lambda: None)] * 10
        C = E_chain(it - 1) if 0 <= it - 1 < 8 else []
        na, ncn = len(A), len(C)
        ci = 0
        for ai in range(na):
            A[ai]()
            tc = min(ncn, (ncn * (ai + 1) + 5) // 6)
            while ci < tc:
                C[ci]()
                ci += 1
            B[ai]()
            if ai == 7 and 0 <= it - 1 < 8:
                for f in T_items(it - 1):
                    f()
    cat_keys = [f"catA{qt}" for qt in range(NT)] + [f"catR{h}_{bq}" for h in range(8) for bq in range(2)]
    if debug:
        m.dma(lambda e: e.dma_start(out=dbg_cat, in_=catT.rearrange("p a b -> p (a b)")), reads=cat_keys, writes=["dbg_cat"], sem="dbg_cat")

    m.barrier(lambda e: e.memset(scr[:, 2:3], 0.0))
    state["off"] = setup_mark
    _catT_again = alloc([128, 16, OWN], BF16)
    h2T = alloc([128, DC, OWN], BF16)
    x1 = alloc([128, NT, D], F32)
    gP = alloc([128, D], F32)
    gQ = alloc([128, D], F32)
    xr = [alloc([128, D], F32) for _ in range(2)]
    hn2 = [alloc([128, D], BF16) for _ in range(2)]
    m.dma(lambda e: e.dma_start(out=gP, in_=g_mix_post.partition_broadcast(128)), writes=["gP"], sem="gP")
    m.dma(lambda e: e.dma_start(out=gQ, in_=g_mlp_pre.partition_broadcast(128)), writes=["gQ"], sem="gQ")
    out_keys = []

    def stream(pieces, load, compute, la):
        loaded = {}
        n = len(pieces)
        for k in range(min(la, n)):
            loaded[k] = load(pieces[k])
        for k in range(n):
            if k + la < n:
                loaded[k + la] = load(pieces[k + la])
            compute(pieces[k], *loaded.pop(k))

    def cat_keys_for(tok_tile):
        return [f"catA{tok_tile}"] + [f"catR{h}_{tok_tile // 4}" for h in range(8)]

    def d_load(pc):
        hf, cg = pc
        return load_w(rows_view(w_out, cg * 512, 512), wview3(512), 4)

    def d_comp(pc, wo, wk):
        hf, cg = pc
        for tt in range(4 * hf, 4 * hf + 4):
            b = (cg * 4 + tt) % 8
            for ec in range(DC):
                m.op("pe", lambda e, ec=ec, tt=tt, b=b: e.matmul(bank(b), lhsT=catT[:, ec, tt * 128:(tt + 1) * 128],
                                                                 rhs=wo[:, ec, :], start=(ec == 0), stop=(ec == DC - 1)),
                     reads=wk + cat_keys_for(tt), writes=[f"pb{b}"])
            if tt % 2 == 0:
                m.op(S, lambda e, tt=tt, b=b: e.copy(out=x1[:, tt, cg * 512:(cg + 1) * 512], in_=bank(b)),
                     reads=[f"pb{b}"], writes=[f"x1_{tt}"])
            else:
                m.op(V, lambda e, tt=tt, b=b: e.tensor_copy(out=x1[:, tt, cg * 512:(cg + 1) * 512], in_=bank(b)),
                     reads=[f"pb{b}"], writes=[f"x1_{tt}"])

    all_cat = [f"catA{qt}" for qt in range(NT)] + [f"catR{h}_{bq}" for h in range(8) for bq in range(2)]
    junkD = [catT[:, 4 * i:4 * i + 4, 0:512] for i in range(2)]
    junk_keys = [f"catA{t}" for t in range(4)]

    dstat = {}

    def d_s1a(tt):
        row0 = 128 + tt * 128
        m.dma(lambda e: e.dma_start(out=xr[tt % 2], in_=xin[row0:row0 + 128, :]), writes=[f"xr{tt % 2}"], sem=f"xr{tt % 2}")
        stc = scol()
        ss = stat[:, stc:stc + 1]
        skey = f"stat{stc}"
        dstat[(1, tt)] = (ss, skey)
        m.op(S, lambda e: e.activation(out=junkD[tt % 2], in_=x1[:, tt, :].rearrange("p (a b) -> p a b", b=512),
                                             func=AF.Square, accum_out=ss),
             reads=[f"x1_{tt}"], writes=junk_keys + [skey])

    def d_s1c(tt):
        ss, skey = dstat[(1, tt)]
        rstd_chain(ss, skey, 1.0 / D)

    def d_s1b(tt):
        ss, skey = dstat[(1, tt)]
        m.op(V, lambda e: e.scalar_tensor_tensor(out=x1[:, tt, :], in0=x1[:, tt, :], scalar=ss, in1=gP,
                                                 op0=ALU.mult, op1=ALU.mult),
             reads=[f"x1_{tt}", skey, "gP"], writes=[f"x1_{tt}"])
        m.op(V, lambda e: e.tensor_tensor(out=x1[:, tt, :], in0=x1[:, tt, :], in1=xr[tt % 2], op=ALU.add),
             reads=[f"x1_{tt}", f"xr{tt % 2}"], writes=[f"x1_{tt}"])
        r0 = tt * 128
        m.dma(lambda e: e.dma_start(out=out[r0:r0 + 128, :], in_=x1[:, tt, :]), reads=[f"x1_{tt}"],
              writes=[f"out{tt}"], sem=f"outw{tt % 2}")
        if debug:
            m.dma(lambda e: e.dma_start(out=dbg_x1[r0:r0 + 128, :], in_=x1[:, tt, :]), reads=[f"x1_{tt}"],
                  writes=[f"dbgx{r0}"], sem=f"dbgx{tt % 2}")

    def d_s2a(tt):
        stc = scol()
        ss = stat[:, stc:stc + 1]
        skey = f"stat{stc}"
        dstat[(2, tt)] = (ss, skey)
        m.op(S, lambda e: e.activation(out=junkD[tt % 2], in_=x1[:, tt, :].rearrange("p (a b) -> p a b", b=512),
                                             func=AF.Square, accum_out=ss),
             reads=[f"x1_{tt}"], writes=junk_keys + [skey])

    def d_s2c(tt):
        ss, skey = dstat[(2, tt)]
        rstd_chain(ss, skey, 1.0 / D)

    def d_s2b(tt):
        ss, skey = dstat[(2, tt)]
        hnb, hnkey = hn2[tt % 2], f"hn2{tt % 2}"
        m.op(V, lambda e: e.scalar_tensor_tensor(out=hnb, in0=x1[:, tt, :], scalar=ss, in1=gQ, op0=ALU.mult, op1=ALU.mult),
             reads=[f"x1_{tt}", skey, "gQ"], writes=[hnkey])

    def d_s3(tt):
        hnb, hnkey = hn2[tt % 2], f"hn2{tt % 2}"
        for half in range(2):
            b = 2 * (tt % 4) + half
            pb = bank_bf(b).rearrange("p (a c) -> p a c", c=128)
            for j in range(8):
                dc = half * 8 + j
                m.op("pe", lambda e, o=pb[:, j, :], i=hnb[:, dc * 128:(dc + 1) * 128]: e.transpose(out=o, in_=i, identity=ident),
                     reads=[hnkey, "ident"], writes=[f"pb{b}"])
            dst = h2T[:, half * 8:(half + 1) * 8, tt * 128:(tt + 1) * 128]
            if half == 0:
                m.op(S, lambda e, o=dst, i=pb[:, 0:8, :]: e.copy(out=o, in_=i), reads=[f"pb{b}"], writes=[f"h2T{tt}"])
            else:
                m.op(V, lambda e, o=dst, i=pb[:, 0:8, :]: e.tensor_copy(out=o, in_=i), reads=[f"pb{b}"], writes=[f"h2T{tt}"])

    def post_step(step, tiles):
        n = len(tiles)
        for lag, fn in ((0, d_s1a), (2, d_s2a), (1, d_s1b), (3, d_s2b), (4, d_s3), (0, d_s1c), (2, d_s2c)):
            if 0 <= step - lag < n:
                fn(tiles[step - lag])

    dpieces = [(hf, cg) for hf in range(2) for cg in range(4)]
    dloaded = {0: d_load(dpieces[0])}
    first_half_steps = {4: [0, 1], 5: [2, 3], 6: [4, 5], 7: [6, 7]}
    for k in range(8):
        if k + 1 < 8:
            dloaded[k + 1] = d_load(dpieces[k + 1])
        d_comp(dpieces[k], *dloaded.pop(k))
        for st_ in first_half_steps.get(k, []):
            post_step(st_, [0, 1, 2, 3])
    for st_ in range(8):
        post_step(st_, [4, 5, 6, 7])
    h2keys = [f"h2T{t}" for t in range(NT)]

    m.barrier(lambda e: e.memset(scr[:, 3:4], 0.0))
    state["off"] = setup_mark
    yA = alloc([128, 4, D], F32)
    _h2T_again = alloc([128, DC, OWN], BF16)
    yB = alloc([128, 4, D], F32)
    u2T = alloc([128, 32, OWN], BF16)
    rr = [alloc([128, 512], F32) for _ in range(2)]
    gP2 = alloc([128, D], F32)
    m.dma(lambda e: e.dma_start(out=gP2, in_=g_mlp_post.partition_broadcast(128)), writes=["gP2"], sem="gP2")

    def yv(tt):
        return (yA if tt < 4 else yB)[:, tt % 4, :]

    lastw = {}

    def final_tile(tt):
        stc = scol()
        ss = stat[:, stc:stc + 1]
        skey = f"stat{stc}"
        ri = tt % 2
        junk = h2T[:, ri * 2:ri * 2 + 2, :].rearrange("p a b -> p (a b)")
        jkeys = list(h2keys)
        m.op(S, lambda e, tt=tt, ss=ss, junk=junk: e.activation(out=junk, in_=yv(tt), func=AF.Square, accum_out=ss),
             reads=[f"y{tt}"], writes=jkeys + [skey])
        rstd_chain(ss, skey, 1.0 / D)
        m.op(V, lambda e, tt=tt, ss=ss: e.scalar_tensor_tensor(out=yv(tt), in0=yv(tt), scalar=ss, in1=gP2, op0=ALU.mult, op1=ALU.mult),
             reads=[f"y{tt}", skey, "gP2"], writes=[f"y{tt}"])
        r0 = tt * 128
        m.dma(lambda e, tt=tt, r0=r0: e.dma_start(out=out[r0:r0 + 128, :], in_=yv(tt), accum_op=ALU.add),
              reads=[f"y{tt}", f"out{tt}"], writes=[f"out{tt}"], sem=f"outa{tt}", queue="pool")
        out_keys.append(f"out{tt}")

    upc = {"n": 0}
    e_pieces = []
    for half in range(2):
        def u_load(fcg, half=half):
            return load_w(rows_view(w_up, (half * 16 + fcg) * 256, 256), wview3(256), 2)

        def u_comp(fcg, wu, wk, half=half):
            for fci in range(2):
                fcl = 2 * fcg + fci
                for sl in range(2):
                    b = upc["n"] % 8
                    ri = upc["n"] % 2
                    upc["n"] += 1
                    for dc in range(DC):
                        m.op("pe", lambda e, dc=dc, fci=fci, b=b, sl=sl: e.matmul(bank(b), lhsT=wu[:, dc, fci * 128:(fci + 1) * 128],
                                                                                  rhs=h2T[:, dc, sl * 512:(sl + 1) * 512],
                                                                                  start=(dc == 0), stop=(dc == DC - 1)),
                             reads=wk + h2keys[sl * 4:(sl + 1) * 4], writes=[f"pb{b}"])
                    m.op(S, lambda e, b=b, ri=ri: e.activation(out=rr[ri], in_=bank(b), func=AF.Relu), reads=[f"pb{b}"], writes=[f"rr{ri}"])
                    m.op(P, lambda e, fcl=fcl, ri=ri, sl=sl: e.tensor_tensor(out=u2T[:, fcl, sl * 512:(sl + 1) * 512], in0=rr[ri], in1=rr[ri], op=ALU.mult),
                         reads=[f"rr{ri}"], writes=[f"u2T{fcl}_{sl}"])

        e_pieces += [(u_load, u_comp, fcg) for fcg in range(16)]

        pieces = [(cg, r) for cg in range(4) for r in range(4)]

        def dn_load(pc, half=half):
            cg, r = pc
            f0 = (half * 32 + r * 8) * 128
            src = w_down[f0:f0 + 1024, cg * 512:(cg + 1) * 512].rearrange("(fc p) c -> p fc c", p=128)
            return load_w(src, wview3(512), 2)

        def dn_comp(pc, wd, wk, half=half):
            cg, r = pc
            if half == 1 and cg == 3:
                lastw[r] = (wd, wk)
                if r < 3:
                    return
                for tt in range(NT):
                    b = tt
                    for r2 in range(4):
                        wd2, wk2 = lastw[r2]
                        for j in range(8):
                            fcl = r2 * 8 + j
                            m.op("pe", lambda e, tt=tt, j=j, fcl=fcl, b=b, wd2=wd2, r2=r2: e.matmul(
                                bank(b), lhsT=u2T[:, fcl, tt * 128:(tt + 1) * 128], rhs=wd2[:, j, :],
                                start=(r2 == 0 and j == 0), stop=(r2 == 3 and j == 7)),
                                 reads=wk2 + [f"u2T{fcl}_{tt // 4}"], writes=[f"pb{b}"])
                    dst = yv(tt)[:, cg * 512:(cg + 1) * 512]
                    m.op(V, lambda e, dst=dst, b=b: e.tensor_tensor(out=dst, in0=dst, in1=bank(b), op=ALU.add),
                         reads=[f"pb{b}", f"y{tt}"], writes=[f"y{tt}"])
                    final_tile(tt)
                return
            for tt in range(NT):
                b = tt
                for j in range(8):
                    fcl = r * 8 + j
                    m.op("pe", lambda e, tt=tt, j=j, fcl=fcl, b=b: e.matmul(bank(b), lhsT=u2T[:, fcl, tt * 128:(tt + 1) * 128], rhs=wd[:, j, :],
                                                                            start=(r == 0 and j == 0), stop=(r == 3 and j == 7)),
                         reads=wk + [f"u2T{fcl}_{tt // 4}"], writes=[f"pb{b}"])
                if r == 3:
                    dst = yv(tt)[:, cg * 512:(cg + 1) * 512]
                    if half == 0:
                        if tt % 2 == 0:
                            m.op(S, lambda e, dst=dst, b=b: e.copy(out=dst, in_=bank(b)), reads=[f"pb{b}"], writes=[f"y{tt}"])
                        else:
                            m.op(V, lambda e, dst=dst, b=b: e.tensor_copy(out=dst, in_=bank(b)), reads=[f"pb{b}"], writes=[f"y{tt}"])
                    else:
                        m.op(V, lambda e, dst=dst, b=b: e.tensor_tensor(out=dst, in0=dst, in1=bank(b), op=ALU.add),
                             reads=[f"pb{b}", f"y{tt}"], writes=[f"y{tt}"])

        e_pieces += [(dn_load, dn_comp, pc) for pc in pieces]

    stream(e_pieces, lambda p: p[0](p[2]), lambda p, w, k: p[1](p[2], w, k), 3)

    fin = list(out_keys)
    if debug:
        fin += ["dbg_cat"] + [k for k in m.last_w if k.startswith("dbgx")]
    stats = m.emit(final_wait_keys=fin)
    return nc, stats


_CACHE = {}


def _prep_inputs(x, w_in, attn_sinks, attn_out_gain, rnn_lb_logits, rnn_norm_gain, w_out,
                 mix_pre_gain, mix_post_gain, mlp_pre_gain, mlp_post_gain, w_up, w_down, cores=range(8)):
    f = lambda a: np.ascontiguousarray(np.asarray(a, dtype=np.float32))
    x = f(x)
    shared = {
        "w_in": f(w_in)[0], "w_out": f(w_out)[0], "w_up": f(w_up)[0], "w_down": f(w_down)[0],
        "sinks": f(attn_sinks).reshape(1, NH), "attn_gain": f(attn_out_gain).reshape(1, 1024),
        "lb_logits": f(rnn_lb_logits).reshape(2, 1024), "rnn_gain": f(rnn_norm_gain).reshape(1, 128),
        "g_mix_pre": f(mix_pre_gain).reshape(1, D), "g_mix_post": f(mix_post_gain).reshape(1, D),
        "g_mlp_pre": f(mlp_pre_gain).reshape(1, D), "g_mlp_post": f(mlp_post_gain).reshape(1, D),
    }
    in_maps = []
    for c in cores:
        b, j = c // 4, c % 4
        xin = np.zeros((TOK, D), np.float32)
        xin[128:] = x[b, j * OWN:(j + 1) * OWN]
        if j > 0:
            xin[:128] = x[b, j * OWN - 128:j * OWN]
        hmv = np.full((128, 1), NEG8 if j == 0 else 0.0, np.float32)
        d = dict(shared)
        d["xin"] = xin
        d["hm"] = hmv
        in_maps.append(d)
    return in_maps


def kernel(**inputs):
    if "nc" not in _CACHE:
        _CACHE["nc"] = build_program(debug=False)[0]
    nc = _CACHE["nc"]
    in_maps = _prep_inputs(**inputs)
    res = run_bass_kernel_spmd(nc, in_maps, core_ids=list(range(8)))
    outp = np.zeros((2, 4096, D), np.float32)
    for c in range(8):
        b, j = c // 4, c % 4
        outp[b, j * OWN:(j + 1) * OWN] = res.results[c]["out"]
    return outp
```

```python
import numpy as np
import concourse.bass as bass
import concourse.mybir as mybir
from concourse.bass_utils import run_bass_kernel_spmd

F32 = mybir.dt.float32
BF16 = mybir.dt.bfloat16
I32 = mybir.dt.int32
U8 = mybir.dt.uint8
AF = mybir.ActivationFunctionType
ALU = mybir.AluOpType

ENGS = ("pe", "dve", "act", "pool", "sp")
HANDLES = {"pe": "tensor", "dve": "vector", "act": "scalar", "pool": "gpsimd", "sp": "sync"}


class _Op:
    __slots__ = ("eng", "fn", "deps", "needed", "dom", "val", "waits", "is_dma", "dsem", "idx")


class MK:
    def __init__(self, nc, same_engine_sync=True):
        self.nc = nc
        self.ops = []
        self.last_w = {}
        self.readers = {}
        self.same_engine_sync = same_engine_sync
        self.token = None
        self.last_eng = {}
        self.last_sem = {}

    def _record(self, eng, fn, reads, writes, is_dma=False, dsem=None, free=False, extra=()):
        op = _Op()
        op.eng, op.fn, op.is_dma, op.dsem = eng, fn, is_dma, dsem
        op.needed = False
        op.idx = len(self.ops)
        deps = set(extra)
        if self.token is not None and not free:
            deps.add(self.token)
        for k in reads:
            w = self.last_w.get(k)
            if w is not None:
                deps.add(w)
        for k in writes:
            w = self.last_w.get(k)
            if w is not None:
                deps.add(w)
            for r in self.readers.get(k, ()):
                deps.add(r)
        deps.discard(op.idx)
        op.deps = deps
        self.ops.append(op)
        for k in reads:
            self.readers.setdefault(k, []).append(op.idx)
        for k in writes:
            self.last_w[k] = op.idx
            self.readers[k] = []
        if is_dma:
            self.last_sem[dsem] = op.idx
        else:
            self.last_eng[eng] = op.idx
        return op

    def op(self, eng, fn, reads=(), writes=(), free=False):
        return self._record(eng, fn, tuple(reads), tuple(writes), free=free)

    def dma(self, fn, reads=(), writes=(), sem="dma0", queue="sp", free=False):
        return self._record(queue, fn, tuple(reads), tuple(writes), is_dma=True, dsem=sem, free=free)

    def barrier(self, fn, exclude_sem_prefix="wb"):
        extra = set(self.last_eng.values())
        for s, i in self.last_sem.items():
            if not s.startswith(exclude_sem_prefix):
                extra.add(i)
        op = self._record("dve", fn, (), (), extra=extra)
        self.token = op.idx
        return op

    def emit(self, final_wait_keys=()):
        nc = self.nc
        ops = self.ops
        self._record("sp", None, tuple(final_wait_keys), ())
        for op in ops:
            for d in op.deps:
                ops[d].needed = True
        cnt = {}
        clock_of = [None] * len(ops)
        eng_clock = {e: {} for e in ENGS}
        per_eng = {e: [] for e in ENGS}
        for op in ops:
            e = op.eng
            ec = eng_clock[e]
            waits = {}
            for d in op.deps:
                dop = ops[d]
                if (not dop.is_dma) and dop.eng == e and (e == "pe" or not self.same_engine_sync):
                    continue
                dom, val = dop.dom, dop.val
                if ec.get(dom, 0) >= val:
                    continue
                if waits.get(dom, 0) < val:
                    waits[dom] = val
            if waits:
                for d in op.deps:
                    dop = ops[d]
                    if dop.dom in waits and waits[dop.dom] >= dop.val:
                        for k2, v2 in clock_of[d].items():
                            if ec.get(k2, 0) < v2:
                                ec[k2] = v2
                for dom, val in waits.items():
                    if ec.get(dom, 0) < val:
                        ec[dom] = val
            op.waits = list(waits.items())
            if op.is_dma:
                dom = ("dma", op.dsem)
                cnt[dom] = cnt.get(dom, 0) + 16
                op.dom, op.val = dom, cnt[dom]
                ck = dict(ec)
                ck[dom] = op.val
                clock_of[op.idx] = ck
            else:
                dom = ("eng", e)
                if op.needed:
                    cnt[dom] = cnt.get(dom, 0) + 1
                    op.dom, op.val = dom, cnt[dom]
                    ck = dict(ec)
                    ck[dom] = op.val
                    clock_of[op.idx] = ck
                else:
                    op.dom, op.val = dom, None
            per_eng[e].append(op)
        self.stats = {e: len(v) for e, v in per_eng.items()}
        self.stats["waits"] = sum(len(o.waits) for o in ops)
        doms = set()
        for op in ops:
            if op.is_dma or op.needed:
                doms.add(op.dom)
        self.stats["sems"] = len(doms)
        from contextlib import ExitStack
        with ExitStack() as st:
            sem = {}
            for i, dom in enumerate(sorted(doms, key=str)):
                sem[dom] = st.enter_context(nc.semaphore(f"s{i}"))
            block = st.enter_context(nc.Block())

            def make(ename):
                lst = per_eng[ename]

                def body(eng):
                    for op in lst:
                        for dom, val in op.waits:
                            eng.wait_ge(sem[dom], val)
                        if op.fn is None:
                            continue
                        ins = op.fn(eng)
                        if op.is_dma:
                            ins.then_inc(sem[op.dom], 16)
                        elif op.needed:
                            ins.then_inc(sem[op.dom], 1)
                return body

            for ename in ENGS:
                if per_eng[ename]:
                    getattr(block, HANDLES[ename])(make(ename))
        return self.stats


D = 2048
DC = 16
NT = 8
TT = 9
TOK = 1152
OWN = 1024
NH = 16
DFF = 8192
EPS = 1e-6
NEG8 = -30000.0
ARENA = 206 * 1024
WUNIT = 4096
NWU = 8

C_Q, C_K, C_V, C_QR, C_FR, C_IR, C_GR = 0, 1024, 1152, 1280, 2304, 3328, 4352


def build_program(debug=False, stop=None, reg=None):
    nc = bass.Bass("TRN2", target_bir_lowering=False)
    reg = {} if reg is None else reg

    def dram(name, shape, dt=F32, kind="ExternalInput"):
        return nc.dram_tensor(name, list(shape), dt, kind=kind).ap()

    xin = dram("xin", [TOK, D])
    hm = dram("hm", [128, 1])
    w_in = dram("w_in", [D, 5376])
    w_out = dram("w_out", [D, D])
    w_up = dram("w_up", [D, DFF])
    w_down = dram("w_down", [DFF, D])
    sinks = dram("sinks", [1, NH])
    attn_gain = dram("attn_gain", [1, 1024])
    lb_logits = dram("lb_logits", [2, 1024])
    rnn_gain = dram("rnn_gain", [1, 128])
    g_mix_pre = dram("g_mix_pre", [1, D])
    g_mix_post = dram("g_mix_post", [1, D])
    g_mlp_pre = dram("g_mlp_pre", [1, D])
    g_mlp_post = dram("g_mlp_post", [1, D])
    out = dram("out", [OWN, D], kind="ExternalOutput")
    if debug:
        dbg_cat = dram("dbg_cat", [128, 16 * OWN], BF16, kind="ExternalOutput")
        dbg_x1 = dram("dbg_x1", [OWN, D], kind="ExternalOutput")

    m = MK(nc)
    arena = nc.alloc_sbuf_tensor("arena", [128, ARENA], U8)
    state = {"off": 0}

    def alloc(shape, dt=F32):
        esz = 2 if dt == BF16 else 4
        n = esz
        for s in shape[1:]:
            n *= s
        off = (state["off"] + 63) // 64 * 64
        reg[len(reg)] = (off, tuple(shape), "bf16" if dt == BF16 else ("i32" if dt == I32 else "f32"))
        assert off + n <= ARENA, f"SBUF arena overflow: {off + n} > {ARENA}"
        state["off"] = off + n
        v = arena[:, off:off + n].bitcast(dt)
        if len(shape) == 3:
            v = v.rearrange("p (a b) -> p a b", b=shape[2])
        elif len(shape) == 4:
            v = v.rearrange("p (a b c) -> p a b c", b=shape[2], c=shape[3])
        return v

    pd = [nc.alloc_psum_tensor(f"pd{i}", [128, 1024], F32) for i in range(4)]

    def bank(b):
        return pd[b // 2][:, (b % 2) * 512:(b % 2) * 512 + 512]

    def bank_bf(b):
        return bank(b).bitcast(BF16)

    V, S, A_, P = "dve", "act", "act", "pool"

    ident = alloc([128, 128], BF16)
    ones_f = alloc([128, 128], F32)
    mask2 = alloc([128, 128], F32)
    epsT = alloc([128, 1], F32)
    scr = alloc([128, 4], F32)
    sink_bc = alloc([128, NH], F32)
    rg = alloc([128, 1], F32)
    lbT = alloc([128, 8], F32)
    omlT = alloc([128, 8], F32)
    hm_sb = alloc([128, 1], F32)
    stat = alloc([128, 64], F32)
    wpool = alloc([128, NWU * WUNIT // 2], BF16)
    persist_mark = state["off"]
    stat_col = {"n": 0}

    def scol(n=1):
        c = stat_col["n"]
        stat_col["n"] += n
        assert stat_col["n"] <= 64
        return c

    wstate = {"next": 0}

    def walloc(units):
        u = wstate["next"]
        u = (u + units - 1) // units * units
        if u + units > NWU:
            u = 0
        wstate["next"] = u + units
        keys = [f"wb{u + i}" for i in range(units)]
        base = wpool[:, u * (WUNIT // 2):(u + units) * (WUNIT // 2)]
        return base, keys, f"wb{u}"

    def load_w(src_ap, view_fn, units):
        base, keys, sem = walloc(units)
        dst = view_fn(base)
        m.dma(lambda e, dst=dst, src=src_ap: e.dma_start(out=dst, in_=src), writes=keys, sem=sem,
              queue="pool", free=True)
        return dst, keys

    def wview3(n):
        return lambda base: base.rearrange("p (a b) -> p a b", b=n)

    def rows_view(w, c0, n):
        return w[:, c0:c0 + n].rearrange("(dc p) c -> p dc c", p=128)

    setup_mark = state["off"]
    catT = alloc([128, 16, OWN], BF16)
    identf = alloc([128, 128], F32)
    m.op(P, lambda e: e.memset(identf, 1.0), writes=["identf"])
    m.op(P, lambda e: e.affine_select(out=identf, in_=identf, pattern=[[-1, 128]], compare_op=ALU.is_equal,
                                      fill=0.0, base=0, channel_multiplier=1), reads=["identf"], writes=["identf"])
    m.op(V, lambda e: e.tensor_copy(out=ident, in_=identf), reads=["identf"], writes=["ident"])
    m.op(P, lambda e: e.memset(ones_f, 1.0), writes=["ones_f"])
    m.op(P, lambda e: e.memset(epsT, EPS), writes=["epsT"])
    m.op(V, lambda e: e.memset(scr, 0.0), writes=["scr"])
    m.op(P, lambda e: e.memset(mask2, 1.0), writes=["mask2"])
    m.op(P, lambda e: e.affine_select(out=mask2, in_=mask2, pattern=[[1, 128]], compare_op=ALU.is_ge,
                                      fill=0.0, base=0, channel_multiplier=-1), reads=["mask2"], writes=["mask2"])
    m.op(P, lambda e: e.memset(mask2[0:64, 64:128], 0.0), reads=["mask2"], writes=["mask2"])
    m.dma(lambda e: e.dma_start(out=sink_bc, in_=sinks.partition_broadcast(128)), writes=["sink_bc"], sem="c_sink")
    m.dma(lambda e: e.dma_start(out=rg, in_=rnn_gain.rearrange("o k -> k o")), writes=["rg"], sem="c_rg")
    m.dma(lambda e: e.dma_start(out=hm_sb, in_=hm), writes=["hm"], sem="c_hm")
    l0 = alloc([128, 8], F32)
    l1 = alloc([128, 8], F32)
    m.dma(lambda e: e.dma_start(out=l0, in_=lb_logits[0:1, :].rearrange("o (h k) -> k (o h)", k=128),
                                allow_slow_non_contiguous=True), writes=["l0"], sem="c_l0")
    m.dma(lambda e: e.dma_start(out=l1, in_=lb_logits[1:2, :].rearrange("o (h k) -> k (o h)", k=128),
                                allow_slow_non_contiguous=True), writes=["l1"], sem="c_l1")
    m.op(S, lambda e: e.activation(out=l0, in_=l0, func=AF.Exp), reads=["l0"], writes=["l0"])
    m.op(S, lambda e: e.activation(out=l1, in_=l1, func=AF.Exp), reads=["l1"], writes=["l1"])
    m.op(V, lambda e: e.tensor_tensor(out=omlT, in0=l0, in1=l1, op=ALU.add), reads=["l0", "l1"], writes=["omlT"])
    m.op(V, lambda e: e.reciprocal(out=omlT, in_=omlT), reads=["omlT"], writes=["omlT"])
    m.op(V, lambda e: e.tensor_tensor(out=lbT, in0=l0, in1=omlT, op=ALU.mult), reads=["l0", "omlT"], writes=["lbT"])
    m.op(V, lambda e: e.tensor_scalar(out=omlT, in0=lbT, scalar1=-1.0, scalar2=1.0, op0=ALU.mult, op1=ALU.add),
         reads=["lbT"], writes=["omlT"])

    hT = alloc([128, DC, TOK], BF16)
    ac_mark = state["off"]
    Mt = alloc([128, 256], F32)
    Mt8 = alloc([128, 256], BF16)
    Mt8_0 = alloc([128, 256], BF16)
    Kr_i = alloc([128, 256], I32)
    Kr = alloc([128, 256], BF16)
    Sl = alloc([128, NH, 128], BF16)
    qc_i = alloc([128, 1], I32)
    qcol = alloc([128, 1], F32)
    sc = alloc([128, NH], F32)
    nsc = alloc([128, NH], F32)
    ab_mark = state["off"]
    gA = alloc([128, D], F32)
    xb = [alloc([128, D], F32) for _ in range(3)]
    hn = [alloc([128, D], BF16) for _ in range(3)]
    m.dma(lambda e: e.dma_start(out=gA, in_=g_mix_pre.partition_broadcast(128)), writes=["gA"], sem="gA")

    def rstd_chain(ss_ap, key, n_inv):
        m.op(V, lambda e: e.tensor_scalar(out=ss_ap, in0=ss_ap, scalar1=n_inv, scalar2=EPS, op0=ALU.mult, op1=ALU.add),
             reads=[key], writes=[key])
        m.op(S, lambda e: e.activation(out=ss_ap, in_=ss_ap, func=AF.Sqrt), reads=[key], writes=[key])
        m.op(V, lambda e: e.reciprocal(out=ss_ap, in_=ss_ap), reads=[key], writes=[key])

    def norm_transpose(src, src_keys, gain, gain_key, hnb, hnkey, dstT, dst_key, col0, stc, bnk):
        ss = stat[:, stc:stc + 1]
        skey = f"stat{stc}"
        m.op(S, lambda e: e.activation(out=hnb, in_=src, func=AF.Square, accum_out=ss),
             reads=src_keys, writes=[hnkey, skey])
        rstd_chain(ss, skey, 1.0 / D)
        m.op(V, lambda e: e.scalar_tensor_tensor(out=hnb, in0=src, scalar=ss, in1=gain, op0=ALU.mult, op1=ALU.mult),
             reads=list(src_keys) + [skey, gain_key], writes=[hnkey])
        for half in range(2):
            b = bnk[half]
            pb = bank_bf(b).rearrange("p (a c) -> p a c", c=128)
            for j in range(8):
                dc = half * 8 + j
                m.op("pe", lambda e, o=pb[:, j, :], i=hnb[:, dc * 128:(dc + 1) * 128]: e.transpose(out=o, in_=i, identity=ident),
                     reads=[hnkey, "ident"], writes=[f"pb{b}"])
            eng = S if half == 0 else V
            dst = dstT[:, half * 8:(half + 1) * 8, col0:col0 + 128]
            if eng == S:
                m.op(S, lambda e, o=dst, i=pb[:, 0:8, :]: e.copy(out=o, in_=i), reads=[f"pb{b}"], writes=[dst_key])
            else:
                m.op(V, lambda e, o=dst, i=pb[:, 0:8, :]: e.tensor_copy(out=o, in_=i), reads=[f"pb{b}"], writes=[dst_key])

    m.op(P, lambda e: e.memset(Mt, 0.0), writes=["Mt"])
    m.op(P, lambda e: e.affine_select(out=Mt, in_=Mt, pattern=[[1, 256]], compare_op=ALU.is_ge, fill=8.0 * NEG8,
                                      base=-1, channel_multiplier=-1), reads=["Mt"], writes=["Mt"])
    m.op(P, lambda e: e.affine_select(out=Mt, in_=Mt, pattern=[[-1, 256]], compare_op=ALU.is_ge, fill=8.0 * NEG8,
                                      base=128, channel_multiplier=1), reads=["Mt"], writes=["Mt"])
    m.op(V, lambda e: e.tensor_copy(out=Mt8, in_=Mt), reads=["Mt"], writes=["Mt8"])
    m.op(V, lambda e: e.tensor_copy(out=Mt8_0[:, 128:256], in_=Mt[:, 128:256]), reads=["Mt"], writes=["Mt8_0"])
    m.op(V, lambda e: e.scalar_tensor_tensor(out=Mt8_0[:, 0:128], in0=hm_sb[:, 0:1].broadcast_to([128, 128]), scalar=8.0,
                                            in1=Mt[:, 0:128], op0=ALU.mult, op1=ALU.add),
         reads=["Mt", "hm"], writes=["Mt8_0"])
    m.op(P, lambda e: e.iota(Kr_i, pattern=[[1, 256]], base=0, channel_multiplier=0), writes=["Kr_i"])
    m.op(V, lambda e: e.tensor_copy(out=Kr, in_=Kr_i), reads=["Kr_i"], writes=["Kr"])
    m.op(P, lambda e: e.iota(qc_i, pattern=[[0, 1]], base=128, channel_multiplier=1), writes=["qc_i"])
    m.op(V, lambda e: e.tensor_copy(out=qcol, in_=qc_i), reads=["qc_i"], writes=["qcol"])
    m.op(P, lambda e: e.memset(Sl, 0.0), writes=["Sl"])
    import ml_dtypes as _mld
    for h in range(NH):
        slope = 2.0 ** (-8.0 * (h + 1) / NH)
        hi = float(np.float32(8.0 * slope).astype(_mld.bfloat16))
        mid = float(np.float32(8.0 * slope - hi).astype(_mld.bfloat16))
        lo = float(np.float32(8.0 * slope - hi - mid).astype(_mld.bfloat16))
        m.op(P, lambda e, h=h, hi=hi: e.memset(Sl[0:1, h, :], hi), reads=["Sl"], writes=["Sl"])
        m.op(P, lambda e, h=h, mid=mid: e.memset(Sl[32:33, h, :], mid), reads=["Sl"], writes=["Sl"])
        m.op(P, lambda e, h=h, lo=lo: e.memset(Sl[64:65, h, :], lo), reads=["Sl"], writes=["Sl"])
        m.op(V, lambda e, h=h, slope=slope: e.scalar_tensor_tensor(out=sc[:, h:h + 1], in0=qcol, scalar=slope, in1=sink_bc[:, h:h + 1],
                                                                  op0=ALU.mult, op1=ALU.add),
             reads=["qcol", "sink_bc"], writes=["sc"])
    m.op(V, lambda e: e.tensor_scalar(out=nsc, in0=sc, scalar1=-1.0, scalar2=None, op0=ALU.mult), reads=["sc"], writes=["nsc"])

    astat = {}

    def a_s1(tt):
        xt = xb[tt % 3]
        m.dma(lambda e: e.dma_start(out=xt, in_=xin[tt * 128:(tt + 1) * 128, :]), writes=[f"xb{tt % 3}"], sem=f"xb{tt % 3}")
        stc = scol()
        ss = stat[:, stc:stc + 1]
        skey = f"stat{stc}"
        astat[tt] = (ss, skey)
        m.op(S, lambda e: e.activation(out=hn[tt % 3], in_=xt, func=AF.Square, accum_out=ss),
             reads=[f"xb{tt % 3}"], writes=[f"hn{tt % 3}", skey])

    def a_s1c(tt):
        ss, skey = astat[tt]
        rstd_chain(ss, skey, 1.0 / D)

    def a_s2(tt):
        ss, skey = astat[tt]
        m.op(V, lambda e: e.scalar_tensor_tensor(out=hn[tt % 3], in0=xb[tt % 3], scalar=ss, in1=gA, op0=ALU.mult, op1=ALU.mult),
             reads=[f"xb{tt % 3}", skey, "gA"], writes=[f"hn{tt % 3}"])

    def a_s3(tt):
        hnb, hnkey = hn[tt % 3], f"hn{tt % 3}"
        for half in range(2):
            b = 2 * (tt % 4) + half
            pb = bank_bf(b).rearrange("p (a c) -> p a c", c=128)
            for j in range(8):
                dc = half * 8 + j
                m.op("pe", lambda e, o=pb[:, j, :], i=hnb[:, dc * 128:(dc + 1) * 128]: e.transpose(out=o, in_=i, identity=ident),
                     reads=[hnkey, "ident"], writes=[f"pb{b}"])
            dst = hT[:, half * 8:(half + 1) * 8, tt * 128:(tt + 1) * 128]
            if half == 0:
                m.op(S, lambda e, o=dst, i=pb[:, 0:8, :]: e.copy(out=o, in_=i), reads=[f"pb{b}"], writes=[f"hT{tt}"])
            else:
                m.op(V, lambda e, o=dst, i=pb[:, 0:8, :]: e.tensor_copy(out=o, in_=i), reads=[f"pb{b}"], writes=[f"hT{tt}"])

    for step in range(TT + 2):
        for lag, fn in ((0, a_s1), (1, a_s2), (2, a_s3), (0, a_s1c)):
            if 0 <= step - lag < TT:
                fn(step - lag)
    hT_keys = [f"hT{tt}" for tt in range(TT)]

    def hkeys(t0, t1):
        return [f"hT{t}" for t in range(t0 // 128, (t1 + 127) // 128)]

    m.barrier(lambda e: e.memset(scr[:, 0:1], 0.0))
    state["off"] = ab_mark
    ag_bc = alloc([128, 1024], F32)
    qT_all = alloc([128, 8, OWN], BF16)
    kT = alloc([128, TOK], BF16)
    v_sb = alloc([128, TT, 128], BF16)
    p_bf = [alloc([128, 256], BF16) for _ in range(3)]
    pT_sb = [alloc([128, 2, 128], BF16) for _ in range(2)]
    attn_sb = alloc([128, NH, 64], F32)
    an_bf = alloc([128, 1024], BF16)
    hstat = [alloc([128, 5, NH], F32) for _ in range(2)]

    m.dma(lambda e: e.dma_start(out=ag_bc, in_=attn_gain.partition_broadcast(128)), writes=["ag_bc"], sem="ag_bc")
    wkv, wkv_keys = load_w(rows_view(w_in, C_K, 256), wview3(256), 2)
    slabs = [(0, 512), (512, 1024), (1024, 1152)]
    for si, (t0, t1) in enumerate(slabs):
        b = si % 2
        n = t1 - t0
        for dc in range(DC):
            m.op("pe", lambda e, b=b, n=n, dc=dc, t0=t0, t1=t1: e.matmul(bank(b)[:, 0:n], lhsT=wkv[:, dc, 0:128],
                                                                        rhs=hT[:, dc, t0:t1], start=(dc == 0), stop=(dc == DC - 1)),
                 reads=wkv_keys + hkeys(t0, t1), writes=[f"pb{b}"])
        m.op(S, lambda e, b=b, n=n, t0=t0, t1=t1: e.copy(out=kT[:, t0:t1], in_=bank(b)[:, 0:n]),
             reads=[f"pb{b}"], writes=["kT"])
    for grp in range(3):
        b = 2 + grp % 2
        tts = list(range(grp * 4, min(TT, grp * 4 + 4)))
        for j, tt in enumerate(tts):
            for dc in range(DC):
                m.op("pe", lambda e, b=b, j=j, tt=tt, dc=dc: e.matmul(bank(b)[:, j * 128:(j + 1) * 128],
                                                                      lhsT=hT[:, dc, tt * 128:(tt + 1) * 128],
                                                                      rhs=wkv[:, dc, 128:256], start=(dc == 0), stop=(dc == DC - 1)),
                     reads=wkv_keys + [f"hT{tt}"], writes=[f"pb{b}"])
        nt_ = len(tts)
        m.op(V, lambda e, b=b, nt_=nt_, t0=tts[0]: e.tensor_copy(
            out=v_sb[:, t0:t0 + nt_, :], in_=bank(b)[:, 0:nt_ * 128].rearrange("p (a c) -> p a c", c=128)),
             reads=[f"pb{b}"], writes=["v_sb"])
    for pr in range(8):
        base, keys, sem = walloc(1)
        wq = base.rearrange("p (a b c) -> p a b c", b=2, c=64)
        for hh in range(2):
            c0 = C_Q + (pr + 8 * hh) * 64
            m.dma(lambda e, wq=wq, hh=hh, c0=c0: e.dma_start(out=wq[:, :, hh, :], in_=rows_view(w_in, c0, 64)),
                  writes=keys, sem=sem, queue="pool", free=True)
        for sl in range(2):
            b = 4 + (pr * 2 + sl) % 2
            t0 = 128 + sl * 512
            for dc in range(DC):
                m.op("pe", lambda e, b=b, wq=wq, dc=dc, t0=t0: e.matmul(bank(b)[:, 0:512], lhsT=wq[:, dc, :, :],
                                                                        rhs=hT[:, dc, t0:t0 + 512], start=(dc == 0), stop=(dc == DC - 1)),
                     reads=keys + hkeys(t0, t0 + 512), writes=[f"pb{b}"])
            eng = S if sl == 0 else V
            if eng == S:
                m.op(S, lambda e, b=b, pr=pr, sl=sl: e.copy(out=qT_all[:, pr, sl * 512:(sl + 1) * 512], in_=bank(b)[:, 0:512]),
                     reads=[f"pb{b}"], writes=[f"qT{pr}"])
            else:
                m.op(V, lambda e, b=b, pr=pr, sl=sl: e.tensor_copy(out=qT_all[:, pr, sl * 512:(sl + 1) * 512], in_=bank(b)[:, 0:512]),
                     reads=[f"pb{b}"], writes=[f"qT{pr}"])

    pending_tail = []
    for qt in range(NT):
        hs = hstat[qt % 2]
        hsk = f"hstat{qt % 2}"
        o_ps = pd[2 + qt % 2]
        okeys = [f"pb{4 + 2 * (qt % 2)}", f"pb{5 + 2 * (qt % 2)}"]

        def st_S(h, qt=qt, hs=hs, hsk=hsk):
            pr, hh = h % 8, h // 8
            i = h % 3
            mt, mk = (Mt8_0, "Mt8_0") if qt == 0 else (Mt8, "Mt8")
            m.op("pe", lambda e: e.matmul(bank(i)[:, 0:256], lhsT=qT_all[hh * 64:(hh + 1) * 64, pr, qt * 128:(qt + 1) * 128],
                                          rhs=kT[hh * 64:(hh + 1) * 64, qt * 128:qt * 128 + 256], start=True, stop=False),
                 reads=[f"qT{pr}", "kT"], writes=[f"pb{i}"])
            m.op("pe", lambda e: e.matmul(bank(i)[:, 0:256], lhsT=ident, rhs=mt, start=False, stop=False),
                 reads=["ident", mk], writes=[f"pb{i}"])
            m.op("pe", lambda e: e.matmul(bank(i)[:, 0:256], lhsT=Sl[:, h, :], rhs=Kr, start=False, stop=True),
                 reads=["Sl", "Kr"], writes=[f"pb{i}"])
            m.op(V, lambda e: e.tensor_reduce(out=hs[:, 0, h:h + 1], in_=bank(i)[:, 0:256], axis=mybir.AxisListType.X, op=ALU.max),
                 reads=[f"pb{i}"], writes=[f"{hsk}_rmax{h}"])
            m.op(V, lambda e: e.tensor_scalar(out=hs[:, 1, h:h + 1], in0=hs[:, 0, h:h + 1], scalar1=-0.125, scalar2=nsc[:, h:h + 1],
                                             op0=ALU.mult, op1=ALU.min),
                 reads=[f"{hsk}_rmax{h}", "nsc"], writes=[f"{hsk}_negm{h}"])
            m.op(S, lambda e: e.activation(out=p_bf[i], in_=bank(i)[:, 0:256], func=AF.Exp, bias=hs[:, 1, h:h + 1], scale=0.125,
                                          accum_out=hs[:, 2, h:h + 1]),
                 reads=[f"pb{i}", f"{hsk}_negm{h}"], writes=[f"p_bf{i}", f"{hsk}_rsum{h}"])

        def st_T(h, qt=qt):
            i = h % 3
            j = h % 2
            pb = bank_bf(3).rearrange("p (a c) -> p a c", c=128)
            for kb in range(2):
                m.op("pe", lambda e, kb=kb: e.transpose(out=pb[:, kb, :], in_=p_bf[i][:, kb * 128:(kb + 1) * 128], identity=ident),
                     reads=[f"p_bf{i}", "ident"], writes=["pb3"])
            if j == 0:
                m.op(S, lambda e: e.copy(out=pT_sb[j], in_=pb[:, 0:2, :]), reads=["pb3"], writes=[f"pT_sb{j}"])
            else:
                m.op(V, lambda e: e.tensor_copy(out=pT_sb[j], in_=pb[:, 0:2, :]), reads=["pb3"], writes=[f"pT_sb{j}"])

        def st_PV(h, qt=qt, o_ps=o_ps, okeys=okeys):
            i = h % 2
            hh = h // 8
            for kb in range(2):
                m.op("pe", lambda e, kb=kb: e.matmul(o_ps[:, h * 64:(h + 1) * 64], lhsT=pT_sb[i][:, kb, :],
                                                     rhs=v_sb[:, qt + kb, hh * 64:(hh + 1) * 64], start=(kb == 0), stop=(kb == 1)),
                     reads=[f"pT_sb{i}", "v_sb"], writes=[okeys[h // 8]])

        def tail(qt=qt, hs=hs, hsk=hsk, o_ps=o_ps, okeys=okeys):
            allk = lambda nm: [f"{hsk}_{nm}{h}" for h in range(NH)]
            stc = scol()
            ss = stat[:, stc:stc + 1]
            skey = f"stat{stc}"
            attn_flat = attn_sb.rearrange("p h c -> p (h c)")
            g = []
            g.append(lambda: m.op(V, lambda e: e.tensor_tensor(out=hs[:, 3, :], in0=sc, in1=hs[:, 1, :], op=ALU.add),
                                  reads=["sc"] + allk("negm"), writes=[f"{hsk}_es"]))
            g.append(lambda: m.op(S, lambda e: e.activation(out=hs[:, 3, :], in_=hs[:, 3, :], func=AF.Exp),
                                  reads=[f"{hsk}_es"], writes=[f"{hsk}_es"]))
            g.append(lambda: m.op(V, lambda e: e.tensor_tensor(out=hs[:, 3, :], in0=hs[:, 3, :], in1=hs[:, 2, :], op=ALU.add),
                                  reads=[f"{hsk}_es"] + allk("rsum"), writes=[f"{hsk}_es"]))
            g.append(lambda: m.op(V, lambda e: e.reciprocal(out=hs[:, 4, :], in_=hs[:, 3, :]), reads=[f"{hsk}_es"], writes=[f"{hsk}_rinv"]))
            g.append(lambda: m.op(V, lambda e: e.tensor_tensor(out=attn_sb, in0=o_ps.rearrange("p (h c) -> p h c", c=64),
                                                              in1=hs[:, 4, :].unsqueeze(2).broadcast_to([128, NH, 64]), op=ALU.mult),
                                  reads=okeys + [f"{hsk}_rinv"], writes=["attn_sb"]))
            g.append(lambda: m.op(S, lambda e: e.activation(out=an_bf, in_=attn_flat, func=AF.Square, accum_out=ss),
                                  reads=["attn_sb"], writes=["an_bf", skey]))
            g.append(lambda: m.op(V, lambda e: e.tensor_scalar(out=ss, in0=ss, scalar1=1.0 / 1024, scalar2=EPS, op0=ALU.mult, op1=ALU.add),
                                  reads=[skey], writes=[skey]))
            g.append(lambda: m.op(S, lambda e: e.activation(out=ss, in_=ss, func=AF.Sqrt), reads=[skey], writes=[skey]))
            g.append(lambda: m.op(V, lambda e: e.reciprocal(out=ss, in_=ss), reads=[skey], writes=[skey]))
            g.append(lambda: m.op(V, lambda e: e.scalar_tensor_tensor(out=an_bf, in0=attn_flat, scalar=ss, in1=ag_bc, op0=ALU.mult, op1=ALU.mult),
                                  reads=["attn_sb", skey, "ag_bc"], writes=["an_bf"]))

            def tr():
                b = 3
                pb = bank_bf(b).rearrange("p (a c) -> p a c", c=128)
                for j in range(8):
                    m.op("pe", lambda e, j=j: e.transpose(out=pb[:, j, :], in_=an_bf[:, j * 128:(j + 1) * 128], identity=ident),
                         reads=["an_bf", "ident"], writes=[f"pb{b}"])
                m.op(S, lambda e: e.copy(out=catT[:, 0:8, qt * 128:(qt + 1) * 128], in_=pb[:, 0:8, :]),
                     reads=[f"pb{b}"], writes=[f"catA{qt}"])
            g.append(tr)
            return g

        for step in range(NH + 3):
            if step < NH:
                st_S(step)
            if 0 <= step - 2 < NH:
                st_T(step - 2)
            if 0 <= step - 3 < NH:
                st_PV(step - 3)
            if step >= 4 and pending_tail:
                pending_tail.pop(0)()
        for f in pending_tail:
            f()
        pending_tail = tail()
    for f in pending_tail:
        f()
    m.barrier(lambda e: e.memset(scr[:, 1:2], 0.0))
    state["off"] = ac_mark
    smask = alloc([128, TOK], F32)
    tA = [alloc([128, TOK], F32) for _ in range(2)]
    tQ = [alloc([128, OWN], F32) for _ in range(2)]
    tB = alloc([128, TOK], F32)
    tC = alloc([128, TOK], F32)
    tE = alloc([128, TOK], F32)
    tF = alloc([128, TOK], F32)
    kd_bf = alloc([128, TOK], BF16)
    gate = [alloc([128, OWN], F32) for _ in range(3)]
    i_tm = [alloc([128, TT, 128], BF16) for _ in range(3)]
    qb = [alloc([128, OWN], BF16) for _ in range(2)]
    kb_ = [alloc([128, TOK], BF16) for _ in range(2)]
    kdtm = [alloc([128, TT, 128], BF16) for _ in range(2)]
    decs = [alloc([128, 18], F32) for _ in range(2)]
    S_f = [alloc([128, 128], F32) for _ in range(2)]
    S_bf = alloc([128, 16, 128], BF16)
    attT_bf = [alloc([128, 128], BF16) for _ in range(2)]
    osq = alloc([128, 512], F32)
    rst = alloc([128, 512], F32)
    t1 = alloc([128, 512], F32)

    m.op(P, lambda e: e.memset(smask, 1.0), writes=["smask"])
    m.op(P, lambda e: e.memset(smask.rearrange("p (c t) -> p c t", t=64)[:, :, 0:1], 0.0), reads=["smask"], writes=["smask"])
    pcnt = {"b": 0}

    headw = {}

    def load_head(h):
        headw[h] = (load_w(rows_view(w_in, C_FR + h * 128, 128), wview3(128), 1),
                    load_w(rows_view(w_in, C_QR + h * 128, 128), wview3(128), 1),
                    load_w(rows_view(w_in, C_GR + h * 128, 128), wview3(128), 1),
                    load_w(rows_view(w_in, C_IR + h * 128, 128), wview3(128), 1))

    def P_items(h):
        h2, h3 = h % 2, h % 3
        if h not in headw:
            load_head(h)
        (wfr, kf), (wqr, kq), (wgr, kg), (wir, ki) = headw.pop(h)
        if h + 1 < 8:
            load_head(h + 1)
        items = []

        def slab(w, wk, t0, t1_, func, dst, dkey):
            def f():
                b = pcnt["b"] % 3
                pcnt["b"] += 1
                n = t1_ - t0
                for dc in range(DC):
                    m.op("pe", lambda e, dc=dc: e.matmul(bank(b)[:, 0:n], lhsT=w[:, dc, :], rhs=hT[:, dc, t0:t1_],
                                                         start=(dc == 0), stop=(dc == DC - 1)),
                         reads=wk + hkeys(t0, t1_), writes=[f"pb{b}"])
                m.op(S, lambda e: e.activation(out=dst, in_=bank(b)[:, 0:n], func=func), reads=[f"pb{b}"], writes=[dkey])
            return f

        for (t0, t1_) in slabs:
            items.append(slab(wfr, kf, t0, t1_, AF.Sigmoid, tA[h2][:, t0:t1_], f"tA{h2}"))
        for sl in range(2):
            t0 = 128 + sl * 512
            items.append(slab(wqr, kq, t0, t0 + 512, AF.Silu, tQ[h2][:, sl * 512:(sl + 1) * 512], f"tQ{h2}"))
        for sl in range(2):
            t0 = 128 + sl * 512
            items.append(slab(wgr, kg, t0, t0 + 512, AF.Silu, gate[h3][:, sl * 512:(sl + 1) * 512], f"gate{h3}"))

        def igrp(grp):
            def f():
                b = pcnt["b"] % 3
                pcnt["b"] += 1
                tts = list(range(grp * 4, min(TT, grp * 4 + 4)))
                for j, tt in enumerate(tts):
                    for dc in range(DC):
                        m.op("pe", lambda e, j=j, tt=tt, dc=dc: e.matmul(bank(b)[:, j * 128:(j + 1) * 128],
                                                                         lhsT=hT[:, dc, tt * 128:(tt + 1) * 128], rhs=wir[:, dc, :],
                                                                         start=(dc == 0), stop=(dc == DC - 1)),
                             reads=ki + [f"hT{tt}"], writes=[f"pb{b}"])
                nt_ = len(tts)
                m.op(V, lambda e: e.tensor_copy(out=i_tm[h3][:, tts[0]:tts[0] + nt_, :],
                                                in_=bank(b)[:, 0:nt_ * 128].rearrange("p (a c) -> p a c", c=128)),
                     reads=[f"pb{b}"], writes=[f"i_tm{h3}"])
            return f

        for grp in range(3):
            items.append(igrp(grp))
        return items

    def E_chain(h):
        ops = []
        h2 = h % 2
        A_, Q_ = tA[h2], tQ[h2]
        ak, qk = f"tA{h2}", f"tQ{h2}"
        ops.append(lambda: m.op(V, lambda e: e.tensor_scalar(out=A_, in0=A_, scalar1=omlT[:, h:h + 1], scalar2=lbT[:, h:h + 1], op0=ALU.mult, op1=ALU.add),
             reads=[ak, "omlT", "lbT"], writes=[ak]))
        ops.append(lambda: m.op(S, lambda e: e.activation(out=tB, in_=A_, func=AF.Ln),
                                reads=[ak], writes=["tB"]))
        ops.append(lambda: m.op(V, lambda e: e.tensor_tensor_scan(out=tC, data0=smask, data1=tB, initial=0.0, op0=ALU.mult, op1=ALU.add),
             reads=["smask", "tB"], writes=["tC"]))
        ops.append(lambda: m.op(S, lambda e: e.activation(out=tE, in_=tC, func=AF.Exp), reads=["tC"], writes=["tE"]))
        ops.append(lambda: m.op(S, lambda e: e.activation(out=tF, in_=tC, func=AF.Exp, scale=-1.0), reads=["tC"], writes=["tF"]))
        ops.append(lambda: m.op(V, lambda e: e.tensor_scalar(out=A_, in0=A_, scalar1=-1.0, scalar2=1.0, op0=ALU.mult, op1=ALU.add),
             reads=[ak], writes=[ak]))
        ops.append(lambda: m.op(V, lambda e: e.tensor_tensor(out=qb[h2], in0=Q_, in1=tE[:, 128:TOK], op=ALU.mult), reads=[qk, "tE"], writes=[f"qb{h2}"]))
        ops.append(lambda: m.op(V, lambda e: e.tensor_tensor(out=kb_[h2], in0=A_, in1=tF, op=ALU.mult), reads=[ak, "tF"], writes=[f"kb{h2}"]))
        tE3 = tE.rearrange("p (c t) -> p c t", t=64)
        ops.append(lambda: m.op(V, lambda e: e.tensor_copy(out=decs[h2], in_=tE3[:, :, 63]), reads=["tE"], writes=[f"decs{h2}"]))
        ops.append(lambda: m.op(V, lambda e: e.tensor_tensor(out=tB.rearrange("p (c t) -> p c t", t=64), in0=tF.rearrange("p (c t) -> p c t", t=64),
                                         in1=tE3[:, :, 63:64].broadcast_to([128, 18, 64]), op=ALU.mult),
             reads=["tF", "tE"], writes=["tB"]))
        ops.append(lambda: m.op(V, lambda e: e.tensor_tensor(out=kd_bf, in0=A_, in1=tB, op=ALU.mult), reads=[ak, "tB"], writes=["kd_bf"]))
        return ops

    def T_items(h):
        h2 = h % 2
        items = []

        def tgrp(grp):
            def f():
                b = 7
                pb = bank_bf(b).rearrange("p (a c) -> p a c", c=128)
                tts = list(range(grp * 8, min(TT, grp * 8 + 8)))
                for j, tt in enumerate(tts):
                    m.op("pe", lambda e, j=j, tt=tt: e.transpose(out=pb[:, j, :], in_=kd_bf[:, tt * 128:(tt + 1) * 128], identity=ident),
                         reads=["kd_bf", "ident"], writes=[f"pb{b}"])
                nt_ = len(tts)
                m.op(S, lambda e: e.copy(out=kdtm[h2][:, tts[0]:tts[0] + nt_, :], in_=pb[:, 0:nt_, :]),
                     reads=[f"pb{b}"], writes=[f"kdtm{h2}"])
            return f

        return [tgrp(0), tgrp(1)]

    attT4 = alloc([128, 4, 128], BF16)

    def S2_items(h):
        h2, h3 = h % 2, h % 3

        def uslot(c):
            if c == 16:
                return 3, 0
            rnd = c // 8
            return (3 + 2 * rnd + (c % 2)), (c % 8) // 2

        def uburst(cs):
            for c in cs:
                pair, part = c // 2, c % 2
                ubk, usl = uslot(c)
                ub = bank(ubk)[:, usl * 128:(usl + 1) * 128]
                m.op("pe", lambda e, pair=pair, part=part, ub=ub: e.matmul(ub, lhsT=kdtm[h2][part * 64:(part + 1) * 64, pair, :],
                                                                          rhs=i_tm[h3][part * 64:(part + 1) * 64, pair, :], start=True, stop=True),
                     reads=[f"kdtm{h2}", f"i_tm{h3}"], writes=[f"pb{ubk}"])

        def chain(cs):
            for c in cs:
                cur, nxt = c % 2, (c + 1) % 2
                ubk, usl = uslot(c)
                ub = bank(ubk)[:, usl * 128:(usl + 1) * 128]
                if c >= 2:
                    m.op(S, lambda e, c=c, cur=cur: e.copy(out=S_bf[:, c - 2, :], in_=S_f[cur]), reads=[f"S_f{cur}"], writes=[f"S_bf{c - 2}"])
                m.op(V, lambda e, c=c, cur=cur, nxt=nxt, ub=ub: e.scalar_tensor_tensor(out=S_f[nxt], in0=S_f[cur], scalar=decs[h2][:, c:c + 1],
                                                                                  in1=ub, op0=ALU.mult, op1=ALU.add),
                     reads=[f"S_f{cur}", f"decs{h2}", f"pb{ubk}"], writes=[f"S_f{nxt}"])

        def att(bq):
            for pl in range(4):
                pair = 1 + bq * 4 + pl
                q0 = (pair - 1) * 128
                m.op("pe", lambda e, pair=pair, q0=q0, pl=pl: e.matmul(bank(7)[:, pl * 128:(pl + 1) * 128], lhsT=kb_[h2][:, pair * 128:(pair + 1) * 128],
                                                                      rhs=qb[h2][:, q0:q0 + 128], start=True, stop=True),
                     reads=[f"kb{h2}", f"qb{h2}"], writes=["pb7"])
            m.op(V, lambda e: e.tensor_tensor(out=attT4, in0=bank(7).rearrange("p (a c) -> p a c", c=128),
                                             in1=mask2.unsqueeze(1).broadcast_to([128, 4, 128]), op=ALU.mult),
                 reads=["pb7", "mask2"], writes=["attT4"])

        def omm(bq):
            ob = 5 + bq % 2
            for pl in range(4):
                pair = 1 + bq * 4 + pl
                q0 = (pair - 1) * 128
                for part in range(2):
                    c = 2 * pair + part
                    col = pl * 128 + part * 64
                    oc = bank(ob)[:, col:col + 64]
                    m.op("pe", lambda e, c=c, oc=oc, part=part, q0=q0: e.matmul(oc, lhsT=S_bf[:, c - 2, :],
                                                                                rhs=qb[h2][:, q0 + part * 64:q0 + part * 64 + 64],
                                                                                start=True, stop=False),
                         reads=[f"S_bf{c - 2}", f"qb{h2}"], writes=[f"pb{ob}"])
                    m.op("pe", lambda e, oc=oc, part=part, pair=pair, pl=pl: e.matmul(
                        oc, lhsT=i_tm[h3][part * 64:(part + 1) * 64, pair, :],
                        rhs=attT4[part * 64:(part + 1) * 64, pl, part * 64:(part + 1) * 64], start=False, stop=True),
                         reads=[f"i_tm{h3}", "attT4"], writes=[f"pb{ob}"])

        def norm(bq):
            ob = 5 + bq % 2
            m.op(S, lambda e: e.activation(out=osq, in_=bank(ob), func=AF.Square), reads=[f"pb{ob}"], writes=["osq"])
            m.op("pe", lambda e: e.matmul(bank(7), lhsT=ones_f, rhs=osq, start=True, stop=True), reads=["ones_f", "osq"], writes=["pb7"])
            m.op(S, lambda e: e.activation(out=rst, in_=bank(7), func=AF.Ln, scale=1.0 / 128, bias=epsT[:, 0:1]),
                 reads=["pb7", "epsT"], writes=["rst"])
            m.op(S, lambda e: e.activation(out=rst, in_=rst, func=AF.Exp, scale=-0.5), reads=["rst"], writes=["rst"])
            m.op(V, lambda e: e.tensor_tensor(out=t1, in0=bank(ob), in1=rst, op=ALU.mult), reads=[f"pb{ob}", "rst"], writes=["t1"])
            m.op(V, lambda e: e.scalar_tensor_tensor(out=catT[:, 8 + h, bq * 512:(bq + 1) * 512], in0=t1, scalar=rg[:, 0:1],
                                                    in1=gate[h3][:, bq * 512:(bq + 1) * 512], op0=ALU.mult, op1=ALU.mult),
                 reads=["t1", "rg", f"gate{h3}"], writes=[f"catR{h}_{bq}"])

        def g0():
            m.op(V, lambda e: e.memset(S_f[0], 0.0), writes=["S_f0"])
            uburst(range(0, 8))

        def g1():
            chain(range(0, 8))
            uburst(range(8, 16))

        def g2():
            chain(range(8, 16))
            uburst([16])

        def g3():
            chain([16])
            m.op(S, lambda e: e.copy(out=S_bf[:, 15, :], in_=S_f[1]), reads=["S_f1"], writes=["S_bf15"])
            att(0)

        def g4():
            omm(0)

        def g5():
            norm(0)
            att(1)

        def g6():
            omm(1)

        def g7():
            norm(1)

        return [g0, g1, g2, g3, g4, g5, g6, g7, (lambda: None), (lambda: None)]

    if stop == "rnn0":
        for f in P_items(0):
            f()
        for f in E_chain(0):
            f()
        for f in T_items(0):
            f()
        for f in S2_items(0):
            f()
        stats = m.emit(final_wait_keys=[])
        return nc, stats
    for it in range(8 + 2):
        A = P_items(it) if it < 8 else [(lambda: None)] * 10
        B = S2_items(it - 2) if 0 <= it - 2 < 8 else [(lambda: None)] * 10
        C = E_chain(it - 1) if 0 <= it - 1 < 8 else []
        na, ncn = len(A), len(C)
        ci = 0
        for ai in range(na):
            A[ai]()
            tc = min(ncn, {0: 1, 1: 1, 2: 3, 3: 6, 4: 8, 5: 10}.get(ai, ncn))
            while ci < tc:
                C[ci]()
                ci += 1
            B[ai]()
            if ai == 7 and 0 <= it - 1 < 8:
                for f in T_items(it - 1):
                    f()
    cat_keys = [f"catA{qt}" for qt in range(NT)] + [f"catR{h}_{bq}" for h in range(8) for bq in range(2)]
    if debug:
        m.dma(lambda e: e.dma_start(out=dbg_cat, in_=catT.rearrange("p a b -> p (a b)")), reads=cat_keys, writes=["dbg_cat"], sem="dbg_cat")

    m.barrier(lambda e: e.memset(scr[:, 2:3], 0.0))
    state["off"] = setup_mark
    _catT_again = alloc([128, 16, OWN], BF16)
    h2T = alloc([128, DC, OWN], BF16)
    x1 = alloc([128, NT, D], F32)
    gP = alloc([128, D], F32)
    gQ = alloc([128, D], F32)
    xr = [alloc([128, D], F32) for _ in range(2)]
    hn2 = [alloc([128, D], BF16) for _ in range(2)]
    m.dma(lambda e: e.dma_start(out=gP, in_=g_mix_post.partition_broadcast(128)), writes=["gP"], sem="gP")
    m.dma(lambda e: e.dma_start(out=gQ, in_=g_mlp_pre.partition_broadcast(128)), writes=["gQ"], sem="gQ")
    out_keys = []

    def stream(pieces, load, compute, la):
        loaded = {}
        n = len(pieces)
        for k in range(min(la, n)):
            loaded[k] = load(pieces[k])
        for k in range(n):
            if k + la < n:
                loaded[k + la] = load(pieces[k + la])
            compute(pieces[k], *loaded.pop(k))

    def cat_keys_for(tok_tile):
        return [f"catA{tok_tile}"] + [f"catR{h}_{tok_tile // 4}" for h in range(8)]

    def d_load(pc):
        hf, cg = pc
        return load_w(rows_view(w_out, cg * 512, 512), wview3(512), 4)

    def d_comp(pc, wo, wk):
        hf, cg = pc
        for tt in range(4 * hf, 4 * hf + 4):
            b = (cg * 4 + tt) % 8
            for ec in range(DC):
                m.op("pe", lambda e, ec=ec, tt=tt, b=b: e.matmul(bank(b), lhsT=catT[:, ec, tt * 128:(tt + 1) * 128],
                                                                 rhs=wo[:, ec, :], start=(ec == 0), stop=(ec == DC - 1)),
                     reads=wk + cat_keys_for(tt), writes=[f"pb{b}"])
            if tt % 2 == 0:
                m.op(S, lambda e, tt=tt, b=b: e.copy(out=x1[:, tt, cg * 512:(cg + 1) * 512], in_=bank(b)),
                     reads=[f"pb{b}"], writes=[f"x1_{tt}"])
            else:
                m.op(V, lambda e, tt=tt, b=b: e.tensor_copy(out=x1[:, tt, cg * 512:(cg + 1) * 512], in_=bank(b)),
                     reads=[f"pb{b}"], writes=[f"x1_{tt}"])

    all_cat = [f"catA{qt}" for qt in range(NT)] + [f"catR{h}_{bq}" for h in range(8) for bq in range(2)]
    junkD = [catT[:, 4 * i:4 * i + 4, 0:512] for i in range(2)]
    junk_keys = [f"catA{t}" for t in range(4)]

    dstat = {}

    def d_s1a(tt):
        row0 = 128 + tt * 128
        m.dma(lambda e: e.dma_start(out=xr[tt % 2], in_=xin[row0:row0 + 128, :]), writes=[f"xr{tt % 2}"], sem=f"xr{tt % 2}")
        stc = scol()
        ss = stat[:, stc:stc + 1]
        skey = f"stat{stc}"
        dstat[(1, tt)] = (ss, skey)
        m.op(S, lambda e: e.activation(out=junkD[tt % 2], in_=x1[:, tt, :].rearrange("p (a b) -> p a b", b=512),
                                             func=AF.Square, accum_out=ss),
             reads=[f"x1_{tt}"], writes=junk_keys + [skey])

    def d_s1c(tt):
        ss, skey = dstat[(1, tt)]
        rstd_chain(ss, skey, 1.0 / D)

    def d_s1b(tt):
        ss, skey = dstat[(1, tt)]
        m.op(V, lambda e: e.scalar_tensor_tensor(out=x1[:, tt, :], in0=x1[:, tt, :], scalar=ss, in1=gP,
                                                 op0=ALU.mult, op1=ALU.mult),
             reads=[f"x1_{tt}", skey, "gP"], writes=[f"x1_{tt}"])
        m.op(V, lambda e: e.tensor_tensor(out=x1[:, tt, :], in0=x1[:, tt, :], in1=xr[tt % 2], op=ALU.add),
             reads=[f"x1_{tt}", f"xr{tt % 2}"], writes=[f"x1_{tt}"])
        r0 = tt * 128
        m.dma(lambda e: e.dma_start(out=out[r0:r0 + 128, :], in_=x1[:, tt, :]), reads=[f"x1_{tt}"],
              writes=[f"out{tt}"], sem=f"outw{tt % 2}")
        if debug:
            m.dma(lambda e: e.dma_start(out=dbg_x1[r0:r0 + 128, :], in_=x1[:, tt, :]), reads=[f"x1_{tt}"],
                  writes=[f"dbgx{r0}"], sem=f"dbgx{tt % 2}")

    def d_s2a(tt):
        stc = scol()
        ss = stat[:, stc:stc + 1]
        skey = f"stat{stc}"
        dstat[(2, tt)] = (ss, skey)
        m.op(S, lambda e: e.activation(out=junkD[tt % 2], in_=x1[:, tt, :].rearrange("p (a b) -> p a b", b=512),
                                             func=AF.Square, accum_out=ss),
             reads=[f"x1_{tt}"], writes=junk_keys + [skey])

    def d_s2c(tt):
        ss, skey = dstat[(2, tt)]
        rstd_chain(ss, skey, 1.0 / D)

    def d_s2b(tt):
        ss, skey = dstat[(2, tt)]
        hnb, hnkey = hn2[tt % 2], f"hn2{tt % 2}"
        m.op(V, lambda e: e.scalar_tensor_tensor(out=hnb, in0=x1[:, tt, :], scalar=ss, in1=gQ, op0=ALU.mult, op1=ALU.mult),
             reads=[f"x1_{tt}", skey, "gQ"], writes=[hnkey])

    def d_s3(tt):
        hnb, hnkey = hn2[tt % 2], f"hn2{tt % 2}"
        for half in range(2):
            b = 2 * (tt % 4) + half
            pb = bank_bf(b).rearrange("p (a c) -> p a c", c=128)
            for j in range(8):
                dc = half * 8 + j
                m.op("pe", lambda e, o=pb[:, j, :], i=hnb[:, dc * 128:(dc + 1) * 128]: e.transpose(out=o, in_=i, identity=ident),
                     reads=[hnkey, "ident"], writes=[f"pb{b}"])
            dst = h2T[:, half * 8:(half + 1) * 8, tt * 128:(tt + 1) * 128]
            if half == 0:
                m.op(S, lambda e, o=dst, i=pb[:, 0:8, :]: e.copy(out=o, in_=i), reads=[f"pb{b}"], writes=[f"h2T{tt}"])
            else:
                m.op(V, lambda e, o=dst, i=pb[:, 0:8, :]: e.tensor_copy(out=o, in_=i), reads=[f"pb{b}"], writes=[f"h2T{tt}"])

    def post_step(step, tiles):
        n = len(tiles)
        for lag, fn in ((0, d_s1a), (2, d_s2a), (1, d_s1b), (3, d_s2b), (4, d_s3), (0, d_s1c), (2, d_s2c)):
            if 0 <= step - lag < n:
                fn(tiles[step - lag])

    dpieces = [(hf, cg) for hf in range(2) for cg in range(4)]
    dloaded = {0: d_load(dpieces[0])}
    first_half_steps = {4: [0, 1], 5: [2, 3], 6: [4, 5], 7: [6, 7]}
    for k in range(8):
        if k + 1 < 8:
            dloaded[k + 1] = d_load(dpieces[k + 1])
        d_comp(dpieces[k], *dloaded.pop(k))
        for st_ in first_half_steps.get(k, []):
            post_step(st_, [0, 1, 2, 3])
    for st_ in range(8):
        post_step(st_, [4, 5, 6, 7])
    h2keys = [f"h2T{t}" for t in range(NT)]

    m.barrier(lambda e: e.memset(scr[:, 3:4], 0.0))
    state["off"] = setup_mark
    yA = alloc([128, 4, D], F32)
    _h2T_again = alloc([128, DC, OWN], BF16)
    yB = alloc([128, 4, D], F32)
    u2T = alloc([128, 32, OWN], BF16)
    rr = [alloc([128, 512], F32) for _ in range(2)]
    gP2 = alloc([128, D], F32)
    m.dma(lambda e: e.dma_start(out=gP2, in_=g_mlp_post.partition_broadcast(128)), writes=["gP2"], sem="gP2")

    def yv(tt):
        return (yA if tt < 4 else yB)[:, tt % 4, :]

    lastw = {}

    def final_tile(tt):
        stc = scol()
        ss = stat[:, stc:stc + 1]
        skey = f"stat{stc}"
        ri = tt % 2
        junk = h2T[:, ri * 2:ri * 2 + 2, :].rearrange("p a b -> p (a b)")
        jkeys = list(h2keys)
        m.op(S, lambda e, tt=tt, ss=ss, junk=junk: e.activation(out=junk, in_=yv(tt), func=AF.Square, accum_out=ss),
             reads=[f"y{tt}"], writes=jkeys + [skey])
        rstd_chain(ss, skey, 1.0 / D)
        m.op(V, lambda e, tt=tt, ss=ss: e.scalar_tensor_tensor(out=yv(tt), in0=yv(tt), scalar=ss, in1=gP2, op0=ALU.mult, op1=ALU.mult),
             reads=[f"y{tt}", skey, "gP2"], writes=[f"y{tt}"])
        r0 = tt * 128
        m.dma(lambda e, tt=tt, r0=r0: e.dma_start(out=out[r0:r0 + 128, :], in_=yv(tt), accum_op=ALU.add),
              reads=[f"y{tt}", f"out{tt}"], writes=[f"out{tt}"], sem=f"outa{tt}", queue="pool")
        out_keys.append(f"out{tt}")

    upc = {"n": 0}
    e_pieces = []
    for half in range(2):
        def u_load(fcg, half=half):
            return load_w(rows_view(w_up, (half * 16 + fcg) * 256, 256), wview3(256), 2)

        def u_comp(fcg, wu, wk, half=half):
            for fci in range(2):
                fcl = 2 * fcg + fci
                for sl in range(2):
                    b = upc["n"] % 8
                    ri = upc["n"] % 2
                    upc["n"] += 1
                    for dc in range(DC):
                        m.op("pe", lambda e, dc=dc, fci=fci, b=b, sl=sl: e.matmul(bank(b), lhsT=wu[:, dc, fci * 128:(fci + 1) * 128],
                                                                                  rhs=h2T[:, dc, sl * 512:(sl + 1) * 512],
                                                                                  start=(dc == 0), stop=(dc == DC - 1)),
                             reads=wk + h2keys[sl * 4:(sl + 1) * 4], writes=[f"pb{b}"])
                    m.op(S, lambda e, b=b, ri=ri: e.activation(out=rr[ri], in_=bank(b), func=AF.Relu), reads=[f"pb{b}"], writes=[f"rr{ri}"])
                    m.op(P, lambda e, fcl=fcl, ri=ri, sl=sl: e.tensor_tensor(out=u2T[:, fcl, sl * 512:(sl + 1) * 512], in0=rr[ri], in1=rr[ri], op=ALU.mult),
                         reads=[f"rr{ri}"], writes=[f"u2T{fcl}_{sl}"])

        e_pieces += [(u_load, u_comp, fcg) for fcg in range(16)]

        pieces = [(cg, r) for cg in range(4) for r in range(4)]

        def dn_load(pc, half=half):
            cg, r = pc
            f0 = (half * 32 + r * 8) * 128
            src = w_down[f0:f0 + 1024, cg * 512:(cg + 1) * 512].rearrange("(fc p) c -> p fc c", p=128)
            return load_w(src, wview3(512), 2)

        def dn_comp(pc, wd, wk, half=half):
            cg, r = pc
            if half == 1 and cg == 3:
                lastw[r] = (wd, wk)
                if r < 3:
                    return
                for tt in range(NT):
                    b = tt
                    for r2 in range(4):
                        wd2, wk2 = lastw[r2]
                        for j in range(8):
                            fcl = r2 * 8 + j
                            m.op("pe", lambda e, tt=tt, j=j, fcl=fcl, b=b, wd2=wd2, r2=r2: e.matmul(
                                bank(b), lhsT=u2T[:, fcl, tt * 128:(tt + 1) * 128], rhs=wd2[:, j, :],
                                start=(r2 == 0 and j == 0), stop=(r2 == 3 and j == 7)),
                                 reads=wk2 + [f"u2T{fcl}_{tt // 4}"], writes=[f"pb{b}"])
                    dst = yv(tt)[:, cg * 512:(cg + 1) * 512]
                    m.op(V, lambda e, dst=dst, b=b: e.tensor_tensor(out=dst, in0=dst, in1=bank(b), op=ALU.add),
                         reads=[f"pb{b}", f"y{tt}"], writes=[f"y{tt}"])
                    final_tile(tt)
                return
            for tt in range(NT):
                b = tt
                for j in range(8):
                    fcl = r * 8 + j
                    m.op("pe", lambda e, tt=tt, j=j, fcl=fcl, b=b: e.matmul(bank(b), lhsT=u2T[:, fcl, tt * 128:(tt + 1) * 128], rhs=wd[:, j, :],
                                                                            start=(r == 0 and j == 0), stop=(r == 3 and j == 7)),
                         reads=wk + [f"u2T{fcl}_{tt // 4}"], writes=[f"pb{b}"])
                if r == 3:
                    dst = yv(tt)[:, cg * 512:(cg + 1) * 512]
                    if half == 0:
                        if tt % 2 == 0:
                            m.op(S, lambda e, dst=dst, b=b: e.copy(out=dst, in_=bank(b)), reads=[f"pb{b}"], writes=[f"y{tt}"])
                        else:
                            m.op(V, lambda e, dst=dst, b=b: e.tensor_copy(out=dst, in_=bank(b)), reads=[f"pb{b}"], writes=[f"y{tt}"])
                    else:
                        m.op(V, lambda e, dst=dst, b=b: e.tensor_tensor(out=dst, in0=dst, in1=bank(b), op=ALU.add),
                             reads=[f"pb{b}", f"y{tt}"], writes=[f"y{tt}"])

        e_pieces += [(dn_load, dn_comp, pc) for pc in pieces]

    stream(e_pieces, lambda p: p[0](p[2]), lambda p, w, k: p[1](p[2], w, k), 3)

    fin = list(out_keys)
    if debug:
        fin += ["dbg_cat"] + [k for k in m.last_w if k.startswith("dbgx")]
    stats = m.emit(final_wait_keys=fin)
    return nc, stats


_CACHE = {}


def _prep_inputs(x, w_in, attn_sinks, attn_out_gain, rnn_lb_logits, rnn_norm_gain, w_out,
                 mix_pre_gain, mix_post_gain, mlp_pre_gain, mlp_post_gain, w_up, w_down, cores=range(8)):
    f = lambda a: np.ascontiguousarray(np.asarray(a, dtype=np.float32))
    x = f(x)
    shared = {
        "w_in": f(w_in)[0], "w_out": f(w_out)[0], "w_up": f(w_up)[0], "w_down": f(w_down)[0],
        "sinks": f(attn_sinks).reshape(1, NH), "attn_gain": f(attn_out_gain).reshape(1, 1024),
        "lb_logits": f(rnn_lb_logits).reshape(2, 1024), "rnn_gain": f(rnn_norm_gain).reshape(1, 128),
        "g_mix_pre": f(mix_pre_gain).reshape(1, D), "g_mix_post": f(mix_post_gain).reshape(1, D),
        "g_mlp_pre": f(mlp_pre_gain).reshape(1, D), "g_mlp_post": f(mlp_post_gain).reshape(1, D),
    }
    in_maps = []
    for c in cores:
        b, j = c // 4, c % 4
        xin = np.zeros((TOK, D), np.float32)
        xin[128:] = x[b, j * OWN:(j + 1) * OWN]
        if j > 0:
            xin[:128] = x[b, j * OWN - 128:j * OWN]
        hmv = np.full((128, 1), NEG8 if j == 0 else 0.0, np.float32)
        d = dict(shared)
        d["xin"] = xin
        d["hm"] = hmv
        in_maps.append(d)
    return in_maps


def kernel(**inputs):
    if "nc" not in _CACHE:
        _CACHE["nc"] = build_program(debug=False)[0]
    nc = _CACHE["nc"]
    in_maps = _prep_inputs(**inputs)
    res = run_bass_kernel_spmd(nc, in_maps, core_ids=list(range(8)))
    outp = np.zeros((2, 4096, D), np.float32)
    for c in range(8):
        b, j = c // 4, c % 4
        outp[b, j * OWN:(j + 1) * OWN] = res.results[c]["out"]
    return outp
```

```python
import numpy as np
import concourse.bass as bass
import concourse.mybir as mybir
from concourse.bass_utils import run_bass_kernel_spmd

F32 = mybir.dt.float32
BF16 = mybir.dt.bfloat16
I32 = mybir.dt.int32
U8 = mybir.dt.uint8
AF = mybir.ActivationFunctionType
ALU = mybir.AluOpType

ENGS = ("pe", "dve", "act", "pool", "sp")
HANDLES = {"pe": "tensor", "dve": "vector", "act": "scalar", "pool": "gpsimd", "sp": "sync"}


class _Op:
    __slots__ = ("eng", "fn", "deps", "needed", "dom", "val", "waits", "is_dma", "dsem", "idx")


class MK:
    def __init__(self, nc, same_engine_sync=True):
        self.nc = nc
        self.ops = []
        self.last_w = {}
        self.readers = {}
        self.same_engine_sync = same_engine_sync
        self.token = None
        self.last_eng = {}
        self.last_sem = {}

    def _record(self, eng, fn, reads, writes, is_dma=False, dsem=None, free=False, extra=()):
        op = _Op()
        op.eng, op.fn, op.is_dma, op.dsem = eng, fn, is_dma, dsem
        op.needed = False
        op.idx = len(self.ops)
        deps = set(extra)
        if self.token is not None and not free:
            deps.add(self.token)
        for k in reads:
            w = self.last_w.get(k)
            if w is not None:
                deps.add(w)
        for k in writes:
            w = self.last_w.get(k)
            if w is not None:
                deps.add(w)
            for r in self.readers.get(k, ()):
                deps.add(r)
        deps.discard(op.idx)
        op.deps = deps
        self.ops.append(op)
        for k in reads:
            self.readers.setdefault(k, []).append(op.idx)
        for k in writes:
            self.last_w[k] = op.idx
            self.readers[k] = []
        if is_dma:
            self.last_sem[dsem] = op.idx
        else:
            self.last_eng[eng] = op.idx
        return op

    def op(self, eng, fn, reads=(), writes=(), free=False):
        return self._record(eng, fn, tuple(reads), tuple(writes), free=free)

    def dma(self, fn, reads=(), writes=(), sem="dma0", queue="sp", free=False):
        return self._record(queue, fn, tuple(reads), tuple(writes), is_dma=True, dsem=sem, free=free)

    def barrier(self, fn, exclude_sem_prefix="wb"):
        extra = set(self.last_eng.values())
        for s, i in self.last_sem.items():
            if not s.startswith(exclude_sem_prefix):
                extra.add(i)
        op = self._record("dve", fn, (), (), extra=extra)
        self.token = op.idx
        return op

    def emit(self, final_wait_keys=()):
        nc = self.nc
        ops = self.ops
        self._record("sp", None, tuple(final_wait_keys), ())
        for op in ops:
            for d in op.deps:
                ops[d].needed = True
        cnt = {}
        clock_of = [None] * len(ops)
        eng_clock = {e: {} for e in ENGS}
        per_eng = {e: [] for e in ENGS}
        for op in ops:
            e = op.eng
            ec = eng_clock[e]
            waits = {}
            for d in op.deps:
                dop = ops[d]
                if (not dop.is_dma) and dop.eng == e and (e == "pe" or not self.same_engine_sync):
                    continue
                dom, val = dop.dom, dop.val
                if ec.get(dom, 0) >= val:
                    continue
                if waits.get(dom, 0) < val:
                    waits[dom] = val
            if waits:
                for d in op.deps:
                    dop = ops[d]
                    if dop.dom in waits and waits[dop.dom] >= dop.val:
                        for k2, v2 in clock_of[d].items():
                            if ec.get(k2, 0) < v2:
                                ec[k2] = v2
                for dom, val in waits.items():
                    if ec.get(dom, 0) < val:
                        ec[dom] = val
            op.waits = list(waits.items())
            if op.is_dma:
                dom = ("dma", op.dsem)
                cnt[dom] = cnt.get(dom, 0) + 16
                op.dom, op.val = dom, cnt[dom]
                ck = dict(ec)
                ck[dom] = op.val
                clock_of[op.idx] = ck
            else:
                dom = ("eng", e)
                if op.needed:
                    cnt[dom] = cnt.get(dom, 0) + 1
                    op.dom, op.val = dom, cnt[dom]
                    ck = dict(ec)
                    ck[dom] = op.val
                    clock_of[op.idx] = ck
                else:
                    op.dom, op.val = dom, None
            per_eng[e].append(op)
        self.stats = {e: len(v) for e, v in per_eng.items()}
        self.stats["waits"] = sum(len(o.waits) for o in ops)
        doms = set()
        for op in ops:
            if op.is_dma or op.needed:
                doms.add(op.dom)
        self.stats["sems"] = len(doms)
        from contextlib import ExitStack
        with ExitStack() as st:
            sem = {}
            for i, dom in enumerate(sorted(doms, key=str)):
                sem[dom] = st.enter_context(nc.semaphore(f"s{i}"))
            block = st.enter_context(nc.Block())

            def make(ename):
                lst = per_eng[ename]

                def body(eng):
                    for op in lst:
                        for dom, val in op.waits:
                            eng.wait_ge(sem[dom], val)
                        if op.fn is None:
                            continue
                        ins = op.fn(eng)
                        if op.is_dma:
                            ins.then_inc(sem[op.dom], 16)
                        elif op.needed:
                            ins.then_inc(sem[op.dom], 1)
                return body

            for ename in ENGS:
                if per_eng[ename]:
                    getattr(block, HANDLES[ename])(make(ename))
        return self.stats


D = 2048
DC = 16
NT = 8
TT = 9
TOK = 1152
OWN = 1024
NH = 16
DFF = 8192
EPS = 1e-6
NEG8 = -30000.0
ARENA = 206 * 1024
WUNIT = 4096
NWU = 8

C_Q, C_K, C_V, C_QR, C_FR, C_IR, C_GR = 0, 1024, 1152, 1280, 2304, 3328, 4352


def build_program(debug=False, stop=None, reg=None):
    nc = bass.Bass("TRN2", target_bir_lowering=False)
    reg = {} if reg is None else reg

    def dram(name, shape, dt=F32, kind="ExternalInput"):
        return nc.dram_tensor(name, list(shape), dt, kind=kind).ap()

    xin = dram("xin", [TOK, D])
    hm = dram("hm", [128, 1])
    w_in = dram("w_in", [D, 5376])
    w_out = dram("w_out", [D, D])
    w_up = dram("w_up", [D, DFF])
    w_down = dram("w_down", [DFF, D])
    sinks = dram("sinks", [1, NH])
    attn_gain = dram("attn_gain", [1, 1024])
    lb_logits = dram("lb_logits", [2, 1024])
    rnn_gain = dram("rnn_gain", [1, 128])
    g_mix_pre = dram("g_mix_pre", [1, D])
    g_mix_post = dram("g_mix_post", [1, D])
    g_mlp_pre = dram("g_mlp_pre", [1, D])
    g_mlp_post = dram("g_mlp_post", [1, D])
    out = dram("out", [OWN, D], kind="ExternalOutput")
    if debug:
        dbg_cat = dram("dbg_cat", [128, 16 * OWN], BF16, kind="ExternalOutput")
        dbg_x1 = dram("dbg_x1", [OWN, D], kind="ExternalOutput")

    m = MK(nc)
    arena = nc.alloc_sbuf_tensor("arena", [128, ARENA], U8)
    state = {"off": 0}

    def alloc(shape, dt=F32):
        esz = 2 if dt == BF16 else 4
        n = esz
        for s in shape[1:]:
            n *= s
        off = (state["off"] + 63) // 64 * 64
        reg[len(reg)] = (off, tuple(shape), "bf16" if dt == BF16 else ("i32" if dt == I32 else "f32"))
        assert off + n <= ARENA, f"SBUF arena overflow: {off + n} > {ARENA}"
        state["off"] = off + n
        v = arena[:, off:off + n].bitcast(dt)
        if len(shape) == 3:
            v = v.rearrange("p (a b) -> p a b", b=shape[2])
        elif len(shape) == 4:
            v = v.rearrange("p (a b c) -> p a b c", b=shape[2], c=shape[3])
        return v

    pd = [nc.alloc_psum_tensor(f"pd{i}", [128, 1024], F32) for i in range(4)]

    def bank(b):
        return pd[b // 2][:, (b % 2) * 512:(b % 2) * 512 + 512]

    def bank_bf(b):
        return bank(b).bitcast(BF16)

    V, S, A_, P = "dve", "act", "act", "pool"

    ident = alloc([128, 128], BF16)
    ones_f = alloc([128, 128], F32)
    mask2 = alloc([128, 128], F32)
    epsT = alloc([128, 1], F32)
    scr = alloc([128, 4], F32)
    sink_bc = alloc([128, NH], F32)
    rg = alloc([128, 1], F32)
    lbT = alloc([128, 8], F32)
    omlT = alloc([128, 8], F32)
    hm_sb = alloc([128, 1], F32)
    stat = alloc([128, 64], F32)
    wpool = alloc([128, NWU * WUNIT // 2], BF16)
    persist_mark = state["off"]
    stat_col = {"n": 0}

    def scol(n=1):
        c = stat_col["n"]
        stat_col["n"] += n
        assert stat_col["n"] <= 64
        return c

    wstate = {"next": 0}

    def walloc(units):
        u = wstate["next"]
        u = (u + units - 1) // units * units
        if u + units > NWU:
            u = 0
        wstate["next"] = u + units
        keys = [f"wb{u + i}" for i in range(units)]
        base = wpool[:, u * (WUNIT // 2):(u + units) * (WUNIT // 2)]
        return base, keys, f"wb{u}"

    def load_w(src_ap, view_fn, units):
        base, keys, sem = walloc(units)
        dst = view_fn(base)
        m.dma(lambda e, dst=dst, src=src_ap: e.dma_start(out=dst, in_=src), writes=keys, sem=sem,
              queue="pool", free=True)
        return dst, keys

    def wview3(n):
        return lambda base: base.rearrange("p (a b) -> p a b", b=n)

    def rows_view(w, c0, n):
        return w[:, c0:c0 + n].rearrange("(dc p) c -> p dc c", p=128)

    setup_mark = state["off"]
    catT = alloc([128, 16, OWN], BF16)
    identf = alloc([128, 128], F32)
    m.op(P, lambda e: e.memset(identf, 1.0), writes=["identf"])
    m.op(P, lambda e: e.affine_select(out=identf, in_=identf, pattern=[[-1, 128]], compare_op=ALU.is_equal,
                                      fill=0.0, base=0, channel_multiplier=1), reads=["identf"], writes=["identf"])
    m.op(V, lambda e: e.tensor_copy(out=ident, in_=identf), reads=["identf"], writes=["ident"])
    m.op(P, lambda e: e.memset(ones_f, 1.0), writes=["ones_f"])
    m.op(P, lambda e: e.memset(epsT, EPS), writes=["epsT"])
    m.op(V, lambda e: e.memset(scr, 0.0), writes=["scr"])
    m.op(P, lambda e: e.memset(mask2, 1.0), writes=["mask2"])
    m.op(P, lambda e: e.affine_select(out=mask2, in_=mask2, pattern=[[1, 128]], compare_op=ALU.is_ge,
                                      fill=0.0, base=0, channel_multiplier=-1), reads=["mask2"], writes=["mask2"])
    m.op(P, lambda e: e.memset(mask2[0:64, 64:128], 0.0), reads=["mask2"], writes=["mask2"])
    m.dma(lambda e: e.dma_start(out=sink_bc, in_=sinks.partition_broadcast(128)), writes=["sink_bc"], sem="c_sink")
    m.dma(lambda e: e.dma_start(out=rg, in_=rnn_gain.rearrange("o k -> k o")), writes=["rg"], sem="c_rg")
    m.dma(lambda e: e.dma_start(out=hm_sb, in_=hm), writes=["hm"], sem="c_hm")
    l0 = alloc([128, 8], F32)
    l1 = alloc([128, 8], F32)
    m.dma(lambda e: e.dma_start(out=l0, in_=lb_logits[0:1, :].rearrange("o (h k) -> k (o h)", k=128),
                                allow_slow_non_contiguous=True), writes=["l0"], sem="c_l0")
    m.dma(lambda e: e.dma_start(out=l1, in_=lb_logits[1:2, :].rearrange("o (h k) -> k (o h)", k=128),
                                allow_slow_non_contiguous=True), writes=["l1"], sem="c_l1")
    m.op(S, lambda e: e.activation(out=l0, in_=l0, func=AF.Exp), reads=["l0"], writes=["l0"])
    m.op(S, lambda e: e.activation(out=l1, in_=l1, func=AF.Exp), reads=["l1"], writes=["l1"])
    m.op(V, lambda e: e.tensor_tensor(out=omlT, in0=l0, in1=l1, op=ALU.add), reads=["l0", "l1"], writes=["omlT"])
    m.op(V, lambda e: e.reciprocal(out=omlT, in_=omlT), reads=["omlT"], writes=["omlT"])
    m.op(V, lambda e: e.tensor_tensor(out=lbT, in0=l0, in1=omlT, op=ALU.mult), reads=["l0", "omlT"], writes=["lbT"])
    m.op(V, lambda e: e.tensor_scalar(out=omlT, in0=lbT, scalar1=-1.0, scalar2=1.0, op0=ALU.mult, op1=ALU.add),
         reads=["lbT"], writes=["omlT"])

    hT = alloc([128, DC, TOK], BF16)
    ac_mark = state["off"]
    Mt = alloc([128, 256], F32)
    Mt8 = alloc([128, 256], BF16)
    Mt8_0 = alloc([128, 256], BF16)
    Kr_i = alloc([128, 256], I32)
    Kr = alloc([128, 256], BF16)
    Sl = alloc([128, NH, 128], BF16)
    qc_i = alloc([128, 1], I32)
    qcol = alloc([128, 1], F32)
    sc = alloc([128, NH], F32)
    nsc = alloc([128, NH], F32)
    ab_mark = state["off"]
    gA = alloc([128, D], F32)
    xb = [alloc([128, D], F32) for _ in range(3)]
    hn = [alloc([128, D], BF16) for _ in range(3)]
    m.dma(lambda e: e.dma_start(out=gA, in_=g_mix_pre.partition_broadcast(128)), writes=["gA"], sem="gA")

    def rstd_chain(ss_ap, key, n_inv):
        m.op(V, lambda e: e.tensor_scalar(out=ss_ap, in0=ss_ap, scalar1=n_inv, scalar2=EPS, op0=ALU.mult, op1=ALU.add),
             reads=[key], writes=[key])
        m.op(S, lambda e: e.activation(out=ss_ap, in_=ss_ap, func=AF.Sqrt), reads=[key], writes=[key])
        m.op(V, lambda e: e.reciprocal(out=ss_ap, in_=ss_ap), reads=[key], writes=[key])

    def norm_transpose(src, src_keys, gain, gain_key, hnb, hnkey, dstT, dst_key, col0, stc, bnk):
        ss = stat[:, stc:stc + 1]
        skey = f"stat{stc}"
        m.op(S, lambda e: e.activation(out=hnb, in_=src, func=AF.Square, accum_out=ss),
             reads=src_keys, writes=[hnkey, skey])
        rstd_chain(ss, skey, 1.0 / D)
        m.op(V, lambda e: e.scalar_tensor_tensor(out=hnb, in0=src, scalar=ss, in1=gain, op0=ALU.mult, op1=ALU.mult),
             reads=list(src_keys) + [skey, gain_key], writes=[hnkey])
        for half in range(2):
            b = bnk[half]
            pb = bank_bf(b).rearrange("p (a c) -> p a c", c=128)
            for j in range(8):
                dc = half * 8 + j
                m.op("pe", lambda e, o=pb[:, j, :], i=hnb[:, dc * 128:(dc + 1) * 128]: e.transpose(out=o, in_=i, identity=ident),
                     reads=[hnkey, "ident"], writes=[f"pb{b}"])
            eng = S if half == 0 else V
            dst = dstT[:, half * 8:(half + 1) * 8, col0:col0 + 128]
            if eng == S:
                m.op(S, lambda e, o=dst, i=pb[:, 0:8, :]: e.copy(out=o, in_=i), reads=[f"pb{b}"], writes=[dst_key])
            else:
                m.op(V, lambda e, o=dst, i=pb[:, 0:8, :]: e.tensor_copy(out=o, in_=i), reads=[f"pb{b}"], writes=[dst_key])

    m.op(P, lambda e: e.memset(Mt, 0.0), writes=["Mt"])
    m.op(P, lambda e: e.affine_select(out=Mt, in_=Mt, pattern=[[1, 256]], compare_op=ALU.is_ge, fill=8.0 * NEG8,
                                      base=-1, channel_multiplier=-1), reads=["Mt"], writes=["Mt"])
    m.op(P, lambda e: e.affine_select(out=Mt, in_=Mt, pattern=[[-1, 256]], compare_op=ALU.is_ge, fill=8.0 * NEG8,
                                      base=128, channel_multiplier=1), reads=["Mt"], writes=["Mt"])
    m.op(V, lambda e: e.tensor_copy(out=Mt8, in_=Mt), reads=["Mt"], writes=["Mt8"])
    m.op(V, lambda e: e.tensor_copy(out=Mt8_0[:, 128:256], in_=Mt[:, 128:256]), reads=["Mt"], writes=["Mt8_0"])
    m.op(V, lambda e: e.scalar_tensor_tensor(out=Mt8_0[:, 0:128], in0=hm_sb[:, 0:1].broadcast_to([128, 128]), scalar=8.0,
                                            in1=Mt[:, 0:128], op0=ALU.mult, op1=ALU.add),
         reads=["Mt", "hm"], writes=["Mt8_0"])
    m.op(P, lambda e: e.iota(Kr_i, pattern=[[1, 256]], base=0, channel_multiplier=0), writes=["Kr_i"])
    m.op(V, lambda e: e.tensor_copy(out=Kr, in_=Kr_i), reads=["Kr_i"], writes=["Kr"])
    m.op(P, lambda e: e.iota(qc_i, pattern=[[0, 1]], base=128, channel_multiplier=1), writes=["qc_i"])
    m.op(V, lambda e: e.tensor_copy(out=qcol, in_=qc_i), reads=["qc_i"], writes=["qcol"])
    m.op(P, lambda e: e.memset(Sl, 0.0), writes=["Sl"])
    import ml_dtypes as _mld
    for h in range(NH):
        slope = 2.0 ** (-8.0 * (h + 1) / NH)
        hi = float(np.float32(8.0 * slope).astype(_mld.bfloat16))
        mid = float(np.float32(8.0 * slope - hi).astype(_mld.bfloat16))
        lo = float(np.float32(8.0 * slope - hi - mid).astype(_mld.bfloat16))
        m.op(P, lambda e, h=h, hi=hi: e.memset(Sl[0:1, h, :], hi), reads=["Sl"], writes=["Sl"])
        m.op(P, lambda e, h=h, mid=mid: e.memset(Sl[32:33, h, :], mid), reads=["Sl"], writes=["Sl"])
        m.op(P, lambda e, h=h, lo=lo: e.memset(Sl[64:65, h, :], lo), reads=["Sl"], writes=["Sl"])
        m.op(V, lambda e, h=h, slope=slope: e.scalar_tensor_tensor(out=sc[:, h:h + 1], in0=qcol, scalar=slope, in1=sink_bc[:, h:h + 1],
                                                                  op0=ALU.mult, op1=ALU.add),
             reads=["qcol", "sink_bc"], writes=["sc"])
    m.op(V, lambda e: e.tensor_scalar(out=nsc, in0=sc, scalar1=-1.0, scalar2=None, op0=ALU.mult), reads=["sc"], writes=["nsc"])

    astat = {}

    def a_s1(tt):
        xt = xb[tt % 3]
        m.dma(lambda e: e.dma_start(out=xt, in_=xin[tt * 128:(tt + 1) * 128, :]), writes=[f"xb{tt % 3}"], sem=f"xb{tt % 3}")
        stc = scol()
        ss = stat[:, stc:stc + 1]
        skey = f"stat{stc}"
        astat[tt] = (ss, skey)
        m.op(S, lambda e: e.activation(out=hn[tt % 3], in_=xt, func=AF.Square, accum_out=ss),
             reads=[f"xb{tt % 3}"], writes=[f"hn{tt % 3}", skey])

    def a_s1c(tt):
        ss, skey = astat[tt]
        rstd_chain(ss, skey, 1.0 / D)

    def a_s2(tt):
        ss, skey = astat[tt]
        m.op(V, lambda e: e.scalar_tensor_tensor(out=hn[tt % 3], in0=xb[tt % 3], scalar=ss, in1=gA, op0=ALU.mult, op1=ALU.mult),
             reads=[f"xb{tt % 3}", skey, "gA"], writes=[f"hn{tt % 3}"])

    def a_s3(tt):
        hnb, hnkey = hn[tt % 3], f"hn{tt % 3}"
        for half in range(2):
            b = 2 * (tt % 4) + half
            pb = bank_bf(b).rearrange("p (a c) -> p a c", c=128)
            for j in range(8):
                dc = half * 8 + j
                m.op("pe", lambda e, o=pb[:, j, :], i=hnb[:, dc * 128:(dc + 1) * 128]: e.transpose(out=o, in_=i, identity=ident),
                     reads=[hnkey, "ident"], writes=[f"pb{b}"])
            dst = hT[:, half * 8:(half + 1) * 8, tt * 128:(tt + 1) * 128]
            if half == 0:
                m.op(S, lambda e, o=dst, i=pb[:, 0:8, :]: e.copy(out=o, in_=i), reads=[f"pb{b}"], writes=[f"hT{tt}"])
            else:
                m.op(V, lambda e, o=dst, i=pb[:, 0:8, :]: e.tensor_copy(out=o, in_=i), reads=[f"pb{b}"], writes=[f"hT{tt}"])

    for step in range(TT + 2):
        for lag, fn in ((0, a_s1), (1, a_s2), (2, a_s3), (0, a_s1c)):
            if 0 <= step - lag < TT:
                fn(step - lag)
    hT_keys = [f"hT{tt}" for tt in range(TT)]

    def hkeys(t0, t1):
        return [f"hT{t}" for t in range(t0 // 128, (t1 + 127) // 128)]

    m.barrier(lambda e: e.memset(scr[:, 0:1], 0.0))
    state["off"] = ab_mark
    ag_bc = alloc([128, 1024], F32)
    qT_all = alloc([128, 8, OWN], BF16)
    kT = alloc([128, TOK], BF16)
    v_sb = alloc([128, TT, 128], BF16)
    p_bf = [alloc([128, 256], BF16) for _ in range(3)]
    pT_sb = [alloc([128, 2, 128], BF16) for _ in range(2)]
    attn_sb = alloc([128, NH, 64], F32)
    an_bf = alloc([128, 1024], BF16)
    hstat = [alloc([128, 5, NH], F32) for _ in range(2)]

    m.dma(lambda e: e.dma_start(out=ag_bc, in_=attn_gain.partition_broadcast(128)), writes=["ag_bc"], sem="ag_bc")
    wkv, wkv_keys = load_w(rows_view(w_in, C_K, 256), wview3(256), 2)
    slabs = [(0, 512), (512, 1024), (1024, 1152)]
    for si, (t0, t1) in enumerate(slabs):
        b = si % 2
        n = t1 - t0
        for dc in range(DC):
            m.op("pe", lambda e, b=b, n=n, dc=dc, t0=t0, t1=t1: e.matmul(bank(b)[:, 0:n], lhsT=wkv[:, dc, 0:128],
                                                                        rhs=hT[:, dc, t0:t1], start=(dc == 0), stop=(dc == DC - 1)),
                 reads=wkv_keys + hkeys(t0, t1), writes=[f"pb{b}"])
        m.op(S, lambda e, b=b, n=n, t0=t0, t1=t1: e.copy(out=kT[:, t0:t1], in_=bank(b)[:, 0:n]),
             reads=[f"pb{b}"], writes=["kT"])
    for grp in range(3):
        b = 2 + grp % 2
        tts = list(range(grp * 4, min(TT, grp * 4 + 4)))
        for j, tt in enumerate(tts):
            for dc in range(DC):
                m.op("pe", lambda e, b=b, j=j, tt=tt, dc=dc: e.matmul(bank(b)[:, j * 128:(j + 1) * 128],
                                                                      lhsT=hT[:, dc, tt * 128:(tt + 1) * 128],
                                                                      rhs=wkv[:, dc, 128:256], start=(dc == 0), stop=(dc == DC - 1)),
                     reads=wkv_keys + [f"hT{tt}"], writes=[f"pb{b}"])
        nt_ = len(tts)
        m.op(V, lambda e, b=b, nt_=nt_, t0=tts[0]: e.tensor_copy(
            out=v_sb[:, t0:t0 + nt_, :], in_=bank(b)[:, 0:nt_ * 128].rearrange("p (a c) -> p a c", c=128)),
             reads=[f"pb{b}"], writes=["v_sb"])
    for pr in range(8):
        base, keys, sem = walloc(1)
        wq = base.rearrange("p (a b c) -> p a b c", b=2, c=64)
        for hh in range(2):
            c0 = C_Q + (pr + 8 * hh) * 64
            m.dma(lambda e, wq=wq, hh=hh, c0=c0: e.dma_start(out=wq[:, :, hh, :], in_=rows_view(w_in, c0, 64)),
                  writes=keys, sem=sem, queue="pool", free=True)
        for sl in range(2):
            b = 4 + (pr * 2 + sl) % 2
            t0 = 128 + sl * 512
            for dc in range(DC):
                m.op("pe", lambda e, b=b, wq=wq, dc=dc, t0=t0: e.matmul(bank(b)[:, 0:512], lhsT=wq[:, dc, :, :],
                                                                        rhs=hT[:, dc, t0:t0 + 512], start=(dc == 0), stop=(dc == DC - 1)),
                     reads=keys + hkeys(t0, t0 + 512), writes=[f"pb{b}"])
            eng = S if sl == 0 else V
            if eng == S:
                m.op(S, lambda e, b=b, pr=pr, sl=sl: e.copy(out=qT_all[:, pr, sl * 512:(sl + 1) * 512], in_=bank(b)[:, 0:512]),
                     reads=[f"pb{b}"], writes=[f"qT{pr}"])
            else:
                m.op(V, lambda e, b=b, pr=pr, sl=sl: e.tensor_copy(out=qT_all[:, pr, sl * 512:(sl + 1) * 512], in_=bank(b)[:, 0:512]),
                     reads=[f"pb{b}"], writes=[f"qT{pr}"])

    headw = {}

    def load_head(h):
        headw[h] = (load_w(rows_view(w_in, C_FR + h * 128, 128), wview3(128), 1),
                    load_w(rows_view(w_in, C_QR + h * 128, 128), wview3(128), 1),
                    load_w(rows_view(w_in, C_GR + h * 128, 128), wview3(128), 1),
                    load_w(rows_view(w_in, C_IR + h * 128, 128), wview3(128), 1))

    load_head(0)
    pending_tail = []
    for qt in range(NT):
        hs = hstat[qt % 2]
        hsk = f"hstat{qt % 2}"
        o_ps = pd[2 + qt % 2]
        okeys = [f"pb{4 + 2 * (qt % 2)}", f"pb{5 + 2 * (qt % 2)}"]

        def st_S(h, qt=qt, hs=hs, hsk=hsk):
            pr, hh = h % 8, h // 8
            i = h % 3
            mt, mk = (Mt8_0, "Mt8_0") if qt == 0 else (Mt8, "Mt8")
            m.op("pe", lambda e: e.matmul(bank(i)[:, 0:256], lhsT=qT_all[hh * 64:(hh + 1) * 64, pr, qt * 128:(qt + 1) * 128],
                                          rhs=kT[hh * 64:(hh + 1) * 64, qt * 128:qt * 128 + 256], start=True, stop=False),
                 reads=[f"qT{pr}", "kT"], writes=[f"pb{i}"])
            m.op("pe", lambda e: e.matmul(bank(i)[:, 0:256], lhsT=ident, rhs=mt, start=False, stop=False),
                 reads=["ident", mk], writes=[f"pb{i}"])
            m.op("pe", lambda e: e.matmul(bank(i)[:, 0:256], lhsT=Sl[:, h, :], rhs=Kr, start=False, stop=True),
                 reads=["Sl", "Kr"], writes=[f"pb{i}"])
            m.op(V, lambda e: e.tensor_reduce(out=hs[:, 0, h:h + 1], in_=bank(i)[:, 0:256], axis=mybir.AxisListType.X, op=ALU.max),
                 reads=[f"pb{i}"], writes=[f"{hsk}_rmax{h}"])
            m.op(V, lambda e: e.tensor_scalar(out=hs[:, 1, h:h + 1], in0=hs[:, 0, h:h + 1], scalar1=-0.125, scalar2=nsc[:, h:h + 1],
                                             op0=ALU.mult, op1=ALU.min),
                 reads=[f"{hsk}_rmax{h}", "nsc"], writes=[f"{hsk}_negm{h}"])
            m.op(S, lambda e: e.activation(out=p_bf[i], in_=bank(i)[:, 0:256], func=AF.Exp, bias=hs[:, 1, h:h + 1], scale=0.125,
                                          accum_out=hs[:, 2, h:h + 1]),
                 reads=[f"pb{i}", f"{hsk}_negm{h}"], writes=[f"p_bf{i}", f"{hsk}_rsum{h}"])

        def st_T(h, qt=qt):
            i = h % 3
            j = h % 2
            pb = bank_bf(3).rearrange("p (a c) -> p a c", c=128)
            for kb in range(2):
                m.op("pe", lambda e, kb=kb: e.transpose(out=pb[:, kb, :], in_=p_bf[i][:, kb * 128:(kb + 1) * 128], identity=ident),
                     reads=[f"p_bf{i}", "ident"], writes=["pb3"])
            if j == 0:
                m.op(S, lambda e: e.copy(out=pT_sb[j], in_=pb[:, 0:2, :]), reads=["pb3"], writes=[f"pT_sb{j}"])
            else:
                m.op(V, lambda e: e.tensor_copy(out=pT_sb[j], in_=pb[:, 0:2, :]), reads=["pb3"], writes=[f"pT_sb{j}"])

        def st_PV(h, qt=qt, o_ps=o_ps, okeys=okeys):
            i = h % 2
            hh = h // 8
            for kb in range(2):
                m.op("pe", lambda e, kb=kb: e.matmul(o_ps[:, h * 64:(h + 1) * 64], lhsT=pT_sb[i][:, kb, :],
                                                     rhs=v_sb[:, qt + kb, hh * 64:(hh + 1) * 64], start=(kb == 0), stop=(kb == 1)),
                     reads=[f"pT_sb{i}", "v_sb"], writes=[okeys[h // 8]])

        def tail(qt=qt, hs=hs, hsk=hsk, o_ps=o_ps, okeys=okeys):
            allk = lambda nm: [f"{hsk}_{nm}{h}" for h in range(NH)]
            stc = scol()
            ss = stat[:, stc:stc + 1]
            skey = f"stat{stc}"
            attn_flat = attn_sb.rearrange("p h c -> p (h c)")
            g = []
            g.append(lambda: m.op(V, lambda e: e.tensor_tensor(out=hs[:, 3, :], in0=sc, in1=hs[:, 1, :], op=ALU.add),
                                  reads=["sc"] + allk("negm"), writes=[f"{hsk}_es"]))
            g.append(lambda: m.op(S, lambda e: e.activation(out=hs[:, 3, :], in_=hs[:, 3, :], func=AF.Exp),
                                  reads=[f"{hsk}_es"], writes=[f"{hsk}_es"]))
            g.append(lambda: m.op(V, lambda e: e.tensor_tensor(out=hs[:, 3, :], in0=hs[:, 3, :], in1=hs[:, 2, :], op=ALU.add),
                                  reads=[f"{hsk}_es"] + allk("rsum"), writes=[f"{hsk}_es"]))
            g.append(lambda: m.op(V, lambda e: e.reciprocal(out=hs[:, 4, :], in_=hs[:, 3, :]), reads=[f"{hsk}_es"], writes=[f"{hsk}_rinv"]))
            g.append(lambda: m.op(V, lambda e: e.tensor_tensor(out=attn_sb, in0=o_ps.rearrange("p (h c) -> p h c", c=64),
                                                              in1=hs[:, 4, :].unsqueeze(2).broadcast_to([128, NH, 64]), op=ALU.mult),
                                  reads=okeys + [f"{hsk}_rinv"], writes=["attn_sb"]))
            g.append(lambda: m.op(S, lambda e: e.activation(out=an_bf, in_=attn_flat, func=AF.Square, accum_out=ss),
                                  reads=["attn_sb"], writes=["an_bf", skey]))
            g.append(lambda: m.op(V, lambda e: e.tensor_scalar(out=ss, in0=ss, scalar1=1.0 / 1024, scalar2=EPS, op0=ALU.mult, op1=ALU.add),
                                  reads=[skey], writes=[skey]))
            g.append(lambda: m.op(S, lambda e: e.activation(out=ss, in_=ss, func=AF.Sqrt), reads=[skey], writes=[skey]))
            g.append(lambda: m.op(V, lambda e: e.reciprocal(out=ss, in_=ss), reads=[skey], writes=[skey]))
            g.append(lambda: m.op(V, lambda e: e.scalar_tensor_tensor(out=an_bf, in0=attn_flat, scalar=ss, in1=ag_bc, op0=ALU.mult, op1=ALU.mult),
                                  reads=["attn_sb", skey, "ag_bc"], writes=["an_bf"]))

            def tr():
                b = 3
                pb = bank_bf(b).rearrange("p (a c) -> p a c", c=128)
                for j in range(8):
                    m.op("pe", lambda e, j=j: e.transpose(out=pb[:, j, :], in_=an_bf[:, j * 128:(j + 1) * 128], identity=ident),
                         reads=["an_bf", "ident"], writes=[f"pb{b}"])
                m.op(S, lambda e: e.copy(out=catT[:, 0:8, qt * 128:(qt + 1) * 128], in_=pb[:, 0:8, :]),
                     reads=[f"pb{b}"], writes=[f"catA{qt}"])
            g.append(tr)
            return g

        for step in range(NH + 3):
            if step < NH:
                st_S(step)
            if 0 <= step - 2 < NH:
                st_T(step - 2)
            if 0 <= step - 3 < NH:
                st_PV(step - 3)
            if step >= 4 and pending_tail:
                pending_tail.pop(0)()
        for f in pending_tail:
            f()
        pending_tail = tail()
    for f in pending_tail:
        f()
    m.barrier(lambda e: e.memset(scr[:, 1:2], 0.0))
    state["off"] = ac_mark
    smask = alloc([128, TOK], F32)
    tA = [alloc([128, TOK], F32) for _ in range(2)]
    tQ = [alloc([128, OWN], F32) for _ in range(2)]
    tB = alloc([128, TOK], F32)
    tC = alloc([128, TOK], F32)
    tE = alloc([128, TOK], F32)
    tF = alloc([128, TOK], F32)
    kd_bf = alloc([128, TOK], BF16)
    gate = [alloc([128, OWN], F32) for _ in range(3)]
    i_tm = [alloc([128, TT, 128], BF16) for _ in range(3)]
    qb = [alloc([128, OWN], BF16) for _ in range(2)]
    kb_ = [alloc([128, TOK], BF16) for _ in range(2)]
    kdtm = [alloc([128, TT, 128], BF16) for _ in range(2)]
    decs = [alloc([128, 18], F32) for _ in range(2)]
    S_f = [alloc([128, 128], F32) for _ in range(2)]
    S_bf = alloc([128, 16, 128], BF16)
    attT_bf = [alloc([128, 128], BF16) for _ in range(2)]
    osq = alloc([128, 512], F32)
    rst = alloc([128, 512], F32)
    t1 = alloc([128, 512], F32)

    m.op(P, lambda e: e.memset(smask, 1.0), writes=["smask"])
    m.op(P, lambda e: e.memset(smask.rearrange("p (c t) -> p c t", t=64)[:, :, 0:1], 0.0), reads=["smask"], writes=["smask"])
    pcnt = {"b": 0}

    def P_items(h):
        h2, h3 = h % 2, h % 3
        if h not in headw:
            load_head(h)
        (wfr, kf), (wqr, kq), (wgr, kg), (wir, ki) = headw.pop(h)
        if h + 1 < 8:
            load_head(h + 1)
        items = []

        def slab(w, wk, t0, t1_, func, dst, dkey):
            def f():
                b = pcnt["b"] % 3
                pcnt["b"] += 1
                n = t1_ - t0
                for dc in range(DC):
                    m.op("pe", lambda e, dc=dc: e.matmul(bank(b)[:, 0:n], lhsT=w[:, dc, :], rhs=hT[:, dc, t0:t1_],
                                                         start=(dc == 0), stop=(dc == DC - 1)),
                         reads=wk + hkeys(t0, t1_), writes=[f"pb{b}"])
                m.op(S, lambda e: e.activation(out=dst, in_=bank(b)[:, 0:n], func=func), reads=[f"pb{b}"], writes=[dkey])
            return f

        for (t0, t1_) in slabs:
            items.append(slab(wfr, kf, t0, t1_, AF.Sigmoid, tA[h2][:, t0:t1_], f"tA{h2}"))
        for sl in range(2):
            t0 = 128 + sl * 512
            items.append(slab(wqr, kq, t0, t0 + 512, AF.Silu, tQ[h2][:, sl * 512:(sl + 1) * 512], f"tQ{h2}"))
        for sl in range(2):
            t0 = 128 + sl * 512
            items.append(slab(wgr, kg, t0, t0 + 512, AF.Silu, gate[h3][:, sl * 512:(sl + 1) * 512], f"gate{h3}"))

        def igrp(grp):
            def f():
                b = pcnt["b"] % 3
                pcnt["b"] += 1
                tts = list(range(grp * 4, min(TT, grp * 4 + 4)))
                for j, tt in enumerate(tts):
                    for dc in range(DC):
                        m.op("pe", lambda e, j=j, tt=tt, dc=dc: e.matmul(bank(b)[:, j * 128:(j + 1) * 128],
                                                                         lhsT=hT[:, dc, tt * 128:(tt + 1) * 128], rhs=wir[:, dc, :],
                                                                         start=(dc == 0), stop=(dc == DC - 1)),
                             reads=ki + [f"hT{tt}"], writes=[f"pb{b}"])
                nt_ = len(tts)
                m.op(V, lambda e: e.tensor_copy(out=i_tm[h3][:, tts[0]:tts[0] + nt_, :],
                                                in_=bank(b)[:, 0:nt_ * 128].rearrange("p (a c) -> p a c", c=128)),
                     reads=[f"pb{b}"], writes=[f"i_tm{h3}"])
            return f

        for grp in range(3):
            items.append(igrp(grp))
        return items

    def E_chain(h):
        ops = []
        h2 = h % 2
        A_, Q_ = tA[h2], tQ[h2]
        ak, qk = f"tA{h2}", f"tQ{h2}"
        ops.append(lambda: m.op(V, lambda e: e.tensor_scalar(out=A_, in0=A_, scalar1=omlT[:, h:h + 1], scalar2=lbT[:, h:h + 1], op0=ALU.mult, op1=ALU.add),
             reads=[ak, "omlT", "lbT"], writes=[ak]))
        ops.append(lambda: m.op(S, lambda e: e.activation(out=tB, in_=A_, func=AF.Ln),
                                reads=[ak], writes=["tB"]))
        ops.append(lambda: m.op(V, lambda e: e.tensor_tensor_scan(out=tC, data0=smask, data1=tB, initial=0.0, op0=ALU.mult, op1=ALU.add),
             reads=["smask", "tB"], writes=["tC"]))
        ops.append(lambda: m.op(S, lambda e: e.activation(out=tE, in_=tC, func=AF.Exp), reads=["tC"], writes=["tE"]))
        ops.append(lambda: m.op(S, lambda e: e.activation(out=tF, in_=tC, func=AF.Exp, scale=-1.0), reads=["tC"], writes=["tF"]))
        ops.append(lambda: m.op(V, lambda e: e.tensor_scalar(out=A_, in0=A_, scalar1=-1.0, scalar2=1.0, op0=ALU.mult, op1=ALU.add),
             reads=[ak], writes=[ak]))
        ops.append(lambda: m.op(V, lambda e: e.tensor_tensor(out=qb[h2], in0=Q_, in1=tE[:, 128:TOK], op=ALU.mult), reads=[qk, "tE"], writes=[f"qb{h2}"]))
        ops.append(lambda: m.op(V, lambda e: e.tensor_tensor(out=kb_[h2], in0=A_, in1=tF, op=ALU.mult), reads=[ak, "tF"], writes=[f"kb{h2}"]))
        tE3 = tE.rearrange("p (c t) -> p c t", t=64)
        ops.append(lambda: m.op(V, lambda e: e.tensor_copy(out=decs[h2], in_=tE3[:, :, 63]), reads=["tE"], writes=[f"decs{h2}"]))
        ops.append(lambda: m.op(V, lambda e: e.tensor_tensor(out=tB.rearrange("p (c t) -> p c t", t=64), in0=tF.rearrange("p (c t) -> p c t", t=64),
                                         in1=tE3[:, :, 63:64].broadcast_to([128, 18, 64]), op=ALU.mult),
             reads=["tF", "tE"], writes=["tB"]))
        ops.append(lambda: m.op(V, lambda e: e.tensor_tensor(out=kd_bf, in0=A_, in1=tB, op=ALU.mult), reads=[ak, "tB"], writes=["kd_bf"]))
        return ops

    def T_items(h):
        h2 = h % 2
        items = []

        def tgrp(grp):
            def f():
                b = 7
                pb = bank_bf(b).rearrange("p (a c) -> p a c", c=128)
                tts = list(range(grp * 8, min(TT, grp * 8 + 8)))
                for j, tt in enumerate(tts):
                    m.op("pe", lambda e, j=j, tt=tt: e.transpose(out=pb[:, j, :], in_=kd_bf[:, tt * 128:(tt + 1) * 128], identity=ident),
                         reads=["kd_bf", "ident"], writes=[f"pb{b}"])
                nt_ = len(tts)
                m.op(S, lambda e: e.copy(out=kdtm[h2][:, tts[0]:tts[0] + nt_, :], in_=pb[:, 0:nt_, :]),
                     reads=[f"pb{b}"], writes=[f"kdtm{h2}"])
            return f

        return [tgrp(0), tgrp(1)]

    attT4 = alloc([128, 4, 128], BF16)

    def S2_items(h):
        h2, h3 = h % 2, h % 3

        def uslot(c):
            if c == 16:
                return 3, 0
            rnd = c // 8
            return (3 + 2 * rnd + (c % 2)), (c % 8) // 2

        def uburst(cs):
            for c in cs:
                pair, part = c // 2, c % 2
                ubk, usl = uslot(c)
                ub = bank(ubk)[:, usl * 128:(usl + 1) * 128]
                m.op("pe", lambda e, pair=pair, part=part, ub=ub: e.matmul(ub, lhsT=kdtm[h2][part * 64:(part + 1) * 64, pair, :],
                                                                          rhs=i_tm[h3][part * 64:(part + 1) * 64, pair, :], start=True, stop=True),
                     reads=[f"kdtm{h2}", f"i_tm{h3}"], writes=[f"pb{ubk}"])

        def chain(cs):
            for c in cs:
                cur, nxt = c % 2, (c + 1) % 2
                ubk, usl = uslot(c)
                ub = bank(ubk)[:, usl * 128:(usl + 1) * 128]
                if c >= 2:
                    m.op(S, lambda e, c=c, cur=cur: e.copy(out=S_bf[:, c - 2, :], in_=S_f[cur]), reads=[f"S_f{cur}"], writes=[f"S_bf{c - 2}"])
                m.op(V, lambda e, c=c, cur=cur, nxt=nxt, ub=ub: e.scalar_tensor_tensor(out=S_f[nxt], in0=S_f[cur], scalar=decs[h2][:, c:c + 1],
                                                                                  in1=ub, op0=ALU.mult, op1=ALU.add),
                     reads=[f"S_f{cur}", f"decs{h2}", f"pb{ubk}"], writes=[f"S_f{nxt}"])

        def att(bq):
            for pl in range(4):
                pair = 1 + bq * 4 + pl
                q0 = (pair - 1) * 128
                m.op("pe", lambda e, pair=pair, q0=q0, pl=pl: e.matmul(bank(7)[:, pl * 128:(pl + 1) * 128], lhsT=kb_[h2][:, pair * 128:(pair + 1) * 128],
                                                                      rhs=qb[h2][:, q0:q0 + 128], start=True, stop=True),
                     reads=[f"kb{h2}", f"qb{h2}"], writes=["pb7"])
            m.op(V, lambda e: e.tensor_tensor(out=attT4, in0=bank(7).rearrange("p (a c) -> p a c", c=128),
                                             in1=mask2.unsqueeze(1).broadcast_to([128, 4, 128]), op=ALU.mult),
                 reads=["pb7", "mask2"], writes=["attT4"])

        def omm(bq):
            ob = 5 + bq % 2
            for pl in range(4):
                pair = 1 + bq * 4 + pl
                q0 = (pair - 1) * 128
                for part in range(2):
                    c = 2 * pair + part
                    col = pl * 128 + part * 64
                    oc = bank(ob)[:, col:col + 64]
                    m.op("pe", lambda e, c=c, oc=oc, part=part, q0=q0: e.matmul(oc, lhsT=S_bf[:, c - 2, :],
                                                                                rhs=qb[h2][:, q0 + part * 64:q0 + part * 64 + 64],
                                                                                start=True, stop=False),
                         reads=[f"S_bf{c - 2}", f"qb{h2}"], writes=[f"pb{ob}"])
                    m.op("pe", lambda e, oc=oc, part=part, pair=pair, pl=pl: e.matmul(
                        oc, lhsT=i_tm[h3][part * 64:(part + 1) * 64, pair, :],
                        rhs=attT4[part * 64:(part + 1) * 64, pl, part * 64:(part + 1) * 64], start=False, stop=True),
                         reads=[f"i_tm{h3}", "attT4"], writes=[f"pb{ob}"])

        def norm(bq):
            ob = 5 + bq % 2
            m.op(S, lambda e: e.activation(out=osq, in_=bank(ob), func=AF.Square), reads=[f"pb{ob}"], writes=["osq"])
            m.op("pe", lambda e: e.matmul(bank(7), lhsT=ones_f, rhs=osq, start=True, stop=True), reads=["ones_f", "osq"], writes=["pb7"])
            m.op(S, lambda e: e.activation(out=rst, in_=bank(7), func=AF.Ln, scale=1.0 / 128, bias=epsT[:, 0:1]),
                 reads=["pb7", "epsT"], writes=["rst"])
            m.op(S, lambda e: e.activation(out=rst, in_=rst, func=AF.Exp, scale=-0.5), reads=["rst"], writes=["rst"])
            m.op(V, lambda e: e.tensor_tensor(out=t1, in0=bank(ob), in1=rst, op=ALU.mult), reads=[f"pb{ob}", "rst"], writes=["t1"])
            m.op(V, lambda e: e.scalar_tensor_tensor(out=catT[:, 8 + h, bq * 512:(bq + 1) * 512], in0=t1, scalar=rg[:, 0:1],
                                                    in1=gate[h3][:, bq * 512:(bq + 1) * 512], op0=ALU.mult, op1=ALU.mult),
                 reads=["t1", "rg", f"gate{h3}"], writes=[f"catR{h}_{bq}"])

        def g0():
            m.op(V, lambda e: e.memset(S_f[0], 0.0), writes=["S_f0"])
            uburst(range(0, 8))

        def g1():
            chain(range(0, 8))
            uburst(range(8, 16))

        def g2():
            chain(range(8, 16))
            uburst([16])

        def g3():
            chain([16])
            m.op(S, lambda e: e.copy(out=S_bf[:, 15, :], in_=S_f[1]), reads=["S_f1"], writes=["S_bf15"])
            att(0)

        def g4():
            omm(0)

        def g5():
            norm(0)
            att(1)

        def g6():
            omm(1)

        def g7():
            norm(1)

        return [g0, g1, g2, g3, g4, g5, g6, g7, (lambda: None), (lambda: None)]

    if stop == "rnn0":
        for f in P_items(0):
            f()
        for f in E_chain(0):
            f()
        for f in T_items(0):
            f()
        for f in S2_items(0):
            f()
        stats = m.emit(final_wait_keys=[])
        return nc, stats
    for it in range(8 + 2):
        A = P_items(it) if it < 8 else [(lambda: None)] * 10
        B = S2_items(it - 2) if 0 <= it - 2 < 8 else [(lambda: None)] * 10
        C = E_chain(it - 1) if 0 <= it - 1 < 8 else []
        na, ncn = len(A), len(C)
        ci = 0
        for ai in range(na):
            A[ai]()
            tc = min(ncn, {0: 1, 1: 1, 2: 3, 3: 6, 4: 8, 5: 10}.get(ai, ncn))
            while ci < tc:
                C[ci]()
                ci += 1
            B[ai]()
            if ai == 7 and 0 <= it - 1 < 8:
                for f in T_items(it - 1):
                    f()
    cat_keys = [f"catA{qt}" for qt in range(NT)] + [f"catR{h}_{bq}" for h in range(8) for bq in range(2)]
    if debug:
        m.dma(lambda e: e.dma_start(out=dbg_cat, in_=catT.rearrange("p a b -> p (a b)")), reads=cat_keys, writes=["dbg_cat"], sem="dbg_cat")

    m.barrier(lambda e: e.memset(scr[:, 2:3], 0.0))
    state["off"] = setup_mark
    _catT_again = alloc([128, 16, OWN], BF16)
    h2T = alloc([128, DC, OWN], BF16)
    x1 = alloc([128, NT, D], F32)
    gP = alloc([128, D], F32)
    gQ = alloc([128, D], F32)
    xr = [alloc([128, D], F32) for _ in range(2)]
    hn2 = [alloc([128, D], BF16) for _ in range(2)]
    m.dma(lambda e: e.dma_start(out=gP, in_=g_mix_post.partition_broadcast(128)), writes=["gP"], sem="gP")
    m.dma(lambda e: e.dma_start(out=gQ, in_=g_mlp_pre.partition_broadcast(128)), writes=["gQ"], sem="gQ")
    out_keys = []

    def stream(pieces, load, compute, la):
        loaded = {}
        n = len(pieces)
        for k in range(min(la, n)):
            loaded[k] = load(pieces[k])
        for k in range(n):
            if k + la < n:
                loaded[k + la] = load(pieces[k + la])
            compute(pieces[k], *loaded.pop(k))

    def cat_keys_for(tok_tile):
        return [f"catA{tok_tile}"] + [f"catR{h}_{tok_tile // 4}" for h in range(8)]

    def d_load(pc):
        hf, cg = pc
        return load_w(rows_view(w_out, cg * 512, 512), wview3(512), 4)

    def d_comp(pc, wo, wk):
        hf, cg = pc
        for tt in range(4 * hf, 4 * hf + 4):
            b = (cg * 4 + tt) % 8
            for ec in range(DC):
                m.op("pe", lambda e, ec=ec, tt=tt, b=b: e.matmul(bank(b), lhsT=catT[:, ec, tt * 128:(tt + 1) * 128],
                                                                 rhs=wo[:, ec, :], start=(ec == 0), stop=(ec == DC - 1)),
                     reads=wk + cat_keys_for(tt), writes=[f"pb{b}"])
            if tt % 2 == 0:
                m.op(S, lambda e, tt=tt, b=b: e.copy(out=x1[:, tt, cg * 512:(cg + 1) * 512], in_=bank(b)),
                     reads=[f"pb{b}"], writes=[f"x1_{tt}"])
            else:
                m.op(V, lambda e, tt=tt, b=b: e.tensor_copy(out=x1[:, tt, cg * 512:(cg + 1) * 512], in_=bank(b)),
                     reads=[f"pb{b}"], writes=[f"x1_{tt}"])

    all_cat = [f"catA{qt}" for qt in range(NT)] + [f"catR{h}_{bq}" for h in range(8) for bq in range(2)]
    junkD = [catT[:, 4 * i:4 * i + 4, 0:512] for i in range(2)]
    junk_keys = [f"catA{t}" for t in range(4)]

    dstat = {}

    def d_s1a(tt):
        row0 = 128 + tt * 128
        m.dma(lambda e: e.dma_start(out=xr[tt % 2], in_=xin[row0:row0 + 128, :]), writes=[f"xr{tt % 2}"], sem=f"xr{tt % 2}")
        stc = scol()
        ss = stat[:, stc:stc + 1]
        skey = f"stat{stc}"
        dstat[(1, tt)] = (ss, skey)
        m.op(S, lambda e: e.activation(out=junkD[tt % 2], in_=x1[:, tt, :].rearrange("p (a b) -> p a b", b=512),
                                             func=AF.Square, accum_out=ss),
             reads=[f"x1_{tt}"], writes=junk_keys + [skey])

    def d_s1c(tt):
        ss, skey = dstat[(1, tt)]
        rstd_chain(ss, skey, 1.0 / D)

    def d_s1b(tt):
        ss, skey = dstat[(1, tt)]
        m.op(V, lambda e: e.scalar_tensor_tensor(out=x1[:, tt, :], in0=x1[:, tt, :], scalar=ss, in1=gP,
                                                 op0=ALU.mult, op1=ALU.mult),
             reads=[f"x1_{tt}", skey, "gP"], writes=[f"x1_{tt}"])
        m.op(V, lambda e: e.tensor_tensor(out=x1[:, tt, :], in0=x1[:, tt, :], in1=xr[tt % 2], op=ALU.add),
             reads=[f"x1_{tt}", f"xr{tt % 2}"], writes=[f"x1_{tt}"])
        r0 = tt * 128
        m.dma(lambda e: e.dma_start(out=out[r0:r0 + 128, :], in_=x1[:, tt, :]), reads=[f"x1_{tt}"],
              writes=[f"out{tt}"], sem=f"outw{tt % 2}")
        if debug:
            m.dma(lambda e: e.dma_start(out=dbg_x1[r0:r0 + 128, :], in_=x1[:, tt, :]), reads=[f"x1_{tt}"],
                  writes=[f"dbgx{r0}"], sem=f"dbgx{tt % 2}")

    def d_s2a(tt):
        stc = scol()
        ss = stat[:, stc:stc + 1]
        skey = f"stat{stc}"
        dstat[(2, tt)] = (ss, skey)
        m.op(S, lambda e: e.activation(out=junkD[tt % 2], in_=x1[:, tt, :].rearrange("p (a b) -> p a b", b=512),
                                             func=AF.Square, accum_out=ss),
             reads=[f"x1_{tt}"], writes=junk_keys + [skey])

    def d_s2c(tt):
        ss, skey = dstat[(2, tt)]
        rstd_chain(ss, skey, 1.0 / D)

    def d_s2b(tt):
        ss, skey = dstat[(2, tt)]
        hnb, hnkey = hn2[tt % 2], f"hn2{tt % 2}"
        m.op(V, lambda e: e.scalar_tensor_tensor(out=hnb, in0=x1[:, tt, :], scalar=ss, in1=gQ, op0=ALU.mult, op1=ALU.mult),
             reads=[f"x1_{tt}", skey, "gQ"], writes=[hnkey])

    def d_s3(tt):
        hnb, hnkey = hn2[tt % 2], f"hn2{tt % 2}"
        for half in range(2):
            b = 2 * (tt % 4) + half
            pb = bank_bf(b).rearrange("p (a c) -> p a c", c=128)
            for j in range(8):
                dc = half * 8 + j
                m.op("pe", lambda e, o=pb[:, j, :], i=hnb[:, dc * 128:(dc + 1) * 128]: e.transpose(out=o, in_=i, identity=ident),
                     reads=[hnkey, "ident"], writes=[f"pb{b}"])
            dst = h2T[:, half * 8:(half + 1) * 8, tt * 128:(tt + 1) * 128]
            if half == 0:
                m.op(S, lambda e, o=dst, i=pb[:, 0:8, :]: e.copy(out=o, in_=i), reads=[f"pb{b}"], writes=[f"h2T{tt}"])
            else:
                m.op(V, lambda e, o=dst, i=pb[:, 0:8, :]: e.tensor_copy(out=o, in_=i), reads=[f"pb{b}"], writes=[f"h2T{tt}"])

    def post_step(step, tiles):
        n = len(tiles)
        for lag, fn in ((0, d_s1a), (2, d_s2a), (1, d_s1b), (3, d_s2b), (4, d_s3), (0, d_s1c), (2, d_s2c)):
            if 0 <= step - lag < n:
                fn(tiles[step - lag])

    dpieces = [(hf, cg) for hf in range(2) for cg in range(4)]
    dloaded = {0: d_load(dpieces[0])}
    first_half_steps = {4: [0, 1], 5: [2, 3], 6: [4, 5], 7: [6, 7]}
    for k in range(8):
        if k + 1 < 8:
            dloaded[k + 1] = d_load(dpieces[k + 1])
        d_comp(dpieces[k], *dloaded.pop(k))
        for st_ in first_half_steps.get(k, []):
            post_step(st_, [0, 1, 2, 3])
    for st_ in range(8):
        post_step(st_, [4, 5, 6, 7])
    h2keys = [f"h2T{t}" for t in range(NT)]

    m.barrier(lambda e: e.memset(scr[:, 3:4], 0.0))
    state["off"] = setup_mark
    yA = alloc([128, 4, D], F32)
    _h2T_again = alloc([128, DC, OWN], BF16)
    yB = alloc([128, 4, D], F32)
    u2T = alloc([128, 32, OWN], BF16)
    rr = [alloc([128, 512], F32) for _ in range(2)]
    gP2 = alloc([128, D], F32)
    m.dma(lambda e: e.dma_start(out=gP2, in_=g_mlp_post.partition_broadcast(128)), writes=["gP2"], sem="gP2")

    def yv(tt):
        return (yA if tt < 4 else yB)[:, tt % 4, :]

    lastw = {}

    def final_tile(tt):
        stc = scol()
        ss = stat[:, stc:stc + 1]
        skey = f"stat{stc}"
        ri = tt % 2
        junk = h2T[:, ri * 2:ri * 2 + 2, :].rearrange("p a b -> p (a b)")
        jkeys = list(h2keys)
        m.op(S, lambda e, tt=tt, ss=ss, junk=junk: e.activation(out=junk, in_=yv(tt), func=AF.Square, accum_out=ss),
             reads=[f"y{tt}"], writes=jkeys + [skey])
        rstd_chain(ss, skey, 1.0 / D)
        m.op(V, lambda e, tt=tt, ss=ss: e.scalar_tensor_tensor(out=yv(tt), in0=yv(tt), scalar=ss, in1=gP2, op0=ALU.mult, op1=ALU.mult),
             reads=[f"y{tt}", skey, "gP2"], writes=[f"y{tt}"])
        r0 = tt * 128
        m.dma(lambda e, tt=tt, r0=r0: e.dma_start(out=out[r0:r0 + 128, :], in_=yv(tt), accum_op=ALU.add),
              reads=[f"y{tt}", f"out{tt}"], writes=[f"out{tt}"], sem=f"outa{tt}", queue="pool")
        out_keys.append(f"out{tt}")

    upc = {"n": 0}
    e_pieces = []
    for half in range(2):
        def u_load(fcg, half=half):
            return load_w(rows_view(w_up, (half * 16 + fcg) * 256, 256), wview3(256), 2)

        def u_comp(fcg, wu, wk, half=half):
            for fci in range(2):
                fcl = 2 * fcg + fci
                for sl in range(2):
                    b = upc["n"] % 8
                    ri = upc["n"] % 2
                    upc["n"] += 1
                    for dc in range(DC):
                        m.op("pe", lambda e, dc=dc, fci=fci, b=b, sl=sl: e.matmul(bank(b), lhsT=wu[:, dc, fci * 128:(fci + 1) * 128],
                                                                                  rhs=h2T[:, dc, sl * 512:(sl + 1) * 512],
                                                                                  start=(dc == 0), stop=(dc == DC - 1)),
                             reads=wk + h2keys[sl * 4:(sl + 1) * 4], writes=[f"pb{b}"])
                    m.op(S, lambda e, b=b, ri=ri: e.activation(out=rr[ri], in_=bank(b), func=AF.Relu), reads=[f"pb{b}"], writes=[f"rr{ri}"])
                    m.op(P, lambda e, fcl=fcl, ri=ri, sl=sl: e.tensor_tensor(out=u2T[:, fcl, sl * 512:(sl + 1) * 512], in0=rr[ri], in1=rr[ri], op=ALU.mult),
                         reads=[f"rr{ri}"], writes=[f"u2T{fcl}_{sl}"])

        e_pieces += [(u_load, u_comp, fcg) for fcg in range(16)]

        pieces = [(cg, r) for cg in range(4) for r in range(4)]

        def dn_load(pc, half=half):
            cg, r = pc
            f0 = (half * 32 + r * 8) * 128
            src = w_down[f0:f0 + 1024, cg * 512:(cg + 1) * 512].rearrange("(fc p) c -> p fc c", p=128)
            return load_w(src, wview3(512), 2)

        def dn_comp(pc, wd, wk, half=half):
            cg, r = pc
            if half == 1 and cg == 3:
                lastw[r] = (wd, wk)
                if r < 3:
                    return
                for tt in range(NT):
                    b = tt
                    for r2 in range(4):
                        wd2, wk2 = lastw[r2]
                        for j in range(8):
                            fcl = r2 * 8 + j
                            m.op("pe", lambda e, tt=tt, j=j, fcl=fcl, b=b, wd2=wd2, r2=r2: e.matmul(
                                bank(b), lhsT=u2T[:, fcl, tt * 128:(tt + 1) * 128], rhs=wd2[:, j, :],
                                start=(r2 == 0 and j == 0), stop=(r2 == 3 and j == 7)),
                                 reads=wk2 + [f"u2T{fcl}_{tt // 4}"], writes=[f"pb{b}"])
                    dst = yv(tt)[:, cg * 512:(cg + 1) * 512]
                    m.op(V, lambda e, dst=dst, b=b: e.tensor_tensor(out=dst, in0=dst, in1=bank(b), op=ALU.add),
                         reads=[f"pb{b}", f"y{tt}"], writes=[f"y{tt}"])
                    final_tile(tt)
                return
            for tt in range(NT):
                b = tt
                for j in range(8):
                    fcl = r * 8 + j
                    m.op("pe", lambda e, tt=tt, j=j, fcl=fcl, b=b: e.matmul(bank(b), lhsT=u2T[:, fcl, tt * 128:(tt + 1) * 128], rhs=wd[:, j, :],
                                                                            start=(r == 0 and j == 0), stop=(r == 3 and j == 7)),
                         reads=wk + [f"u2T{fcl}_{tt // 4}"], writes=[f"pb{b}"])
                if r == 3:
                    dst = yv(tt)[:, cg * 512:(cg + 1) * 512]
                    if half == 0:
                        if tt % 2 == 0:
                            m.op(S, lambda e, dst=dst, b=b: e.copy(out=dst, in_=bank(b)), reads=[f"pb{b}"], writes=[f"y{tt}"])
                        else:
                            m.op(V, lambda e, dst=dst, b=b: e.tensor_copy(out=dst, in_=bank(b)), reads=[f"pb{b}"], writes=[f"y{tt}"])
                    else:
                        m.op(V, lambda e, dst=dst, b=b: e.tensor_tensor(out=dst, in0=dst, in1=bank(b), op=ALU.add),
                             reads=[f"pb{b}", f"y{tt}"], writes=[f"y{tt}"])

        e_pieces += [(dn_load, dn_comp, pc) for pc in pieces]

    stream(e_pieces, lambda p: p[0](p[2]), lambda p, w, k: p[1](p[2], w, k), 3)

    fin = list(out_keys)
    if debug:
        fin += ["dbg_cat"] + [k for k in m.last_w if k.startswith("dbgx")]
    stats = m.emit(final_wait_keys=fin)
    return nc, stats


_CACHE = {}


def _prep_inputs(x, w_in, attn_sinks, attn_out_gain, rnn_lb_logits, rnn_norm_gain, w_out,
                 mix_pre_gain, mix_post_gain, mlp_pre_gain, mlp_post_gain, w_up, w_down, cores=range(8)):
    f = lambda a: np.ascontiguousarray(np.asarray(a, dtype=np.float32))
    x = f(x)
    shared = {
        "w_in": f(w_in)[0], "w_out": f(w_out)[0], "w_up": f(w_up)[0], "w_down": f(w_down)[0],
        "sinks": f(attn_sinks).reshape(1, NH), "attn_gain": f(attn_out_gain).reshape(1, 1024),
        "lb_logits": f(rnn_lb_logits).reshape(2, 1024), "rnn_gain": f(rnn_norm_gain).reshape(1, 128),
        "g_mix_pre": f(mix_pre_gain).reshape(1, D), "g_mix_post": f(mix_post_gain).reshape(1, D),
        "g_mlp_pre": f(mlp_pre_gain).reshape(1, D), "g_mlp_post": f(mlp_post_gain).reshape(1, D),
    }
    in_maps = []
    for c in cores:
        b, j = c // 4, c % 4
        xin = np.zeros((TOK, D), np.float32)
        xin[128:] = x[b, j * OWN:(j + 1) * OWN]
        if j > 0:
            xin[:128] = x[b, j * OWN - 128:j * OWN]
        hmv = np.full((128, 1), NEG8 if j == 0 else 0.0, np.float32)
        d = dict(shared)
        d["xin"] = xin
        d["hm"] = hmv
        in_maps.append(d)
    return in_maps


def kernel(**inputs):
    if "nc" not in _CACHE:
        _CACHE["nc"] = build_program(debug=False)[0]
    nc = _CACHE["nc"]
    in_maps = _prep_inputs(**inputs)
    res = run_bass_kernel_spmd(nc, in_maps, core_ids=list(range(8)))
    outp = np.zeros((2, 4096, D), np.float32)
    for c in range(8):
        b, j = c // 4, c % 4
        outp[b, j * OWN:(j + 1) * OWN] = res.results[c]["out"]
    return outp
```
